# Optimizing a Trainium2 kernel written in Bass

```python
import math
import jax, jax.numpy as jnp
from jax import lax
import numpy as np

D_MODEL = 1024
BATCH = 2
SEQ = 8192
DEPTH = 4

D_MIX = D_MODEL
D_SSM = D_MIX // 2
SSM_GROUP = 16
N_SSM_GROUPS = D_SSM // SSM_GROUP
SSM_STATE = 64
D_GMLP = D_MIX - D_SSM
GMLP_HEADS = 8
GMLP_HEAD_DIM = D_GMLP // GMLP_HEADS
CHUNK = 128
MEM_LEN = 256
XATTN_HEADS = 4
XATTN_HEAD_DIM = D_MODEL // XATTN_HEADS
D_FF = 2816
CONV_WIDTH = 3
RMS_EPS = 1e-6
DT_MIN = 1e-3
DT_MAX = 1e-1

kernel_name = "hybrid_s5_gmlp_xattn_convffn"


def rmsnorm(x, g):
    xf = x.astype(jnp.float32)
    xf = xf * lax.rsqrt(jnp.mean(xf * xf, axis=-1, keepdims=True) + RMS_EPS)
    return xf.astype(x.dtype) * g


def _ssm_combine(left, right):
    a1r, a1i, b1r, b1i = left
    a2r, a2i, b2r, b2i = right
    ar = a2r * a1r - a2i * a1i
    ai = a2r * a1i + a2i * a1r
    br = a2r * b1r - a2i * b1i + b2r
    bi = a2r * b1i + a2i * b1r + b2i
    return ar, ai, br, bi


def s5_ssm(u, lam_re, lam_im, log_dt, b_re, b_im, c_re, c_im, d_skip):
    bsz, L, _ = u.shape
    f32 = jnp.float32
    uf = u.astype(f32).reshape(bsz, L, N_SSM_GROUPS, SSM_GROUP)
    lr, li = lam_re.astype(f32), lam_im.astype(f32)
    dt = jnp.exp(log_dt.astype(f32))[:, None]
    mag = jnp.exp(lr * dt)
    ab_r, ab_i = mag * jnp.cos(li * dt), mag * jnp.sin(li * dt)
    nr, ni = ab_r - 1.0, ab_i
    den = lr * lr + li * li
    fr = (nr * lr + ni * li) / den
    fi = (ni * lr - nr * li) / den
    br, bi = b_re.astype(f32), b_im.astype(f32)
    bb_r = fr[..., None] * br - fi[..., None] * bi
    bb_i = fr[..., None] * bi + fi[..., None] * br
    drive_r = jnp.einsum('blgc,gpc->blgp', uf, bb_r)
    drive_i = jnp.einsum('blgc,gpc->blgp', uf, bb_i)
    a_r = jnp.broadcast_to(ab_r, drive_r.shape)
    a_i = jnp.broadcast_to(ab_i, drive_i.shape)
    _, _, xr, xi = lax.associative_scan(_ssm_combine, (a_r, a_i, drive_r, drive_i), axis=1)
    y = (jnp.einsum('blgp,gcp->blgc', xr, c_re.astype(f32))
         - jnp.einsum('blgp,gcp->blgc', xi, c_im.astype(f32))
         + d_skip.astype(f32) * uf)
    return y.reshape(bsz, L, D_SSM).astype(u.dtype)


def chunked_spatial_gate(u, v, g_v, w_s, b_s):
    bsz, L, _ = u.shape
    nc = L // CHUNK
    vh = v.reshape(bsz, nc, CHUNK, GMLP_HEADS, GMLP_HEAD_DIM)
    vh = rmsnorm(vh, g_v.reshape(GMLP_HEADS, GMLP_HEAD_DIM))
    mask = jnp.tril(jnp.ones((CHUNK, CHUNK), dtype=w_s.dtype))
    mixed = jnp.einsum('hts,bnshd->bnthd', w_s * mask, vh) + b_s.T[None, None, :, :, None]
    return u * mixed.reshape(bsz, L, D_GMLP)


def memory_cross_attention(h, mem_n, w_q, w_k, w_v, w_o):
    bsz, L, _ = h.shape
    q = (h @ w_q).reshape(bsz, L, XATTN_HEADS, XATTN_HEAD_DIM)
    k = (mem_n @ w_k).reshape(bsz, -1, XATTN_HEADS, XATTN_HEAD_DIM)
    v = (mem_n @ w_v).reshape(bsz, -1, XATTN_HEADS, XATTN_HEAD_DIM)
    s = jnp.einsum('blhd,bmhd->bhlm', q, k).astype(jnp.float32) / math.sqrt(XATTN_HEAD_DIM)
    p = jax.nn.softmax(s, axis=-1).astype(v.dtype)
    o = jnp.einsum('bhlm,bmhd->blhd', p, v).reshape(bsz, L, D_MODEL)
    return o @ w_o


def causal_depthwise_conv(x, w, b):
    c = x.shape[-1]
    y = lax.conv_general_dilated(x, w[:, None, :], window_strides=(1,),
                                 padding=[(CONV_WIDTH - 1, 0)],
                                 dimension_numbers=('NWC', 'WIO', 'NWC'),
                                 feature_group_count=c)
    return y + b


def conv_ffn(h, w_up, conv_w, conv_b, w_down):
    a = causal_depthwise_conv(h @ w_up, conv_w, conv_b)
    val, gate = jnp.split(a, 2, axis=-1)
    return (val * jax.nn.gelu(gate)) @ w_down


def setup_inputs(seed: int = 0) -> dict:
    key = jax.random.key(seed)
    ks = jax.random.split(key, 32)
    f32 = jnp.float32
    nrm = lambda k, shape, scale: jax.random.normal(k, shape, f32) * scale
    gain = lambda k, n: 1.0 + 0.01 * jax.random.normal(k, (DEPTH, n), f32)
    G, P, C = N_SSM_GROUPS, SSM_STATE, SSM_GROUP
    lam_im_init = math.pi * jnp.arange(P, dtype=f32)
    return {
        "x": nrm(ks[0], (BATCH, SEQ, D_MODEL), 1.0),
        "mem": nrm(ks[1], (BATCH, MEM_LEN, D_MODEL), 1.0),
        "g_mix_pre": gain(ks[2], D_MODEL),
        "w_in": nrm(ks[3], (DEPTH, D_MODEL, 2 * D_SSM + 2 * D_GMLP), D_MODEL ** -0.5),
        "lam_re": -0.5 + 0.01 * jax.random.normal(ks[4], (DEPTH, G, P), f32),
        "lam_im": lam_im_init + 0.01 * jax.random.normal(ks[5], (DEPTH, G, P), f32),
        "log_dt": jax.random.uniform(ks[6], (DEPTH, G), f32, math.log(DT_MIN), math.log(DT_MAX)),
        "b_re": nrm(ks[7], (DEPTH, G, P, C), (2 * C) ** -0.5),
        "b_im": nrm(ks[8], (DEPTH, G, P, C), (2 * C) ** -0.5),
        "c_re": nrm(ks[9], (DEPTH, G, C, P), P ** -0.5),
        "c_im": nrm(ks[10], (DEPTH, G, C, P), P ** -0.5),
        "d_skip": nrm(ks[11], (DEPTH, G, C), 1.0),
        "g_v": gain(ks[12], D_GMLP),
        "w_s": nrm(ks[13], (DEPTH, GMLP_HEADS, CHUNK, CHUNK), CHUNK ** -0.5),
        "b_s": 1.0 + 0.01 * jax.random.normal(ks[14], (DEPTH, GMLP_HEADS, CHUNK), f32),
        "w_out": nrm(ks[15], (DEPTH, D_MIX, D_MODEL), D_MIX ** -0.5),
        "g_mix_post": gain(ks[16], D_MODEL),
        "g_x_pre": gain(ks[17], D_MODEL),
        "g_mem": gain(ks[18], D_MODEL),
        "w_q": nrm(ks[19], (DEPTH, D_MODEL, D_MODEL), D_MODEL ** -0.5),
        "w_k": nrm(ks[20], (DEPTH, D_MODEL, D_MODEL), D_MODEL ** -0.5),
        "w_v": nrm(ks[21], (DEPTH, D_MODEL, D_MODEL), D_MODEL ** -0.5),
        "w_o": nrm(ks[22], (DEPTH, D_MODEL, D_MODEL), D_MODEL ** -0.5),
        "g_x_post": gain(ks[23], D_MODEL),
        "g_ffn_pre": gain(ks[24], D_MODEL),
        "w_up": nrm(ks[25], (DEPTH, D_MODEL, 2 * D_FF), D_MODEL ** -0.5),
        "conv_w": nrm(ks[26], (DEPTH, CONV_WIDTH, 2 * D_FF), CONV_WIDTH ** -0.5),
        "conv_b": nrm(ks[27], (DEPTH, 2 * D_FF), 0.01),
        "w_down": nrm(ks[28], (DEPTH, D_FF, D_MODEL), D_FF ** -0.5),
        "g_ffn_post": gain(ks[29], D_MODEL),
    }


def reference(x, mem, g_mix_pre, w_in, lam_re, lam_im, log_dt, b_re, b_im, c_re, c_im,
              d_skip, g_v, w_s, b_s, w_out, g_mix_post, g_x_pre, g_mem, w_q, w_k, w_v,
              w_o, g_x_post, g_ffn_pre, w_up, conv_w, conv_b, w_down, g_ffn_post):
    splits = [D_SSM, 2 * D_SSM, 2 * D_SSM + D_GMLP]
    for l in range(DEPTH):
        h = rmsnorm(x, g_mix_pre[l])
        s_in, s_gate, g_u, g_vv = jnp.split(h @ w_in[l], splits, axis=-1)
        y_ssm = s5_ssm(s_in, lam_re[l], lam_im[l], log_dt[l], b_re[l], b_im[l],
                       c_re[l], c_im[l], d_skip[l])
        y_ssm = jax.nn.gelu(y_ssm) * jax.nn.sigmoid(s_gate)
        y_gmlp = chunked_spatial_gate(g_u, g_vv, g_v[l], w_s[l], b_s[l])
        mix = jnp.concatenate([y_ssm, y_gmlp], axis=-1) @ w_out[l]
        x = x + rmsnorm(mix, g_mix_post[l])
        h = rmsnorm(x, g_x_pre[l])
        mem_n = rmsnorm(mem, g_mem[l])
        xa = memory_cross_attention(h, mem_n, w_q[l], w_k[l], w_v[l], w_o[l])
        x = x + rmsnorm(xa, g_x_post[l])
        h = rmsnorm(x, g_ffn_pre[l])
        f = conv_ffn(h, w_up[l], conv_w[l], conv_b[l], w_down[l])
        x = x + rmsnorm(f, g_ffn_post[l])
    return x
```

```python
import contextlib
import numpy as np
import concourse.bass as bass
import concourse.mybir as mybir
from concourse.bass_utils import run_bass_kernel_spmd

F32 = mybir.dt.float32
BF16 = mybir.dt.bfloat16
AF = mybir.ActivationFunctionType
ALU = mybir.AluOpType

ENGS = ("pe", "dve", "act", "pool", "sp")
SAME_ENGINE_SYNC = True


class Res:
    def __init__(self, name):
        self.name = name
        self.last_w = None
        self.readers = {}
        self._subs = {}

    def sub(self, k):
        r = self._subs.get(k)
        if r is None:
            r = Res(f"{self.name}.{k}")
            self._subs[k] = r
        return r


class Buf(Res):
    def __init__(self, name, t):
        super().__init__(name)
        self.t = t


class Prog:
    def __init__(self, nc):
        self.nc = nc
        self.stack = contextlib.ExitStack()
        self.q = {e: [] for e in ENGS}
        self.ecnt = {e: 0 for e in ENGS}
        self.seen = {e: {} for e in ENGS}
        self.dmacnt = {}
        self.nres = 0

    def sbuf(self, name, shape, dtype):
        t = self.stack.enter_context(self.nc.sbuf_tensor(name, list(shape), dtype))
        return Buf(name, t)

    def psum(self, name, shape, dtype):
        t = self.stack.enter_context(self.nc.psum_tensor(name, list(shape), dtype))
        return Buf(name, t)

    def res(self, name):
        return Res(name)

    def _record(self, eng, fn, reads, writes, dmakey=None, inc=1):
        reads = list(reads)
        writes = list(writes)
        for r in list(reads):
            reads.extend(getattr(r, "also", ()))
        for w in list(writes):
            writes.extend(getattr(w, "also", ()))
        waits = {}
        seen = self.seen[eng]

        def need(ev):
            if ev is None:
                return
            key, val, src = ev
            if src == eng and (eng == "pe" or not SAME_ENGINE_SYNC):
                return
            if seen.get(key, 0) >= val:
                return
            if waits.get(key, 0) < val:
                waits[key] = val

        for r in reads:
            need(r.last_w)
        for w in writes:
            lw = w.last_w
            if not (dmakey is not None and lw is not None and lw[0] == ("dma", dmakey)):
                need(lw)
            for k, (v, s) in w.readers.items():
                need((k, v, s))
        for k, v in waits.items():
            seen[k] = v
        if dmakey is None:
            self.ecnt[eng] += 1
            ev = (("eng", eng), self.ecnt[eng], eng)
        else:
            self.dmacnt[dmakey] = self.dmacnt.get(dmakey, 0) + inc
            ev = (("dma", dmakey), self.dmacnt[dmakey], None)
        self.q[eng].append((fn, list(waits.items()), ev, inc if dmakey is not None else 1))
        for r in reads:
            old = r.readers.get(ev[0])
            if old is None or old[0] < ev[1]:
                r.readers[ev[0]] = (ev[1], ev[2])
        for w in writes:
            w.last_w = ev
            w.readers = {}
        return ev

    def pe(self, fn, reads=(), writes=()):
        return self._record("pe", fn, reads, writes)

    def dve(self, fn, reads=(), writes=()):
        return self._record("dve", fn, reads, writes)

    def act(self, fn, reads=(), writes=()):
        return self._record("act", fn, reads, writes)

    def pool(self, fn, reads=(), writes=()):
        return self._record("pool", fn, reads, writes)

    def eng(self, e, fn, reads=(), writes=()):
        return self._record(e, fn, reads, writes)

    def dma(self, eng, out, in_, reads=(), writes=(), key=None, **kw):
        key = key or writes[0].name
        return self._record(eng, lambda e: e.dma_start(out=out, in_=in_, **kw), reads, writes, dmakey=key, inc=16)

    def collective(self, fn, reads=(), writes=(), key=None):
        key = key or writes[0].name
        return self._record("pool", fn, reads, writes, dmakey=key, inc=16)

    def finish(self, finals):
        nc = self.nc
        waits = {}
        for r in finals:
            key, val, _ = r.last_w
            waits[key] = max(waits.get(key, 0), val)
        self.q["sp"].append((None, list(waits.items()), None, 0))
        keys = [("eng", e) for e in ENGS if self.ecnt[e] > 0] + [("dma", k) for k in self.dmacnt]
        sems = {}
        for i, k in enumerate(keys):
            sems[k] = self.stack.enter_context(nc.semaphore(f"s{i}_{k[1]}"[:24].replace(".", "_")))
        self.nsems = len(keys)
        q = self.q

        def replay(ename, e):
            for fn, ws, ev, inc in q[ename]:
                for k, v in ws:
                    e.wait_ge(sems[k], v)
                if fn is None:
                    continue
                ins = fn(e)
                ins.then_inc(sems[ev[0]], inc)

        with nc.Block() as block:
            @block.tensor
            def _(e):
                replay("pe", e)

            @block.vector
            def _(e):
                replay("dve", e)

            @block.scalar
            def _(e):
                replay("act", e)

            @block.gpsimd
            def _(e):
                replay("pool", e)

            @block.sync
            def _(e):
                replay("sp", e)
        self.stack.close()

    def barrier(self):
        for e in ENGS:
            waits = {}
            for e2 in ENGS:
                if e2 != e and self.ecnt[e2] > self.seen[e].get(("eng", e2), 0):
                    waits[("eng", e2)] = self.ecnt[e2]
            if e != "pe" and self.ecnt[e] > self.seen[e].get(("eng", e), 0):
                waits[("eng", e)] = self.ecnt[e]
            for k, v in self.dmacnt.items():
                if v > self.seen[e].get(("dma", k), 0):
                    waits[("dma", k)] = v
            for k, v in waits.items():
                self.seen[e][k] = v
            if waits:
                self.q[e].append((None, list(waits.items()), None, 0))

    @staticmethod
    def alias(a, b):
        a.also = list(getattr(a, "also", [])) + [b]
        b.also = list(getattr(b, "also", [])) + [a]

    def carve(self, name, arena, off_bytes, shape, dtype):
        esz = 4 if dtype == F32 else 2
        n = 1
        for s in shape[1:]:
            n *= s
        nb = n * esz
        assert off_bytes % 4 == 0 and nb % 4 == 0
        a = arena.t[:, off_bytes // 4:(off_bytes + nb) // 4]
        if dtype != F32:
            a = a.bitcast(dtype)
        if len(shape) == 3:
            a = a.rearrange("p (a b) -> p a b", a=shape[1])
        elif len(shape) == 4:
            a = a.rearrange("p (a b c) -> p a b c", a=shape[1], b=shape[2])
        b = Buf(name, a)
        b.nbytes = nb
        return b


D = 1024
KT = 8
NT = 512
PI = 3.14159265358979
PNAMES = ["g_mix_pre", "w_in", "lam_re", "lam_im", "log_dt", "b_re", "b_im", "c_re", "c_im", "d_skip", "g_v",
          "w_s", "b_s", "w_out", "g_mix_post", "g_x_pre", "g_mem", "w_q", "w_k", "w_v", "w_o", "g_x_post",
          "g_ffn_pre", "w_up", "conv_w", "conv_b", "w_down", "g_ffn_post"]
PSHAPES = {"g_mix_pre": [4, 1024], "w_in": [4, 1024, 2048], "lam_re": [4, 32, 64], "lam_im": [4, 32, 64],
           "log_dt": [4, 32], "b_re": [4, 32, 64, 16], "b_im": [4, 32, 64, 16], "c_re": [4, 32, 16, 64],
           "c_im": [4, 32, 16, 64], "d_skip": [4, 32, 16], "g_v": [4, 512], "w_s": [4, 8, 128, 128],
           "b_s": [4, 8, 128], "w_out": [4, 1024, 1024], "g_mix_post": [4, 1024], "g_x_pre": [4, 1024],
           "g_mem": [4, 1024], "w_q": [4, 1024, 1024], "w_k": [4, 1024, 1024], "w_v": [4, 1024, 1024],
           "w_o": [4, 1024, 1024], "g_x_post": [4, 1024], "g_ffn_pre": [4, 1024], "w_up": [4, 1024, 5632],
           "conv_w": [4, 3, 5632], "conv_b": [4, 5632], "w_down": [4, 2816, 1024], "g_ffn_post": [4, 1024]}


class Arena:
    def __init__(self, P, buf, total):
        self.P, self.buf, self.total, self.off = P, buf, total, 0

    def alloc(self, name, shape, dtype):
        b = self.P.carve(name, self.buf, self.off, shape, dtype)
        self.off += (b.nbytes + 31) // 32 * 32
        assert self.off <= self.total, (name, self.off, self.total)
        return b

    def reset(self):
        self.P.barrier()
        self.off = 0


def build_program(S=8192, DEPTH=4, STAGES=("mix", "xat", "ffn")):
    nc = bass.Bass("TRN2", target_bir_lowering=False)
    NTILES = S // NT
    x_d = nc.dram_tensor("x", [S, D], F32, kind="ExternalInput").ap()
    mem_d = nc.dram_tensor("mem", [256, D], F32, kind="ExternalInput").ap()
    pd = {n: nc.dram_tensor(n, PSHAPES[n], F32, kind="ExternalInput").ap() for n in PNAMES}
    out_d = nc.dram_tensor("out", [S, D], F32, kind="ExternalOutput").ap()
    xs_d = nc.dram_tensor("xs", [D, S], F32).ap()
    xs_v = xs_d.rearrange("(k p) t -> p k t", p=128)
    P = Prog(nc)
    xs_r = [P.res(f"xs{t}") for t in range(NTILES)]
    out_r = P.res("out")

    ident = P.sbuf("ident", [128, 128], F32)
    identb = P.sbuf("identb", [128, 128], BF16)
    onesb = P.sbuf("onesb", [128, 128], BF16)
    blkones = P.sbuf("blkones", [128, 128], BF16)
    maskJ = P.sbuf("maskJ", [128, 128], F32)
    tril = P.sbuf("tril", [128, 128], F32)
    cb = P.sbuf("cbias", [128, 4], F32)
    gv = {n: P.sbuf("gv_" + n, [128, 4, 8], F32) for n in
          ["g_mix_pre", "g_mix_post", "g_x_pre", "g_mem", "g_x_post", "g_ffn_pre", "g_ffn_post"]}
    gvv = P.sbuf("gv_g_v", [128, 4, 4], F32)
    memhT = P.sbuf("memhT", [128, 8, 256], F32)
    ARENA_BYTES = 194 * 1024
    arena_buf = P.sbuf("arena", [128, ARENA_BYTES // 4], F32)
    A = Arena(P, arena_buf, ARENA_BYTES)
    pst = P.psum("pst", [128, 8 * 512], F32)
    pb = []
    for b in range(8):
        r = Buf(f"pb{b}", pst.t[:, b * 512:(b + 1) * 512])
        pb.append(r)

    def cst(e, ap, v, wr):
        P.eng(e, lambda en: en.memset(ap, v), writes=wr)

    cst("pool", ident.t[:], 1.0, [ident])
    P.pool(lambda e: e.affine_select(out=ident.t[:], in_=ident.t[:], pattern=[[1, 128]], compare_op=ALU.is_equal,
                                     fill=0.0, base=0, channel_multiplier=-1), reads=[ident], writes=[ident])
    P.dve(lambda e: e.tensor_copy(out=identb.t[:], in_=ident.t[:]), reads=[ident], writes=[identb])
    cst("pool", onesb.t[:], 1.0, [onesb])
    cst("pool", blkones.t[:], 0.0, [blkones])
    cst("pool", blkones.t[0:64, 0:64], 1.0, [blkones])
    cst("pool", blkones.t[64:128, 64:128], 1.0, [blkones])
    cst("pool", maskJ.t[:], 1.0, [maskJ])
    P.pool(lambda e: e.affine_select(out=maskJ.t[:].rearrange("p (j c) -> p j c", j=8),
                                     in_=maskJ.t[:].rearrange("p (j c) -> p j c", j=8),
                                     pattern=[[16, 8], [0, 16]], compare_op=ALU.is_ge, fill=0.0, base=15,
                                     channel_multiplier=-1), reads=[maskJ], writes=[maskJ])
    cst("pool", tril.t[:], 1.0, [tril])
    P.pool(lambda e: e.affine_select(out=tril.t[:], in_=tril.t[:], pattern=[[1, 128]], compare_op=ALU.is_ge,
                                     fill=0.0, base=0, channel_multiplier=-1), reads=[tril], writes=[tril])
    cst("pool", cb.t[:, 0:1], PI / 2, [cb])
    cst("pool", cb.t[:, 1:2], 1e-6, [cb])
    cst("pool", cb.t[:, 2:3], 0.0, [cb])
    with nc.allow_non_contiguous_dma(reason="small param loads"):
        pass
    import os
    KDBG = os.environ.get("KDBG", "")
    for n, t in (gv.items() if "nogv" not in KDBG else []):
        P.dma("sp", t.t[:], pd[n].rearrange("l (k p) -> p l k", p=128), writes=[t], allow_slow_non_contiguous=True)
    if "nogv" not in KDBG:
      P.dma("sp", gvv.t[:], pd["g_v"].rearrange("l (k p) -> p l k", p=128), writes=[gvv], allow_slow_non_contiguous=True)

    def rstd_from_ss(ss_ps, rstd, n_feat, rd):
        P.act(lambda e: e.activation(out=rstd.t[:], in_=ss_ps, func=AF.Sqrt, bias=cb.t[:, 1:2], scale=1.0 / n_feat),
              reads=rd + [cb], writes=[rstd])
        P.dve(lambda e: e.reciprocal(out=rstd.t[:], in_=rstd.t[:]), reads=[rstd], writes=[rstd])

    def load_x(xt, ti):
        P.dma("sp", xt.t[:], xs_v[:, :, ti * NT:(ti + 1) * NT], reads=[xs_r[ti]], writes=[xt])

    def store_x(xt, ti):
        P.dma("sp", xs_v[:, :, ti * NT:(ti + 1) * NT], xt.t[:], reads=[xt], writes=[xs_r[ti]])

    def prenorm(xt, ht, sq, rstd, gname, l):
        P.act(lambda e: e.activation(out=sq.t[:], in_=xt.t[:], func=AF.Square), reads=[xt], writes=[sq])
        for k in range(KT):
            P.pe(lambda e, k=k: e.matmul(pb[0].t[:], lhsT=onesb.t[:], rhs=sq.t[:, k, :], start=(k == 0),
                                         stop=(k == KT - 1)), reads=[sq, onesb], writes=[pb[0]])
        rstd_from_ss(pb[0].t[:], rstd, D, [pb[0]])
        g = gv[gname]
        for k in range(KT):
            P.dve(lambda e, k=k: e.scalar_tensor_tensor(out=ht.t[:, k, :], in0=xt.t[:, k, :], scalar=g.t[:, l, k:k + 1],
                                                        in1=rstd.t[:], op0=ALU.mult, op1=ALU.mult),
                  reads=[xt, g, rstd], writes=[ht])

    def postnorm_res(xt, ybuf, sq, rstd, gname, l, mm_fn):
        for o in range(KT):
            bank = pb[1 + (o % 2)]
            mm_fn(o, bank)
            P.act(lambda e, o=o, bank=bank: e.activation(out=ybuf.t[:, o, :], in_=bank.t[:], func=AF.Copy),
                  reads=[bank], writes=[ybuf])
            P.dve(lambda e, o=o, bank=bank: e.tensor_tensor(out=sq.t[:, o, :], in0=bank.t[:], in1=ybuf.t[:, o, :],
                                                            op=ALU.mult), reads=[bank, ybuf], writes=[sq])
        for k in range(KT):
            P.pe(lambda e, k=k: e.matmul(pb[0].t[:], lhsT=onesb.t[:], rhs=sq.t[:, k, :], start=(k == 0),
                                         stop=(k == KT - 1)), reads=[sq, onesb], writes=[pb[0]])
        rstd_from_ss(pb[0].t[:], rstd, D, [pb[0]])
        g = gv[gname]
        for o in range(KT):
            P.dve(lambda e, o=o: e.scalar_tensor_tensor(out=ybuf.t[:, o, :], in0=ybuf.t[:, o, :],
                                                        scalar=g.t[:, l, o:o + 1], in1=rstd.t[:], op0=ALU.mult,
                                                        op1=ALU.mult), reads=[ybuf, g, rstd], writes=[ybuf])
            P.pool(lambda e, o=o: e.tensor_tensor(out=xt.t[:, o, :], in0=xt.t[:, o, :], in1=ybuf.t[:, o, :],
                                                  op=ALU.add), reads=[xt, ybuf], writes=[xt])

    def load_w(dst, src_ap, ncols, eng="pool"):
        c0 = 0
        while c0 < ncols:
            c1 = min(ncols, c0 + 2048)
            P.dma(eng, dst.t[:, :, c0:c1], src_ap[:, :, c0:c1], writes=[dst])
            c0 = c1

    xtok = A.alloc("xtok", [128, 4, D], F32)
    xt0 = A.alloc("xt0", [128, KT, NT], F32)
    for ti in range(NTILES):
        P.dma("sp", xtok.t[:], x_d[ti * NT:(ti + 1) * NT, :].rearrange("(s p) d -> p s d", p=128), writes=[xtok])
        for k in range(KT):
            bank = pb[1 + (k % 4)]
            for s in range(4):
                P.pe(lambda e, k=k, s=s, bank=bank: e.transpose(out=bank.t[:, s * 128:(s + 1) * 128],
                                                                in_=xtok.t[:, s, k * 128:(k + 1) * 128],
                                                                identity=ident.t[:]),
                     reads=[xtok, ident], writes=[bank])
            if k % 2 == 0:
                P.act(lambda e, k=k, bank=bank: e.activation(out=xt0.t[:, k, :], in_=bank.t[:], func=AF.Copy),
                      reads=[bank], writes=[xt0])
            else:
                P.dve(lambda e, k=k, bank=bank: e.tensor_copy(out=xt0.t[:, k, :], in_=bank.t[:]),
                      reads=[bank], writes=[xt0])
        store_x(xt0, ti)
    if "nomem" in KDBG:
        return_early_mem = True
    memt = A.alloc("memt", [128, 2, D], F32)
    msq = A.alloc("msq", [128, D], F32)
    mss = A.alloc("mss", [128, 2], F32)
    P.dma("sp", memt.t[:], mem_d.rearrange("(s p) d -> p s d", p=128), writes=[memt])
    for s in range(2):
        P.act(lambda e, s=s: e.activation(out=msq.t[:], in_=memt.t[:, s, :], func=AF.Square,
                                          accum_out=mss.t[:, s:s + 1]), reads=[memt], writes=[msq, mss])
    P.act(lambda e: e.activation(out=mss.t[:], in_=mss.t[:], func=AF.Sqrt, bias=cb.t[:, 1:2], scale=1.0 / D),
          reads=[mss, cb], writes=[mss])
    P.dve(lambda e: e.reciprocal(out=mss.t[:], in_=mss.t[:]), reads=[mss], writes=[mss])
    for s in range(2):
        P.dve(lambda e, s=s: e.tensor_scalar(out=memt.t[:, s, :], in0=memt.t[:, s, :], scalar1=mss.t[:, s:s + 1],
                                             scalar2=None, op0=ALU.mult), reads=[memt, mss], writes=[memt])
    for k in range(KT):
        bank = pb[1 + (k % 4)]
        for s in range(2):
            P.pe(lambda e, k=k, s=s, bank=bank: e.transpose(out=bank.t[:, s * 128:(s + 1) * 128],
                                                            in_=memt.t[:, s, k * 128:(k + 1) * 128],
                                                            identity=ident.t[:]), reads=[memt, ident], writes=[bank])
        P.act(lambda e, k=k, bank=bank: e.activation(out=memhT.t[:, k, :], in_=bank.t[:, 0:256], func=AF.Copy),
              reads=[bank], writes=[memhT])

    ctx = dict(nc=nc, P=P, A=A, pb=pb, pd=pd, gv=gv, gvv=gvv, cb=cb, ident=ident, identb=identb, onesb=onesb,
               blkones=blkones, maskJ=maskJ, tril=tril, memhT=memhT, load_x=load_x, store_x=store_x,
               prenorm=prenorm, postnorm_res=postnorm_res, load_w=load_w, rstd_from_ss=rstd_from_ss,
               NTILES=NTILES, xs_r=xs_r, xs_v=xs_v, pst=pst)
    for l in range(DEPTH):
        if "mix" in STAGES:
            A.reset()
            mixer_layer(ctx, l)
        if "xat" in STAGES:
            A.reset()
            xattn_layer(ctx, l)
        if "ffn" in STAGES:
            A.reset()
            ffn_layer(ctx, l)

    A.reset()
    xt0e = A.alloc("xt0e", [128, KT, NT], F32)
    xtoke = A.alloc("xtoke", [128, 4, D], F32)
    for ti in range(NTILES):
        load_x(xt0e, ti)
        for s in range(4):
            for kh in range(2):
                bank = pb[1 + ((2 * s + kh) % 4)]
                for kk in range(4):
                    k = kh * 4 + kk
                    P.pe(lambda e, k=k, kk=kk, s=s, bank=bank: e.transpose(
                        out=bank.t[:, kk * 128:(kk + 1) * 128], in_=xt0e.t[:, k, s * 128:(s + 1) * 128],
                        identity=ident.t[:]), reads=[xt0e, ident], writes=[bank])
                if kh == 0:
                    P.act(lambda e, s=s, kh=kh, bank=bank: e.activation(out=xtoke.t[:, s, kh * 512:(kh + 1) * 512],
                                                                        in_=bank.t[:], func=AF.Copy),
                          reads=[bank], writes=[xtoke])
                else:
                    P.dve(lambda e, s=s, kh=kh, bank=bank: e.tensor_copy(out=xtoke.t[:, s, kh * 512:(kh + 1) * 512],
                                                                         in_=bank.t[:]), reads=[bank], writes=[xtoke])
        P.dma("sp", out_d[ti * NT:(ti + 1) * NT, :].rearrange("(s p) d -> p s d", p=128), xtoke.t[:],
              reads=[xtoke], writes=[out_r], key="out")
    P.finish([out_r])
    return nc


def xattn_layer(c, l):
    P, A, pb, pd = c["P"], c["A"], c["pb"], c["pd"]
    onesb, gv, memhT = c["onesb"], c["gv"], c["memhT"]
    wv3 = lambda n: pd[n][l].rearrange("(k p) n -> p k n", p=128)
    wq = A.alloc("wq", [128, KT, D], BF16)
    wo = A.alloc("wo", [128, KT, D], BF16)
    wk = A.alloc("wk", [128, KT, D], BF16)
    wvv = A.alloc("wv", [128, KT, D], BF16)
    for w, n in ((wk, "w_k"), (wvv, "w_v"), (wq, "w_q"), (wo, "w_o")):
        c["load_w"](w, wv3(n), D)
    memn = A.alloc("memn", [128, KT, 256], BF16)
    kT = A.alloc("kT", [128, KT, 256], BF16)
    vt = A.alloc("vt", [128, 2, D], BF16)
    xt = A.alloc("xt", [128, KT, NT], F32)
    ht = A.alloc("ht", [128, KT, NT], BF16)
    sq = A.alloc("sq", [128, KT, NT], BF16)
    rstd = A.alloc("rstd", [128, NT], F32)
    qT = A.alloc("qT", [128, KT, NT], BF16)
    expT = A.alloc("expT", [128, 2, NT], BF16)
    rden = A.alloc("rden", [128, NT], F32)
    oT = A.alloc("oT", [128, KT, NT], BF16)
    ybuf = A.alloc("ybuf", [128, KT, NT], F32)
    gm = gv["g_mem"]
    for k in range(KT):
        P.dve(lambda e, k=k: e.tensor_scalar(out=memn.t[:, k, :], in0=memhT.t[:, k, :], scalar1=gm.t[:, l, k:k + 1],
                                             scalar2=None, op0=ALU.mult), reads=[memhT, gm], writes=[memn])
    for o in range(KT):
        bank = pb[3 + (o % 2)]
        for k in range(KT):
            P.pe(lambda e, o=o, k=k, bank=bank: e.matmul(bank.t[:, 0:256], lhsT=wk.t[:, k, o * 128:(o + 1) * 128],
                                                         rhs=memn.t[:, k, :], start=(k == 0), stop=(k == KT - 1)),
                 reads=[wk, memn], writes=[bank])
        P.act(lambda e, o=o, bank=bank: e.activation(out=kT.t[:, o, :], in_=bank.t[:, 0:256], func=AF.Copy),
              reads=[bank], writes=[kT])
    for mt in range(2):
        for hf in range(2):
            bank = pb[3 + (hf % 2)]
            for k in range(KT):
                P.pe(lambda e, mt=mt, hf=hf, k=k, bank=bank: e.matmul(
                    bank.t[:], lhsT=memn.t[:, k, mt * 128:(mt + 1) * 128], rhs=wvv.t[:, k, hf * 512:(hf + 1) * 512],
                    start=(k == 0), stop=(k == KT - 1)), reads=[wvv, memn], writes=[bank])
            P.act(lambda e, mt=mt, hf=hf, bank=bank: e.activation(out=vt.t[:, mt, hf * 512:(hf + 1) * 512],
                                                                  in_=bank.t[:], func=AF.Copy),
                  reads=[bank], writes=[vt])
    for ti in range(c["NTILES"]):
        c["load_x"](xt, ti)
        c["prenorm"](xt, ht, sq, rstd, "g_x_pre", l)
        for o in range(KT):
            bank = pb[3 + (o % 2)]
            for k in range(KT):
                P.pe(lambda e, o=o, k=k, bank=bank: e.matmul(bank.t[:], lhsT=wq.t[:, k, o * 128:(o + 1) * 128],
                                                             rhs=ht.t[:, k, :], start=(k == 0), stop=(k == KT - 1)),
                     reads=[wq, ht], writes=[bank])
            P.act(lambda e, o=o, bank=bank: e.activation(out=qT.t[:, o, :], in_=bank.t[:], func=AF.Copy),
                  reads=[bank], writes=[qT])
        for h in range(4):
            for mt in range(2):
                bank = pb[3 + (mt % 2)]
                for dd in range(2):
                    P.pe(lambda e, h=h, mt=mt, dd=dd, bank=bank: e.matmul(
                        bank.t[:], lhsT=kT.t[:, 2 * h + dd, mt * 128:(mt + 1) * 128], rhs=qT.t[:, 2 * h + dd, :],
                        start=(dd == 0), stop=(dd == 1)), reads=[kT, qT], writes=[bank])
                P.act(lambda e, mt=mt, bank=bank: e.activation(out=expT.t[:, mt, :], in_=bank.t[:], func=AF.Exp,
                                                               scale=1.0 / 16.0), reads=[bank], writes=[expT])
            for mt in range(2):
                P.pe(lambda e, mt=mt: e.matmul(pb[5].t[:], lhsT=onesb.t[:], rhs=expT.t[:, mt, :], start=(mt == 0),
                                               stop=(mt == 1)), reads=[expT, onesb], writes=[pb[5]])
            P.dve(lambda e: e.reciprocal(out=rden.t[:], in_=pb[5].t[:]), reads=[pb[5]], writes=[rden])
            for dd in range(2):
                bank = pb[6 + dd]
                for mt in range(2):
                    P.pe(lambda e, h=h, mt=mt, dd=dd, bank=bank: e.matmul(
                        bank.t[:], lhsT=vt.t[:, mt, (2 * h + dd) * 128:(2 * h + dd + 1) * 128], rhs=expT.t[:, mt, :],
                        start=(mt == 0), stop=(mt == 1)), reads=[vt, expT], writes=[bank])
                P.dve(lambda e, h=h, dd=dd, bank=bank: e.tensor_tensor(out=oT.t[:, 2 * h + dd, :], in0=bank.t[:],
                                                                       in1=rden.t[:], op=ALU.mult),
                      reads=[bank, rden], writes=[oT])

        def mm(o, bank):
            for k in range(KT):
                P.pe(lambda e, o=o, k=k, bank=bank: e.matmul(bank.t[:], lhsT=wo.t[:, k, o * 128:(o + 1) * 128],
                                                             rhs=oT.t[:, k, :], start=(k == 0), stop=(k == KT - 1)),
                     reads=[wo, oT], writes=[bank])
        c["postnorm_res"](xt, ybuf, sq, rstd, "g_x_post", l, mm)
        c["store_x"](xt, ti)


def ffn_layer(c, l):
    P, A, pb, pd = c["P"], c["A"], c["pb"], c["pd"]
    onesb, gv, cb = c["onesb"], c["gv"], c["cb"]
    NF = 22
    wup = A.alloc("wup", [128, KT, 5632], BF16)
    wdn = A.alloc("wdn", [128, NF, D], BF16)
    c["load_w"](wup, pd["w_up"][l].rearrange("(k p) n -> p k n", p=128), 5632)
    c["load_w"](wdn, pd["w_down"][l].rearrange("(f p) n -> p f n", p=128), D)
    cw = A.alloc("cw", [128, 3, 44], F32)
    cbv = A.alloc("cbv", [128, 44], F32)
    zl = A.alloc("zl", [128, 44, 2], F32)
    P.dma("sp", cw.t[:], pd["conv_w"][l].rearrange("w (f p) -> p w f", p=128), writes=[cw],
          allow_slow_non_contiguous=True)
    P.dma("sp", cbv.t[:], pd["conv_b"][l].rearrange("(f p) -> p f", p=128), writes=[cbv],
          allow_slow_non_contiguous=True)
    P.pool(lambda e: e.memset(zl.t[:], 0.0), writes=[zl])
    off_xt = A.off
    xt = A.alloc("xt", [128, KT, NT], F32)
    ybuf = P.carve("ybuf_f", A.buf, off_xt, [128, KT, NT], F32)
    off_ht = A.off
    ht = A.alloc("ht", [128, KT, NT], BF16)
    sq2 = P.carve("sq2_f", A.buf, off_ht, [128, KT, NT], BF16)
    rstd = A.alloc("rstd", [128, NT], F32)
    off_g = A.off
    gbuf = A.alloc("gbuf", [128, NF, NT], BF16)
    sq = P.carve("sq_f", A.buf, off_g, [128, KT, NT], BF16)
    acc = [A.alloc("accv", [128, NT], F32), A.alloc("accg", [128, NT], F32)]
    xo = [A.alloc("xo0", [128, NT], F32), A.alloc("xo1", [128, NT], F32)]
    xs_r = c["xs_r"]
    xs_v = c["xs_v"]
    for ti in range(c["NTILES"]):
        c["load_x"](xt, ti)
        c["prenorm"](xt, ht, sq, rstd, "g_ffn_pre", l)
        for f in range(NF):
            for vg in range(2):
                ci = vg * NF + f
                bank = pb[3 + vg] if f % 2 == 0 else pb[5 + vg]
                a = acc[vg]
                for k in range(KT):
                    P.pe(lambda e, ci=ci, k=k, bank=bank: e.matmul(bank.t[:], lhsT=wup.t[:, k, ci * 128:(ci + 1) * 128],
                                                                   rhs=ht.t[:, k, :], start=(k == 0),
                                                                   stop=(k == KT - 1)), reads=[wup, ht], writes=[bank])
                P.act(lambda e, ci=ci, bank=bank, a=a: e.activation(out=a.t[:], in_=bank.t[:], func=AF.Identity,
                                                                    bias=cbv.t[:, ci:ci + 1],
                                                                    scale=cw.t[:, 2, ci:ci + 1]),
                      reads=[bank, cbv, cw], writes=[a])
                P.dve(lambda e, ci=ci, bank=bank, a=a: e.scalar_tensor_tensor(
                    out=a.t[:, 1:NT], in0=bank.t[:, 0:NT - 1], scalar=cw.t[:, 1, ci:ci + 1], in1=a.t[:, 1:NT],
                    op0=ALU.mult, op1=ALU.add), reads=[bank, cw, a], writes=[a])
                P.dve(lambda e, ci=ci, bank=bank, a=a: e.scalar_tensor_tensor(
                    out=a.t[:, 2:NT], in0=bank.t[:, 0:NT - 2], scalar=cw.t[:, 0, ci:ci + 1], in1=a.t[:, 2:NT],
                    op0=ALU.mult, op1=ALU.add), reads=[bank, cw, a], writes=[a])
                P.dve(lambda e, ci=ci, a=a: e.scalar_tensor_tensor(
                    out=a.t[:, 0:2], in0=zl.t[:, ci, 0:2], scalar=cw.t[:, 0, ci:ci + 1], in1=a.t[:, 0:2],
                    op0=ALU.mult, op1=ALU.add), reads=[zl, cw, a], writes=[a])
                P.dve(lambda e, ci=ci, a=a: e.scalar_tensor_tensor(
                    out=a.t[:, 0:1], in0=zl.t[:, ci, 1:2], scalar=cw.t[:, 1, ci:ci + 1], in1=a.t[:, 0:1],
                    op0=ALU.mult, op1=ALU.add), reads=[zl, cw, a], writes=[a])
                P.act(lambda e, ci=ci, bank=bank: e.activation(out=zl.t[:, ci, :], in_=bank.t[:, NT - 2:NT],
                                                               func=AF.Copy), reads=[bank], writes=[zl])
            P.act(lambda e: e.activation(out=acc[1].t[:], in_=acc[1].t[:], func=AF.Gelu_apprx_tanh),
                  reads=[acc[1]], writes=[acc[1]])
            P.pool(lambda e, f=f: e.tensor_tensor(out=gbuf.t[:, f, :], in0=acc[0].t[:], in1=acc[1].t[:], op=ALU.mult),
                   reads=[acc[0], acc[1]], writes=[gbuf])

        def mm(o, bank):
            for f in range(NF):
                P.pe(lambda e, o=o, f=f, bank=bank: e.matmul(bank.t[:], lhsT=wdn.t[:, f, o * 128:(o + 1) * 128],
                                                             rhs=gbuf.t[:, f, :], start=(f == 0), stop=(f == NF - 1)),
                     reads=[wdn, gbuf], writes=[bank])
        g = gv["g_ffn_post"]
        for o in range(KT):
            bank = pb[1 + (o % 2)]
            mm(o, bank)
            P.act(lambda e, o=o, bank=bank: e.activation(out=ybuf.t[:, o, :], in_=bank.t[:], func=AF.Copy),
                  reads=[bank], writes=[ybuf, xt])
            P.dve(lambda e, o=o, bank=bank: e.tensor_tensor(out=sq2.t[:, o, :], in0=bank.t[:], in1=ybuf.t[:, o, :],
                                                            op=ALU.mult), reads=[bank, ybuf], writes=[sq2, ht])
        for k in range(KT):
            P.pe(lambda e, k=k: e.matmul(pb[0].t[:], lhsT=onesb.t[:], rhs=sq2.t[:, k, :], start=(k == 0),
                                         stop=(k == KT - 1)), reads=[sq2, onesb], writes=[pb[0]])
        c["rstd_from_ss"](pb[0].t[:], rstd, D, [pb[0]])
        for o in range(KT):
            xb = xo[o % 2]
            P.dma("sp", xb.t[:], xs_v[:, o, ti * NT:(ti + 1) * NT], reads=[xs_r[ti]], writes=[xb])
            P.dve(lambda e, o=o: e.scalar_tensor_tensor(out=ybuf.t[:, o, :], in0=ybuf.t[:, o, :],
                                                        scalar=g.t[:, l, o:o + 1], in1=rstd.t[:], op0=ALU.mult,
                                                        op1=ALU.mult), reads=[ybuf, g, rstd], writes=[ybuf, xt])
            P.pool(lambda e, o=o, xb=xb: e.tensor_tensor(out=xb.t[:], in0=xb.t[:], in1=ybuf.t[:, o, :], op=ALU.add),
                   reads=[xb, ybuf], writes=[xb])
            P.dma("sp", xs_v[:, o, ti * NT:(ti + 1) * NT], xb.t[:], reads=[xb], writes=[xs_r[ti]])


def _run(inputs, S=8192, DEPTH=4, STAGES=("mix", "xat", "ffn"), NCORES=8):
    nc = build_program(S=S, DEPTH=DEPTH, STAGES=STAGES)
    in_maps = []
    for c in range(NCORES):
        b = c % 2
        m = {"x": np.ascontiguousarray(inputs["x"][b, :S]), "mem": np.ascontiguousarray(inputs["mem"][b])}
        for n in PNAMES:
            m[n] = np.ascontiguousarray(inputs[n])
        in_maps.append(m)
    res = run_bass_kernel_spmd(nc, in_maps, core_ids=list(range(NCORES)))
    return np.stack([res.results[0]["out"], res.results[1 % NCORES]["out"]], axis=0)


def kernel(**inputs):
    inputs = {k: np.asarray(v) for k, v in inputs.items()}
    return _run(inputs).astype(np.float32)


def mixer_layer(c, l):
    P, A, pb, pd = c["P"], c["A"], c["pb"], c["pd"]
    onesb, gv, gvv, cb = c["onesb"], c["gv"], c["gvv"], c["cb"]
    ident, identb, blkones, maskJ, tril = c["ident"], c["identb"], c["blkones"], c["maskJ"], c["tril"]
    pst = c["pst"]
    NG, NPR = 32, 16
    w_in = A.alloc("w_in", [128, KT, 2048], BF16)
    w_out = A.alloc("w_out", [128, KT, D], BF16)
    c["load_w"](w_in, pd["w_in"][l].rearrange("(k p) n -> p k n", p=128), 2048)
    c["load_w"](w_out, pd["w_out"][l].rearrange("(k p) n -> p k n", p=128), D)
    WDT = [A.alloc("WDTr", [128, NPR, 128], BF16), A.alloc("WDTi", [128, NPR, 128], BF16)]
    Wo = [A.alloc("Wor", [128, NPR, 128], BF16), A.alloc("Woi", [128, NPR, 128], BF16)]
    Kmat = A.alloc("Kmat", [128, NG, 128], BF16)
    C1 = A.alloc("C1", [128, NPR, 64], F32)
    S1 = A.alloc("S1", [128, NPR, 64], F32)
    Mtab = A.alloc("Mtab", [128, NPR, 64], F32)
    rho8 = A.alloc("rho8", [128, NPR], F32)
    Lf = [A.alloc("Lfr", [128, NPR, 65], F32), A.alloc("Lfi", [128, NPR, 65], F32)]
    WsT = A.alloc("WsT", [128, 8, 128], BF16)
    bs2 = A.alloc("bs2", [128, 1024], BF16)
    off_tmp = A.off
    dv = P.res("dv")
    sm = A.alloc("sm", [128, 64, NPR], F32)
    Bc = [A.alloc("Bre", [128, NPR, 16], F32), A.alloc("Bim", [128, NPR, 16], F32)]
    Ct = [A.alloc("Ctr", [128, NPR, 16], F32), A.alloc("Cti", [128, NPR, 16], F32)]
    Bb = [A.alloc("Bbr", [128, NPR, 16], F32), A.alloc("Bbi", [128, NPR, 16], F32)]
    WD = [A.alloc("WDr", [128, NPR, 128], F32), A.alloc("WDi", [128, NPR, 128], F32)]
    WDt = [A.alloc("WDtr", [128, NPR, 128], BF16), A.alloc("WDti", [128, NPR, 128], BF16)]
    T = [A.alloc("dT1", [128, NPR, 128], F32), A.alloc("dT2", [128, NPR, 128], F32)]
    dcol = A.alloc("dcol", [128, NG], F32)
    wsl = A.alloc("wsl", [128, 8, 128], F32)
    bsf = A.alloc("bsf", [128, 1024], F32)
    tK = A.alloc("tK", [128, 128], F32)

    def S(i):
        return sm.t[:, i, :]

    def tt(out, a, b, op, eng="dve"):
        P.eng(eng, lambda e: e.tensor_tensor(out=out, in0=a, in1=b, op=op), reads=[dv], writes=[dv])

    def ts(out, a, s1, op, eng="dve"):
        P.eng(eng, lambda e: e.tensor_scalar(out=out, in0=a, scalar1=s1, scalar2=None, op0=op), reads=[dv], writes=[dv])

    def actf(out, a, func, scale=1.0, bias=None):
        b = cb.t[:, 2:3] if bias is None else bias
        P.act(lambda e: e.activation(out=out, in_=a, func=func, bias=b, scale=scale), reads=[dv, cb], writes=[dv])

    def cmul(o_r, o_i, ar, ai, br, bi, t1, t2):
        tt(t1, ar, br, ALU.mult); tt(t2, ai, bi, ALU.mult); tt(o_r, t1, t2, ALU.subtract)
        tt(t1, ar, bi, ALU.mult); tt(t2, ai, br, ALU.mult); tt(o_i, t1, t2, ALU.add)

    def dmap(out, src):
        P.dma("sp", out, src, writes=[dv], key="dvload", allow_slow_non_contiguous=True)

    LR, LI, LDT, DT, X1, X2, MG, CS, SN, AR, AI, NR, NI, TA, TB, RHO, IRHO, U8R, U8I, I8R, I8I, FR, FI, DEN = range(24)
    AP0 = 24
    dmap(S(LR), pd["lam_re"][l].rearrange("(pr m) p -> (m p) pr", m=2))
    dmap(S(LI), pd["lam_im"][l].rearrange("(pr m) p -> (m p) pr", m=2))
    for m in range(2):
        dmap(sm.t[m * 64:(m + 1) * 64, LDT, :],
             pd["log_dt"][l].rearrange("(pr m) -> m pr", m=2)[m].partition_broadcast(64))
        for comp, nm in ((0, "c_re"), (1, "c_im")):
            for pr_ in range(NPR):
                dmap(Ct[comp].t[m * 64:(m + 1) * 64, pr_, :],
                     pd[nm][l].rearrange("(pr m) c p -> m pr p c", m=2)[m][pr_])
    dmap(Bc[0].t[:], pd["b_re"][l].rearrange("(pr m) p c -> (m p) pr c", m=2))
    dmap(Bc[1].t[:], pd["b_im"][l].rearrange("(pr m) p c -> (m p) pr c", m=2))
    for j in range(8):
        dmap(dcol.t[j * 16:(j + 1) * 16, :], pd["d_skip"][l].rearrange("g c -> c g"))
    dmap(wsl.t[:], pd["w_s"][l].rearrange("h t s -> t h s"))
    for r in (0, 32):
        dmap(bsf.t[r:r + 1, :], pd["b_s"][l:l + 1].rearrange("o h t -> o (h t)"))
    actf(S(DT), S(LDT), AF.Exp)
    tt(S(X1), S(LR), S(DT), ALU.mult)
    tt(S(X2), S(LI), S(DT), ALU.mult)
    actf(S(MG), S(X1), AF.Exp, scale=1.0 / 64)
    actf(S(CS), S(X2), AF.Sin, scale=1.0 / 64, bias=cb.t[:, 0:1])
    actf(S(SN), S(X2), AF.Sin, scale=1.0 / 64)
    tt(S(AR), S(MG), S(CS), ALU.mult)
    tt(S(AI), S(MG), S(SN), ALU.mult)
    cur = (AR, AI)
    nxt = (NR, NI)
    for _ in range(6):
        tt(S(TA), S(cur[0]), S(cur[0]), ALU.mult)
        tt(S(TB), S(cur[1]), S(cur[1]), ALU.mult)
        tt(S(nxt[0]), S(TA), S(TB), ALU.subtract)
        tt(S(TA), S(cur[0]), S(cur[1]), ALU.mult)
        tt(S(nxt[1]), S(TA), S(TA), ALU.add)
        cur, nxt = nxt, cur
    A1 = cur

    def apr(m):
        return S(AP0 + 2 * m)

    def api(m):
        return S(AP0 + 2 * m + 1)
    P.dve(lambda e: e.memset(apr(0), 1.0), reads=[dv], writes=[dv])
    P.dve(lambda e: e.memset(api(0), 0.0), reads=[dv], writes=[dv])
    P.dve(lambda e: e.tensor_copy(out=apr(1), in_=S(A1[0])), reads=[dv], writes=[dv])
    P.dve(lambda e: e.tensor_copy(out=api(1), in_=S(A1[1])), reads=[dv], writes=[dv])
    for m in range(2, 9):
        cmul(apr(m), api(m), apr(m - 1), api(m - 1), apr(1), api(1), S(TA), S(TB))
    actf(S(RHO), S(X1), AF.Exp, scale=8.0)
    P.dve(lambda e: e.tensor_copy(out=rho8.t[:], in_=S(RHO)), reads=[dv], writes=[dv, rho8])
    P.dve(lambda e: e.reciprocal(out=S(IRHO), in_=S(RHO)), reads=[dv], writes=[dv])
    tt(S(U8R), apr(8), S(IRHO), ALU.mult)
    tt(S(U8I), api(8), S(IRHO), ALU.mult)
    tt(S(TA), S(IRHO), S(IRHO), ALU.mult)
    tt(S(I8R), apr(8), S(TA), ALU.mult)
    tt(S(I8I), api(8), S(TA), ALU.mult)
    ts(S(I8I), S(I8I), -1.0, ALU.mult)
    ts(S(NR), apr(1), -1.0, ALU.add)
    tt(S(TA), S(LR), S(LR), ALU.mult)
    tt(S(TB), S(LI), S(LI), ALU.mult)
    tt(S(DEN), S(TA), S(TB), ALU.add)
    P.dve(lambda e: e.reciprocal(out=S(DEN), in_=S(DEN)), reads=[dv], writes=[dv])
    tt(S(TA), S(NR), S(LR), ALU.mult)
    tt(S(TB), api(1), S(LI), ALU.mult)
    tt(S(FR), S(TA), S(TB), ALU.add)
    tt(S(FR), S(FR), S(DEN), ALU.mult)
    tt(S(TA), api(1), S(LR), ALU.mult)
    tt(S(TB), S(NR), S(LI), ALU.mult)
    tt(S(FI), S(TA), S(TB), ALU.subtract)
    tt(S(FI), S(FI), S(DEN), ALU.mult)

    def bc(ap2, n):
        return ap2.unsqueeze(2).to_broadcast([128, NPR, n])
    t16 = [T[0].t[:, :, 0:16], T[1].t[:, :, 0:16]]
    cmul(Bb[0].t[:], Bb[1].t[:], bc(S(FR), 16), bc(S(FI), 16), Bc[0].t[:], Bc[1].t[:], t16[0], t16[1])
    WD4 = [w.t[:].rearrange("p a (j c) -> p a j c", j=8) for w in WD]
    for j in range(8):
        cmul(WD4[0][:, :, j, :], WD4[1][:, :, j, :], bc(apr(7 - j), 16), bc(api(7 - j), 16), Bb[0].t[:], Bb[1].t[:],
             t16[0], t16[1])
    cmul(WDt[0].t[:], WDt[1].t[:], bc(S(I8R), 128), bc(S(I8I), 128), WD[0].t[:], WD[1].t[:], T[0].t[:], T[1].t[:])
    for comp in range(2):
        for q4 in range(4):
            bank = pb[3 + (q4 % 2)]
            for q in range(4):
                pr = q4 * 4 + q
                P.pe(lambda e, comp=comp, pr=pr, q=q, bank=bank: e.transpose(
                    out=bank.t[:, q * 128:(q + 1) * 128], in_=WD[comp].t[:, pr, :], identity=ident.t[:]),
                    reads=[dv, ident], writes=[bank])
            P.dve(lambda e, comp=comp, q4=q4, bank=bank: e.tensor_copy(
                out=WDT[comp].t[:, q4 * 4:(q4 + 1) * 4, :], in_=bank.t[:].rearrange("p (a b) -> p a b", a=4)),
                reads=[bank], writes=[WDT[comp]])
    Wo4 = [w.t[:].rearrange("p a (j c) -> p a j c", j=8) for w in Wo]
    for j in range(8):
        ar_, ai_ = bc(apr(j + 1), 16), bc(api(j + 1), 16)
        tt(t16[0], Ct[0].t[:], ar_, ALU.mult); tt(t16[1], Ct[1].t[:], ai_, ALU.mult)
        P.dve(lambda e, j=j: e.tensor_tensor(out=Wo4[0][:, :, j, :], in0=t16[0], in1=t16[1], op=ALU.subtract),
              reads=[dv], writes=[dv, Wo[0]])
        tt(t16[0], Ct[0].t[:], ai_, ALU.mult); tt(t16[1], Ct[1].t[:], ar_, ALU.mult)
        tt(t16[0], t16[0], t16[1], ALU.add)
        P.dve(lambda e, j=j: e.tensor_scalar(out=Wo4[1][:, :, j, :], in0=t16[0], scalar1=-1.0, scalar2=None,
                                             op0=ALU.mult), reads=[dv], writes=[dv, Wo[1]])
    for g in range(NG):
        pr, m = divmod(g, 2)
        bank = pb[5 + (g % 2)]
        sl = slice(m * 64, (m + 1) * 64)
        P.pe(lambda e, pr=pr, sl=sl, bank=bank: e.matmul(bank.t[:, 0:128], lhsT=WDt[0].t[sl, pr, :],
                                                          rhs=Wo[0].t[sl, pr, :], start=True, stop=False),
             reads=[dv, Wo[0]], writes=[bank])
        P.pe(lambda e, pr=pr, sl=sl, bank=bank: e.matmul(bank.t[:, 0:128], lhsT=WDt[1].t[sl, pr, :],
                                                          rhs=Wo[1].t[sl, pr, :], start=False, stop=True),
             reads=[dv, Wo[1]], writes=[bank])
        P.dve(lambda e, bank=bank: e.tensor_tensor(out=tK.t[:], in0=bank.t[:, 0:128], in1=maskJ.t[:], op=ALU.mult),
              reads=[bank, maskJ, dv], writes=[dv])
        P.dve(lambda e, g=g: e.scalar_tensor_tensor(out=Kmat.t[:, g, :], in0=ident.t[:], scalar=dcol.t[:, g:g + 1],
                                                    in1=tK.t[:], op0=ALU.mult, op1=ALU.add),
              reads=[dv, ident], writes=[dv, Kmat])
    P.dve(lambda e: e.tensor_copy(out=C1.t[:, :, 0], in_=S(U8R)), reads=[dv], writes=[dv, C1])
    P.dve(lambda e: e.tensor_copy(out=S1.t[:, :, 0], in_=S(U8I)), reads=[dv], writes=[dv, S1])
    s_ = 1
    while s_ < 64:
        cr = C1.t[:, :, s_ - 1:s_].to_broadcast([128, NPR, s_])
        ci = S1.t[:, :, s_ - 1:s_].to_broadcast([128, NPR, s_])
        t1, t2 = T[0].t[:, :, 0:s_], T[1].t[:, :, 0:s_]
        cmul(C1.t[:, :, s_:2 * s_], S1.t[:, :, s_:2 * s_], cr, ci, C1.t[:, :, 0:s_], S1.t[:, :, 0:s_], t1, t2)
        s_ *= 2
    P.dve(lambda e: e.tensor_copy(out=Mtab.t[:], in_=bc(rho8.t[:], 64)), reads=[dv, rho8], writes=[dv, Mtab])
    P.dve(lambda e: e.memset(Mtab.t[:, :, 0], 0.0), reads=[dv], writes=[dv, Mtab])
    for comp in range(2):
        P.dve(lambda e, comp=comp: e.memset(Lf[comp].t[:], 0.0), reads=[dv], writes=[dv, Lf[comp]])
    for h in range(8):
        bank = pb[3 + (h % 2)]
        P.pe(lambda e, h=h, bank=bank: e.transpose(out=bank.t[:, 0:128], in_=wsl.t[:, h, :], identity=ident.t[:]),
             reads=[dv, ident], writes=[bank])
        P.dve(lambda e, h=h, bank=bank: e.tensor_tensor(out=WsT.t[:, h, :], in0=bank.t[:, 0:128], in1=tril.t[:],
                                                        op=ALU.mult), reads=[bank, tril], writes=[WsT])
    P.dve(lambda e: e.memset(bs2.t[:], 0.0), reads=[dv], writes=[dv, bs2])
    P.dve(lambda e: e.tensor_copy(out=bs2.t[0:1, :], in_=bsf.t[0:1, :]), reads=[dv], writes=[dv, bs2])
    P.dve(lambda e: e.tensor_copy(out=bs2.t[32:33, :], in_=bsf.t[32:33, :]), reads=[dv], writes=[dv, bs2])
    P.dve(lambda e: e.tensor_tensor(out=bsf.t[32:33, :], in0=bsf.t[32:33, :], in1=bs2.t[32:33, :], op=ALU.subtract),
          reads=[dv, bs2], writes=[dv])
    P.dve(lambda e: e.tensor_copy(out=bs2.t[32:33, :], in_=bsf.t[32:33, :]), reads=[dv], writes=[dv, bs2])
    P.barrier()
    A.off = off_tmp
    xt = A.alloc("xt", [128, KT, NT], F32)
    off_ht = A.off
    ht = A.alloc("ht", [128, KT, NT], BF16)
    ycat = P.carve("ycat", A.buf, off_ht, [128, KT, NT], BF16)
    P.alias(ht, ycat)
    rstd = A.alloc("rstd", [128, NT], F32)
    off_sq = A.off
    sq = A.alloc("sq", [128, KT, NT], BF16)
    vtok = P.carve("vtok", A.buf, off_sq, [128, 4, NT], BF16)
    ub = P.carve("ub", A.buf, off_sq + 4096, [128, 4, NT], BF16)
    P.alias(sq, vtok); P.alias(sq, ub)
    off_u = A.off
    Ublk = A.alloc("Ublk", [128, NG, 8, 16], BF16)
    Tt = [P.carve("Tt1", A.buf, off_u, [128, NPR, 64], F32), P.carve("Tt2", A.buf, off_u + 4096, [128, NPR, 64], F32)]
    P.alias(Ublk, Tt[0]); P.alias(Ublk, Tt[1])
    UblkT = A.alloc("UblkT", [128, NG, 64], BF16)
    Z = [A.alloc("Zr", [128, NPR, 64], F32), A.alloc("Zi", [128, NPR, 64], F32)]
    W = [A.alloc("Wr", [128, NPR, 64], F32), A.alloc("Wi", [128, NPR, 64], F32)]
    Lb = [A.alloc("Lbr", [128, NPR, 64], BF16), A.alloc("Lbi", [128, NPR, 64], BF16)]
    off_y = A.off
    Yblk = A.alloc("Yblk", [128, 8, NT], F32)
    ybuf = P.carve("ybuf", A.buf, off_y, [128, KT, NT], F32)
    P.alias(Yblk, ybuf)
    ytmp = A.alloc("ytmp", [128, NT], F32)
    sg = A.alloc("sg", [128, 4, NT], BF16)
    vn = A.alloc("vn", [128, NT], BF16)
    vsq = A.alloc("vsq", [128, NT], BF16)
    vrs = A.alloc("vrs", [128, NT], F32)
    t16s = A.alloc("t16s", [128, 2, NPR], F32)
    Dre = pst.t[:, 3 * 512:5 * 512].rearrange("p (a n) -> p a n", a=NPR)
    Dim = pst.t[:, 5 * 512:7 * 512].rearrange("p (a n) -> p a n", a=NPR)
    Dv = [Dre, Dim]
    Dbank = [[pb[3], pb[4]], [pb[5], pb[6]]]
    pbT = {b: pb[b].t.bitcast(BF16) for b in (5, 6, 7)}
    gpost = "g_mix_post"

    for ti in range(c["NTILES"]):
        c["load_x"](xt, ti)
        c["prenorm"](xt, ht, sq, rstd, "g_mix_pre", l)
        for j in range(8):
            bank = pb[3 + (j % 2)]
            for k in range(KT):
                P.pe(lambda e, j=j, k=k, bank=bank: e.matmul(bank.t[0:64, :], lhsT=ht.t[:, k, j:NT:8],
                                                             rhs=w_in.t[:, k, 0:512], start=(k == 0),
                                                             stop=(k == KT - 1)), reads=[ht, w_in], writes=[bank])
            src = bank.t[0:64, :].rearrange("p (g c) -> p g c", g=NG)
            if j % 2 == 0:
                P.act(lambda e, j=j, src=src: e.activation(out=Ublk.t[0:64, :, j, :], in_=src, func=AF.Copy),
                      reads=[bank], writes=[Ublk])
            else:
                P.dve(lambda e, j=j, src=src: e.tensor_copy(out=Ublk.t[0:64, :, j, :], in_=src),
                      reads=[bank], writes=[Ublk])
        for g8 in range(4):
            bno = 5 + (g8 % 2)
            for q in range(8):
                g = g8 * 8 + q
                P.pe(lambda e, g=g, q=q, bno=bno: e.transpose(
                    out=pbT[bno][:, q * 64:(q + 1) * 64], in_=Ublk.t[0:64, g, :, :].rearrange("p j c -> p (j c)"),
                    identity=identb.t[0:64, 0:64]), reads=[Ublk, identb], writes=[pb[bno]])
            P.act(lambda e, g8=g8, bno=bno: e.activation(
                out=UblkT.t[:, g8 * 8:(g8 + 1) * 8, :], in_=pbT[bno][:, 0:512].rearrange("p (a n) -> p a n", a=8),
                func=AF.Copy), reads=[pb[bno]], writes=[UblkT])
        for ct in range(4):
            for grp in range(3):
                col = 512 * (grp + 1) + ct * 128
                bank = pb[1 + ((ct * 3 + grp) % 2)]
                for k in range(KT):
                    P.pe(lambda e, col=col, k=k, bank=bank: e.matmul(bank.t[:], lhsT=w_in.t[:, k, col:col + 128],
                                                                     rhs=ht.t[:, k, :], start=(k == 0),
                                                                     stop=(k == KT - 1)), reads=[ht, w_in], writes=[bank])
                if grp == 0:
                    P.act(lambda e, ct=ct, bank=bank: e.activation(out=sg.t[:, ct, :], in_=bank.t[:], func=AF.Sigmoid),
                          reads=[bank], writes=[sg])
                elif grp == 1:
                    P.act(lambda e, ct=ct, bank=bank: e.activation(out=ub.t[:, ct, :], in_=bank.t[:], func=AF.Copy),
                          reads=[bank], writes=[ub])
                else:
                    P.act(lambda e, bank=bank: e.activation(out=vsq.t[:], in_=bank.t[:], func=AF.Square),
                          reads=[bank], writes=[vsq])
                    P.pe(lambda e: e.matmul(pb[0].t[:], lhsT=blkones.t[:], rhs=vsq.t[:], start=True, stop=True),
                         reads=[vsq, blkones], writes=[pb[0]])
                    c["rstd_from_ss"](pb[0].t[:], vrs, 64, [pb[0]])
                    P.dve(lambda e, ct=ct, bank=bank: e.scalar_tensor_tensor(
                        out=vn.t[:], in0=bank.t[:], scalar=gvv.t[:, l, ct:ct + 1], in1=vrs.t[:], op0=ALU.mult,
                        op1=ALU.mult), reads=[bank, gvv, vrs], writes=[vn])
                    for cc in range(4):
                        P.pe(lambda e, cc=cc: e.transpose(out=pbT[7][:, cc * 128:(cc + 1) * 128],
                                                          in_=vn.t[:, cc * 128:(cc + 1) * 128], identity=identb.t[:]),
                             reads=[vn, identb], writes=[pb[7]])
                    P.dve(lambda e, ct=ct: e.tensor_copy(
                        out=vtok.t[:, :, ct * 128:(ct + 1) * 128],
                        in_=pbT[7][:, 0:512].rearrange("p (a n) -> p a n", a=4)), reads=[pb[7]], writes=[vtok])
        if ti > 0:
            for comp in range(2):
                P.act(lambda e, comp=comp: e.activation(out=Lf[comp].t[:, :, 0], in_=Lf[comp].t[:, :, 64], func=AF.Copy),
                      reads=[Lf[comp]], writes=[Lf[comp]])
        for g in range(NG):
            pr, m = divmod(g, 2)
            for comp in range(2):
                bank = Dbank[comp][pr // 8]
                P.pe(lambda e, g=g, pr=pr, m=m, comp=comp: e.matmul(
                    Dv[comp][m * 64:(m + 1) * 64, pr, :], lhsT=WDT[comp].t[:, pr, m * 64:(m + 1) * 64],
                    rhs=UblkT.t[:, g, :], start=True, stop=True), reads=[UblkT, WDT[comp]], writes=[bank])
        dr = [pb[3], pb[4], pb[5], pb[6]]

        def dtt(out, a, b, op, rd, wr):
            P.dve(lambda e: e.tensor_tensor(out=out, in0=a, in1=b, op=op), reads=rd, writes=wr)
        dtt(Tt[0].t[:], Dre, C1.t[:], ALU.mult, dr + [C1], [Tt[0]])
        dtt(Tt[1].t[:], Dim, S1.t[:], ALU.mult, dr + [S1], [Tt[1]])
        dtt(Z[0].t[:], Tt[0].t[:], Tt[1].t[:], ALU.add, Tt, [Z[0]])
        dtt(Tt[0].t[:], Dim, C1.t[:], ALU.mult, dr + [C1], [Tt[0]])
        dtt(Tt[1].t[:], Dre, S1.t[:], ALU.mult, dr + [S1], [Tt[1]])
        dtt(Z[1].t[:], Tt[0].t[:], Tt[1].t[:], ALU.subtract, Tt, [Z[1]])
        for comp in range(2):
            dtt(t16s.t[:, comp, :], Lf[comp].t[:, :, 0], rho8.t[:], ALU.mult, [Lf[comp], rho8], [t16s])
            dtt(Z[comp].t[:, :, 0], Z[comp].t[:, :, 0], t16s.t[:, comp, :], ALU.add, [Z[comp], t16s], [Z[comp]])
            P.dve(lambda e, comp=comp: e.tensor_tensor_scan(
                out=W[comp].t[:].rearrange("p a n -> p (a n)"), data0=Mtab.t[:].rearrange("p a n -> p (a n)"),
                data1=Z[comp].t[:].rearrange("p a n -> p (a n)"), initial=0.0, op0=ALU.mult, op1=ALU.add),
                reads=[Mtab, Z[comp]], writes=[W[comp]])
        dtt(Tt[0].t[:], C1.t[:], W[0].t[:], ALU.mult, [C1, W[0]], [Tt[0]])
        dtt(Tt[1].t[:], S1.t[:], W[1].t[:], ALU.mult, [S1, W[1]], [Tt[1]])
        dtt(Lf[0].t[:, :, 1:65], Tt[0].t[:], Tt[1].t[:], ALU.subtract, Tt, [Lf[0]])
        dtt(Tt[0].t[:], C1.t[:], W[1].t[:], ALU.mult, [C1, W[1]], [Tt[0]])
        dtt(Tt[1].t[:], S1.t[:], W[0].t[:], ALU.mult, [S1, W[0]], [Tt[1]])
        dtt(Lf[1].t[:, :, 1:65], Tt[0].t[:], Tt[1].t[:], ALU.add, Tt, [Lf[1]])
        for comp in range(2):
            P.act(lambda e, comp=comp: e.activation(out=Lb[comp].t[:], in_=Lf[comp].t[:, :, 0:64], func=AF.Copy),
                  reads=[Lf[comp]], writes=[Lb[comp]])
        for g4 in range(8):
            bank = pb[1 + (g4 % 2)]
            for q in range(4):
                g = g4 * 4 + q
                pr, m = divmod(g, 2)
                sl = slice(m * 64, (m + 1) * 64)
                osl = bank.t[0:64, q * 128:(q + 1) * 128]
                P.pe(lambda e, g=g, osl=osl: e.matmul(osl, lhsT=UblkT.t[:, g, :], rhs=Kmat.t[:, g, :], start=True,
                                                      stop=False), reads=[UblkT, Kmat], writes=[bank])
                P.pe(lambda e, pr=pr, sl=sl, osl=osl: e.matmul(osl, lhsT=Lb[0].t[sl, pr, :], rhs=Wo[0].t[sl, pr, :],
                                                               start=False, stop=False), reads=[Lb[0], Wo[0]],
                     writes=[bank])
                P.pe(lambda e, pr=pr, sl=sl, osl=osl: e.matmul(osl, lhsT=Lb[1].t[sl, pr, :], rhs=Wo[1].t[sl, pr, :],
                                                               start=False, stop=True), reads=[Lb[1], Wo[1]],
                     writes=[bank])
            P.dve(lambda e, g4=g4, bank=bank: e.tensor_copy(
                out=Yblk.t[0:64, :, g4 * 64:(g4 + 1) * 64].rearrange("p j (g c) -> p j g c", g=4),
                in_=bank.t[0:64, :].rearrange("p (g j c) -> p j g c", g=4, j=8)), reads=[bank], writes=[Yblk])
        for ct in range(4):
            bank = pb[3 + (ct % 2)]
            for j in range(8):
                P.pe(lambda e, ct=ct, j=j, bank=bank: e.transpose(
                    out=bank.t[:, j * 64:(j + 1) * 64], in_=Yblk.t[0:64, j, ct * 128:(ct + 1) * 128],
                    identity=ident.t[0:64, 0:64]), reads=[Yblk, ident], writes=[bank])
            P.act(lambda e, bank=bank: e.activation(out=ytmp.t[:].rearrange("p (n j) -> p n j", j=8),
                                                    in_=bank.t[:].rearrange("p (j n) -> p n j", j=8),
                                                    func=AF.Gelu_apprx_tanh), reads=[bank], writes=[ytmp])
            P.pool(lambda e, ct=ct: e.tensor_tensor(out=ycat.t[:, ct, :], in0=ytmp.t[:], in1=sg.t[:, ct, :],
                                                    op=ALU.mult), reads=[ytmp, sg], writes=[ycat])
        for ct in range(4):
            bank = pb[1 + (ct % 2)]
            for cc in range(4):
                for hh in range(2):
                    h = 2 * ct + hh
                    osl = bank.t[hh * 64:(hh + 1) * 64, cc * 128:(cc + 1) * 128]
                    P.pe(lambda e, cc=cc, h=h, osl=osl: e.matmul(osl, lhsT=vtok.t[:, cc, h * 64:(h + 1) * 64],
                                                                 rhs=WsT.t[:, h, :], start=True, stop=False),
                         reads=[vtok, WsT], writes=[bank])
                    P.pe(lambda e, h=h, osl=osl: e.matmul(osl, lhsT=onesb.t[0:64, 0:64],
                                                          rhs=bs2.t[0:64, h * 128:(h + 1) * 128], start=False,
                                                          stop=True), reads=[bs2, onesb], writes=[bank])
            P.dve(lambda e, ct=ct, bank=bank: e.tensor_tensor(out=ycat.t[:, 4 + ct, :], in0=bank.t[:],
                                                              in1=ub.t[:, ct, :], op=ALU.mult),
                  reads=[bank, ub], writes=[ycat])

        def mm(o, bank):
            for k in range(KT):
                P.pe(lambda e, o=o, k=k, bank=bank: e.matmul(bank.t[:], lhsT=w_out.t[:, k, o * 128:(o + 1) * 128],
                                                             rhs=ycat.t[:, k, :], start=(k == 0), stop=(k == KT - 1)),
                     reads=[w_out, ycat], writes=[bank])
        c["postnorm_res"](xt, ybuf, sq, rstd, gpost, l, mm)
        c["store_x"](xt, ti)
```

```python
import contextlib
import numpy as np
import concourse.bass as bass
import concourse.mybir as mybir
from concourse.bass_utils import run_bass_kernel_spmd

F32 = mybir.dt.float32
BF16 = mybir.dt.bfloat16
AF = mybir.ActivationFunctionType
ALU = mybir.AluOpType

ENGS = ("pe", "dve", "act", "pool", "sp")
SAME_ENGINE_SYNC = True


class Res:
    def __init__(self, name):
        self.name = name
        self.last_w = None
        self.readers = {}
        self._subs = {}

    def sub(self, k):
        r = self._subs.get(k)
        if r is None:
            r = Res(f"{self.name}.{k}")
            self._subs[k] = r
        return r


class Buf(Res):
    def __init__(self, name, t):
        super().__init__(name)
        self.t = t


class Prog:
    def __init__(self, nc):
        self.nc = nc
        self.stack = contextlib.ExitStack()
        self.q = {e: [] for e in ENGS}
        self.ecnt = {e: 0 for e in ENGS}
        self.seen = {e: {} for e in ENGS}
        self.dmacnt = {}
        self.nres = 0

    def sbuf(self, name, shape, dtype):
        t = self.stack.enter_context(self.nc.sbuf_tensor(name, list(shape), dtype))
        return Buf(name, t)

    def psum(self, name, shape, dtype):
        t = self.stack.enter_context(self.nc.psum_tensor(name, list(shape), dtype))
        return Buf(name, t)

    def res(self, name):
        return Res(name)

    def _record(self, eng, fn, reads, writes, dmakey=None, inc=1):
        reads = list(reads)
        writes = list(writes)
        for r in list(reads):
            reads.extend(getattr(r, "also", ()))
        for w in list(writes):
            writes.extend(getattr(w, "also", ()))
        waits = {}
        seen = self.seen[eng]

        def need(ev):
            if ev is None:
                return
            key, val, src = ev
            if src == eng and (eng == "pe" or not SAME_ENGINE_SYNC):
                return
            if seen.get(key, 0) >= val:
                return
            if waits.get(key, 0) < val:
                waits[key] = val

        for r in reads:
            need(r.last_w)
        for w in writes:
            lw = w.last_w
            if not (dmakey is not None and lw is not None and lw[0] == ("dma", dmakey)):
                need(lw)
            for k, (v, s) in w.readers.items():
                need((k, v, s))
        for k, v in waits.items():
            seen[k] = v
        if dmakey is None:
            self.ecnt[eng] += 1
            ev = (("eng", eng), self.ecnt[eng], eng)
        else:
            self.dmacnt[dmakey] = self.dmacnt.get(dmakey, 0) + inc
            ev = (("dma", dmakey), self.dmacnt[dmakey], None)
        self.q[eng].append((fn, list(waits.items()), ev, inc if dmakey is not None else 1))
        for r in reads:
            old = r.readers.get(ev[0])
            if old is None or old[0] < ev[1]:
                r.readers[ev[0]] = (ev[1], ev[2])
        for w in writes:
            w.last_w = ev
            w.readers = {}
        return ev

    def pe(self, fn, reads=(), writes=()):
        return self._record("pe", fn, reads, writes)

    def dve(self, fn, reads=(), writes=()):
        return self._record("dve", fn, reads, writes)

    def act(self, fn, reads=(), writes=()):
        return self._record("act", fn, reads, writes)

    def pool(self, fn, reads=(), writes=()):
        return self._record("pool", fn, reads, writes)

    def eng(self, e, fn, reads=(), writes=()):
        return self._record(e, fn, reads, writes)

    def dma(self, eng, out, in_, reads=(), writes=(), key=None, **kw):
        key = key or writes[0].name
        return self._record(eng, lambda e: e.dma_start(out=out, in_=in_, **kw), reads, writes, dmakey=key, inc=16)

    def collective(self, fn, reads=(), writes=(), key=None):
        key = key or writes[0].name
        return self._record("pool", fn, reads, writes, dmakey=key, inc=16)

    def finish(self, finals):
        nc = self.nc
        waits = {}
        for r in finals:
            key, val, _ = r.last_w
            waits[key] = max(waits.get(key, 0), val)
        self.q["sp"].append((None, list(waits.items()), None, 0))
        keys = [("eng", e) for e in ENGS if self.ecnt[e] > 0] + [("dma", k) for k in self.dmacnt]
        sems = {}
        for i, k in enumerate(keys):
            sems[k] = self.stack.enter_context(nc.semaphore(f"s{i}_{k[1]}"[:24].replace(".", "_")))
        self.nsems = len(keys)
        q = self.q

        def replay(ename, e):
            for fn, ws, ev, inc in q[ename]:
                for k, v in ws:
                    e.wait_ge(sems[k], v)
                if fn is None:
                    continue
                ins = fn(e)
                ins.then_inc(sems[ev[0]], inc)

        with nc.Block() as block:
            @block.tensor
            def _(e):
                replay("pe", e)

            @block.vector
            def _(e):
                replay("dve", e)

            @block.scalar
            def _(e):
                replay("act", e)

            @block.gpsimd
            def _(e):
                replay("pool", e)

            @block.sync
            def _(e):
                replay("sp", e)
        self.stack.close()

    def barrier(self):
        for e in ENGS:
            waits = {}
            for e2 in ENGS:
                if e2 != e and self.ecnt[e2] > self.seen[e].get(("eng", e2), 0):
                    waits[("eng", e2)] = self.ecnt[e2]
            if e != "pe" and self.ecnt[e] > self.seen[e].get(("eng", e), 0):
                waits[("eng", e)] = self.ecnt[e]
            for k, v in self.dmacnt.items():
                if v > self.seen[e].get(("dma", k), 0):
                    waits[("dma", k)] = v
            for k, v in waits.items():
                self.seen[e][k] = v
            if waits:
                self.q[e].append((None, list(waits.items()), None, 0))

    @staticmethod
    def alias(a, b):
        a.also = list(getattr(a, "also", [])) + [b]
        b.also = list(getattr(b, "also", [])) + [a]

    def carve(self, name, arena, off_bytes, shape, dtype):
        esz = 4 if dtype == F32 else 2
        n = 1
        for s in shape[1:]:
            n *= s
        nb = n * esz
        assert off_bytes % 4 == 0 and nb % 4 == 0
        a = arena.t[:, off_bytes // 4:(off_bytes + nb) // 4]
        if dtype != F32:
            a = a.bitcast(dtype)
        if len(shape) == 3:
            a = a.rearrange("p (a b) -> p a b", a=shape[1])
        elif len(shape) == 4:
            a = a.rearrange("p (a b c) -> p a b c", a=shape[1], b=shape[2])
        b = Buf(name, a)
        b.nbytes = nb
        return b


D = 1024
KT = 8
NT = 512
PI = 3.14159265358979
PNAMES = ["g_mix_pre", "w_in", "lam_re", "lam_im", "log_dt", "b_re", "b_im", "c_re", "c_im", "d_skip", "g_v",
          "w_s", "b_s", "w_out", "g_mix_post", "g_x_pre", "g_mem", "w_q", "w_k", "w_v", "w_o", "g_x_post",
          "g_ffn_pre", "w_up", "conv_w", "conv_b", "w_down", "g_ffn_post"]
PSHAPES = {"g_mix_pre": [4, 1024], "w_in": [4, 1024, 2048], "lam_re": [4, 32, 64], "lam_im": [4, 32, 64],
           "log_dt": [4, 32], "b_re": [4, 32, 64, 16], "b_im": [4, 32, 64, 16], "c_re": [4, 32, 16, 64],
           "c_im": [4, 32, 16, 64], "d_skip": [4, 32, 16], "g_v": [4, 512], "w_s": [4, 8, 128, 128],
           "b_s": [4, 8, 128], "w_out": [4, 1024, 1024], "g_mix_post": [4, 1024], "g_x_pre": [4, 1024],
           "g_mem": [4, 1024], "w_q": [4, 1024, 1024], "w_k": [4, 1024, 1024], "w_v": [4, 1024, 1024],
           "w_o": [4, 1024, 1024], "g_x_post": [4, 1024], "g_ffn_pre": [4, 1024], "w_up": [4, 1024, 5632],
           "conv_w": [4, 3, 5632], "conv_b": [4, 5632], "w_down": [4, 2816, 1024], "g_ffn_post": [4, 1024]}


class Arena:
    def __init__(self, P, buf, total):
        self.P, self.buf, self.total, self.off = P, buf, total, 0

    def alloc(self, name, shape, dtype):
        b = self.P.carve(name, self.buf, self.off, shape, dtype)
        self.off += (b.nbytes + 31) // 32 * 32
        assert self.off <= self.total, (name, self.off, self.total)
        return b

    def reset(self):
        self.P.barrier()
        self.off = 0


def build_program(S=8192, DEPTH=4, STAGES=("mix", "xat", "ffn")):
    nc = bass.Bass("TRN2", target_bir_lowering=False)
    NTILES = S // NT
    x_d = nc.dram_tensor("x", [S, D], F32, kind="ExternalInput").ap()
    mem_d = nc.dram_tensor("mem", [256, D], F32, kind="ExternalInput").ap()
    pd = {n: nc.dram_tensor(n, PSHAPES[n], F32, kind="ExternalInput").ap() for n in PNAMES}
    out_d = nc.dram_tensor("out", [S, D], F32, kind="ExternalOutput").ap()
    xs_d = nc.dram_tensor("xs", [D, S], F32).ap()
    xs_v = xs_d.rearrange("(k p) t -> p k t", p=128)
    P = Prog(nc)
    xs_r = [P.res(f"xs{t}") for t in range(NTILES)]
    out_r = P.res("out")

    ident = P.sbuf("ident", [128, 128], F32)
    identb = P.sbuf("identb", [128, 128], BF16)
    onesb = P.sbuf("onesb", [128, 128], BF16)
    blkones = P.sbuf("blkones", [128, 128], BF16)
    maskJ = P.sbuf("maskJ", [128, 128], F32)
    tril = P.sbuf("tril", [128, 128], F32)
    cb = P.sbuf("cbias", [128, 4], F32)
    gv = {n: P.sbuf("gv_" + n, [128, 4, 8], F32) for n in
          ["g_mix_pre", "g_mix_post", "g_x_pre", "g_mem", "g_x_post", "g_ffn_pre", "g_ffn_post"]}
    gvv = P.sbuf("gv_g_v", [128, 4, 4], F32)
    memhT = P.sbuf("memhT", [128, 8, 256], F32)
    ARENA_BYTES = 196 * 1024
    arena_buf = P.sbuf("arena", [128, ARENA_BYTES // 4], F32)
    A = Arena(P, arena_buf, ARENA_BYTES)
    pst = P.psum("pst", [128, 8 * 512], F32)
    pb = []
    for b in range(8):
        r = Buf(f"pb{b}", pst.t[:, b * 512:(b + 1) * 512])
        pb.append(r)

    def cst(e, ap, v, wr):
        P.eng(e, lambda en: en.memset(ap, v), writes=wr)

    cst("pool", ident.t[:], 1.0, [ident])
    P.pool(lambda e: e.affine_select(out=ident.t[:], in_=ident.t[:], pattern=[[1, 128]], compare_op=ALU.is_equal,
                                     fill=0.0, base=0, channel_multiplier=-1), reads=[ident], writes=[ident])
    P.dve(lambda e: e.tensor_copy(out=identb.t[:], in_=ident.t[:]), reads=[ident], writes=[identb])
    cst("pool", onesb.t[:], 1.0, [onesb])
    cst("pool", blkones.t[:], 0.0, [blkones])
    cst("pool", blkones.t[0:64, 0:64], 1.0, [blkones])
    cst("pool", blkones.t[64:128, 64:128], 1.0, [blkones])
    cst("pool", maskJ.t[:], 1.0, [maskJ])
    P.pool(lambda e: e.affine_select(out=maskJ.t[:].rearrange("p (j c) -> p j c", j=8),
                                     in_=maskJ.t[:].rearrange("p (j c) -> p j c", j=8),
                                     pattern=[[16, 8], [0, 16]], compare_op=ALU.is_ge, fill=0.0, base=15,
                                     channel_multiplier=-1), reads=[maskJ], writes=[maskJ])
    cst("pool", tril.t[:], 1.0, [tril])
    P.pool(lambda e: e.affine_select(out=tril.t[:], in_=tril.t[:], pattern=[[1, 128]], compare_op=ALU.is_ge,
                                     fill=0.0, base=0, channel_multiplier=-1), reads=[tril], writes=[tril])
    cst("pool", cb.t[:, 0:1], PI / 2, [cb])
    cst("pool", cb.t[:, 1:2], 1e-6, [cb])
    cst("pool", cb.t[:, 2:3], 0.0, [cb])
    with nc.allow_non_contiguous_dma(reason="small param loads"):
        pass
    import os
    KDBG = os.environ.get("KDBG", "")
    for n, t in (gv.items() if "nogv" not in KDBG else []):
        P.dma("sp", t.t[:], pd[n].rearrange("l (k p) -> p l k", p=128), writes=[t], allow_slow_non_contiguous=True)
    if "nogv" not in KDBG:
      P.dma("sp", gvv.t[:], pd["g_v"].rearrange("l (k p) -> p l k", p=128), writes=[gvv], allow_slow_non_contiguous=True)

    def rstd_from_ss(ss_ps, rstd, n_feat, rd):
        P.act(lambda e: e.activation(out=rstd.t[:], in_=ss_ps, func=AF.Sqrt, bias=cb.t[:, 1:2], scale=1.0 / n_feat),
              reads=rd + [cb], writes=[rstd])
        P.dve(lambda e: e.reciprocal(out=rstd.t[:], in_=rstd.t[:]), reads=[rstd], writes=[rstd])

    def load_x(xt, ti):
        P.dma("sp", xt.t[:], xs_v[:, :, ti * NT:(ti + 1) * NT], reads=[xs_r[ti]], writes=[xt])

    def store_x(xt, ti):
        P.dma("sp", xs_v[:, :, ti * NT:(ti + 1) * NT], xt.t[:], reads=[xt], writes=[xs_r[ti]])

    def prenorm(xt, ht, sq, rstd, gname, l):
        P.act(lambda e: e.activation(out=sq.t[:], in_=xt.t[:], func=AF.Square), reads=[xt], writes=[sq])
        for k in range(KT):
            P.pe(lambda e, k=k: e.matmul(pb[0].t[:], lhsT=onesb.t[:], rhs=sq.t[:, k, :], start=(k == 0),
                                         stop=(k == KT - 1)), reads=[sq, onesb], writes=[pb[0]])
        rstd_from_ss(pb[0].t[:], rstd, D, [pb[0]])
        g = gv[gname]
        for k in range(KT):
            P.dve(lambda e, k=k: e.scalar_tensor_tensor(out=ht.t[:, k, :], in0=xt.t[:, k, :], scalar=g.t[:, l, k:k + 1],
                                                        in1=rstd.t[:], op0=ALU.mult, op1=ALU.mult),
                  reads=[xt, g, rstd], writes=[ht])

    def postnorm_res(xt, ybuf, sq, rstd, gname, l, mm_fn):
        for o in range(KT):
            bank = pb[1 + (o % 2)]
            mm_fn(o, bank)
            P.act(lambda e, o=o, bank=bank: e.activation(out=ybuf.t[:, o, :], in_=bank.t[:], func=AF.Copy),
                  reads=[bank], writes=[ybuf])
            P.dve(lambda e, o=o, bank=bank: e.tensor_tensor(out=sq.t[:, o, :], in0=bank.t[:], in1=ybuf.t[:, o, :],
                                                            op=ALU.mult), reads=[bank, ybuf], writes=[sq])
        for k in range(KT):
            P.pe(lambda e, k=k: e.matmul(pb[0].t[:], lhsT=onesb.t[:], rhs=sq.t[:, k, :], start=(k == 0),
                                         stop=(k == KT - 1)), reads=[sq, onesb], writes=[pb[0]])
        rstd_from_ss(pb[0].t[:], rstd, D, [pb[0]])
        g = gv[gname]
        for o in range(KT):
            P.dve(lambda e, o=o: e.scalar_tensor_tensor(out=ybuf.t[:, o, :], in0=ybuf.t[:, o, :],
                                                        scalar=g.t[:, l, o:o + 1], in1=rstd.t[:], op0=ALU.mult,
                                                        op1=ALU.mult), reads=[ybuf, g, rstd], writes=[ybuf])
            P.pool(lambda e, o=o: e.tensor_tensor(out=xt.t[:, o, :], in0=xt.t[:, o, :], in1=ybuf.t[:, o, :],
                                                  op=ALU.add), reads=[xt, ybuf], writes=[xt])

    def load_w(dst, src_ap, ncols, eng="pool"):
        c0 = 0
        while c0 < ncols:
            c1 = min(ncols, c0 + 2048)
            P.dma(eng, dst.t[:, :, c0:c1], src_ap[:, :, c0:c1], writes=[dst])
            c0 = c1

    xtok = A.alloc("xtok", [128, 4, D], F32)
    xt0 = A.alloc("xt0", [128, KT, NT], F32)
    for ti in range(NTILES):
        P.dma("sp", xtok.t[:], x_d[ti * NT:(ti + 1) * NT, :].rearrange("(s p) d -> p s d", p=128), writes=[xtok])
        for k in range(KT):
            bank = pb[1 + (k % 4)]
            for s in range(4):
                P.pe(lambda e, k=k, s=s, bank=bank: e.transpose(out=bank.t[:, s * 128:(s + 1) * 128],
                                                                in_=xtok.t[:, s, k * 128:(k + 1) * 128],
                                                                identity=ident.t[:]),
                     reads=[xtok, ident], writes=[bank])
            if k % 2 == 0:
                P.act(lambda e, k=k, bank=bank: e.activation(out=xt0.t[:, k, :], in_=bank.t[:], func=AF.Copy),
                      reads=[bank], writes=[xt0])
            else:
                P.dve(lambda e, k=k, bank=bank: e.tensor_copy(out=xt0.t[:, k, :], in_=bank.t[:]),
                      reads=[bank], writes=[xt0])
        store_x(xt0, ti)
    if "nomem" in KDBG:
        return_early_mem = True
    memt = A.alloc("memt", [128, 2, D], F32)
    msq = A.alloc("msq", [128, D], F32)
    mss = A.alloc("mss", [128, 2], F32)
    P.dma("sp", memt.t[:], mem_d.rearrange("(s p) d -> p s d", p=128), writes=[memt])
    for s in range(2):
        P.act(lambda e, s=s: e.activation(out=msq.t[:], in_=memt.t[:, s, :], func=AF.Square,
                                          accum_out=mss.t[:, s:s + 1]), reads=[memt], writes=[msq, mss])
    P.act(lambda e: e.activation(out=mss.t[:], in_=mss.t[:], func=AF.Sqrt, bias=cb.t[:, 1:2], scale=1.0 / D),
          reads=[mss, cb], writes=[mss])
    P.dve(lambda e: e.reciprocal(out=mss.t[:], in_=mss.t[:]), reads=[mss], writes=[mss])
    for s in range(2):
        P.dve(lambda e, s=s: e.tensor_scalar(out=memt.t[:, s, :], in0=memt.t[:, s, :], scalar1=mss.t[:, s:s + 1],
                                             scalar2=None, op0=ALU.mult), reads=[memt, mss], writes=[memt])
    for k in range(KT):
        bank = pb[1 + (k % 4)]
        for s in range(2):
            P.pe(lambda e, k=k, s=s, bank=bank: e.transpose(out=bank.t[:, s * 128:(s + 1) * 128],
                                                            in_=memt.t[:, s, k * 128:(k + 1) * 128],
                                                            identity=ident.t[:]), reads=[memt, ident], writes=[bank])
        P.act(lambda e, k=k, bank=bank: e.activation(out=memhT.t[:, k, :], in_=bank.t[:, 0:256], func=AF.Copy),
              reads=[bank], writes=[memhT])

    ctx = dict(nc=nc, P=P, A=A, pb=pb, pd=pd, gv=gv, gvv=gvv, cb=cb, ident=ident, identb=identb, onesb=onesb,
               blkones=blkones, maskJ=maskJ, tril=tril, memhT=memhT, load_x=load_x, store_x=store_x,
               prenorm=prenorm, postnorm_res=postnorm_res, load_w=load_w, rstd_from_ss=rstd_from_ss,
               NTILES=NTILES, xs_r=xs_r, xs_v=xs_v, pst=pst)
    for l in range(DEPTH):
        if "mix" in STAGES:
            A.reset()
            mixer_layer(ctx, l)
        if "xat" in STAGES:
            A.reset()
            xattn_layer(ctx, l)
        if "ffn" in STAGES:
            A.reset()
            ffn_layer(ctx, l)

    A.reset()
    xt0e = A.alloc("xt0e", [128, KT, NT], F32)
    xtoke = A.alloc("xtoke", [128, 4, D], F32)
    for ti in range(NTILES):
        load_x(xt0e, ti)
        for s in range(4):
            for kh in range(2):
                bank = pb[1 + ((2 * s + kh) % 4)]
                for kk in range(4):
                    k = kh * 4 + kk
                    P.pe(lambda e, k=k, kk=kk, s=s, bank=bank: e.transpose(
                        out=bank.t[:, kk * 128:(kk + 1) * 128], in_=xt0e.t[:, k, s * 128:(s + 1) * 128],
                        identity=ident.t[:]), reads=[xt0e, ident], writes=[bank])
                if kh == 0:
                    P.act(lambda e, s=s, kh=kh, bank=bank: e.activation(out=xtoke.t[:, s, kh * 512:(kh + 1) * 512],
                                                                        in_=bank.t[:], func=AF.Copy),
                          reads=[bank], writes=[xtoke])
                else:
                    P.dve(lambda e, s=s, kh=kh, bank=bank: e.tensor_copy(out=xtoke.t[:, s, kh * 512:(kh + 1) * 512],
                                                                         in_=bank.t[:]), reads=[bank], writes=[xtoke])
        P.dma("sp", out_d[ti * NT:(ti + 1) * NT, :].rearrange("(s p) d -> p s d", p=128), xtoke.t[:],
              reads=[xtoke], writes=[out_r], key="out")
    P.finish([out_r])
    return nc


def xattn_layer(c, l):
    P, A, pb, pd = c["P"], c["A"], c["pb"], c["pd"]
    onesb, gv, memhT = c["onesb"], c["gv"], c["memhT"]
    wv3 = lambda n: pd[n][l].rearrange("(k p) n -> p k n", p=128)
    wq = A.alloc("wq", [128, KT, D], BF16)
    wo = A.alloc("wo", [128, KT, D], BF16)
    wk = A.alloc("wk", [128, KT, D], BF16)
    wvv = A.alloc("wv", [128, KT, D], BF16)
    for w, n in ((wk, "w_k"), (wvv, "w_v"), (wq, "w_q"), (wo, "w_o")):
        c["load_w"](w, wv3(n), D)
    memn = A.alloc("memn", [128, KT, 256], BF16)
    kT = A.alloc("kT", [128, KT, 256], BF16)
    vt = A.alloc("vt", [128, 2, D], BF16)
    xt = A.alloc("xt", [128, KT, NT], F32)
    ht = A.alloc("ht", [128, KT, NT], BF16)
    sq = A.alloc("sq", [128, KT, NT], BF16)
    rstd = A.alloc("rstd", [128, NT], F32)
    qT = A.alloc("qT", [128, KT, NT], BF16)
    expT = A.alloc("expT", [128, 2, NT], BF16)
    rden = A.alloc("rden", [128, NT], F32)
    oT = A.alloc("oT", [128, KT, NT], BF16)
    ybuf = A.alloc("ybuf", [128, KT, NT], F32)
    gm = gv["g_mem"]
    for k in range(KT):
        P.dve(lambda e, k=k: e.tensor_scalar(out=memn.t[:, k, :], in0=memhT.t[:, k, :], scalar1=gm.t[:, l, k:k + 1],
                                             scalar2=None, op0=ALU.mult), reads=[memhT, gm], writes=[memn])
    for o in range(KT):
        bank = pb[3 + (o % 2)]
        for k in range(KT):
            P.pe(lambda e, o=o, k=k, bank=bank: e.matmul(bank.t[:, 0:256], lhsT=wk.t[:, k, o * 128:(o + 1) * 128],
                                                         rhs=memn.t[:, k, :], start=(k == 0), stop=(k == KT - 1)),
                 reads=[wk, memn], writes=[bank])
        P.act(lambda e, o=o, bank=bank: e.activation(out=kT.t[:, o, :], in_=bank.t[:, 0:256], func=AF.Copy),
              reads=[bank], writes=[kT])
    for mt in range(2):
        for hf in range(2):
            bank = pb[3 + (hf % 2)]
            for k in range(KT):
                P.pe(lambda e, mt=mt, hf=hf, k=k, bank=bank: e.matmul(
                    bank.t[:], lhsT=memn.t[:, k, mt * 128:(mt + 1) * 128], rhs=wvv.t[:, k, hf * 512:(hf + 1) * 512],
                    start=(k == 0), stop=(k == KT - 1)), reads=[wvv, memn], writes=[bank])
            P.act(lambda e, mt=mt, hf=hf, bank=bank: e.activation(out=vt.t[:, mt, hf * 512:(hf + 1) * 512],
                                                                  in_=bank.t[:], func=AF.Copy),
                  reads=[bank], writes=[vt])
    for ti in range(c["NTILES"]):
        c["load_x"](xt, ti)
        c["prenorm"](xt, ht, sq, rstd, "g_x_pre", l)
        for o in range(KT):
            bank = pb[3 + (o % 2)]
            for k in range(KT):
                P.pe(lambda e, o=o, k=k, bank=bank: e.matmul(bank.t[:], lhsT=wq.t[:, k, o * 128:(o + 1) * 128],
                                                             rhs=ht.t[:, k, :], start=(k == 0), stop=(k == KT - 1)),
                     reads=[wq, ht], writes=[bank])
            P.act(lambda e, o=o, bank=bank: e.activation(out=qT.t[:, o, :], in_=bank.t[:], func=AF.Copy),
                  reads=[bank], writes=[qT])
        for h in range(4):
            for mt in range(2):
                bank = pb[3 + (mt % 2)]
                for dd in range(2):
                    P.pe(lambda e, h=h, mt=mt, dd=dd, bank=bank: e.matmul(
                        bank.t[:], lhsT=kT.t[:, 2 * h + dd, mt * 128:(mt + 1) * 128], rhs=qT.t[:, 2 * h + dd, :],
                        start=(dd == 0), stop=(dd == 1)), reads=[kT, qT], writes=[bank])
                P.act(lambda e, mt=mt, bank=bank: e.activation(out=expT.t[:, mt, :], in_=bank.t[:], func=AF.Exp,
                                                               scale=1.0 / 16.0), reads=[bank], writes=[expT])
            for mt in range(2):
                P.pe(lambda e, mt=mt: e.matmul(pb[5].t[:], lhsT=onesb.t[:], rhs=expT.t[:, mt, :], start=(mt == 0),
                                               stop=(mt == 1)), reads=[expT, onesb], writes=[pb[5]])
            P.dve(lambda e: e.reciprocal(out=rden.t[:], in_=pb[5].t[:]), reads=[pb[5]], writes=[rden])
            for dd in range(2):
                bank = pb[6 + dd]
                for mt in range(2):
                    P.pe(lambda e, h=h, mt=mt, dd=dd, bank=bank: e.matmul(
                        bank.t[:], lhsT=vt.t[:, mt, (2 * h + dd) * 128:(2 * h + dd + 1) * 128], rhs=expT.t[:, mt, :],
                        start=(mt == 0), stop=(mt == 1)), reads=[vt, expT], writes=[bank])
                P.dve(lambda e, h=h, dd=dd, bank=bank: e.tensor_tensor(out=oT.t[:, 2 * h + dd, :], in0=bank.t[:],
                                                                       in1=rden.t[:], op=ALU.mult),
                      reads=[bank, rden], writes=[oT])

        def mm(o, bank):
            for k in range(KT):
                P.pe(lambda e, o=o, k=k, bank=bank: e.matmul(bank.t[:], lhsT=wo.t[:, k, o * 128:(o + 1) * 128],
                                                             rhs=oT.t[:, k, :], start=(k == 0), stop=(k == KT - 1)),
                     reads=[wo, oT], writes=[bank])
        c["postnorm_res"](xt, ybuf, sq, rstd, "g_x_post", l, mm)
        c["store_x"](xt, ti)


def ffn_layer(c, l):
    P, A, pb, pd = c["P"], c["A"], c["pb"], c["pd"]
    onesb, gv, cb = c["onesb"], c["gv"], c["cb"]
    NF = 22
    wup = A.alloc("wup", [128, KT, 5632], BF16)
    wdn = A.alloc("wdn", [128, NF, D], BF16)
    c["load_w"](wup, pd["w_up"][l].rearrange("(k p) n -> p k n", p=128), 5632)
    c["load_w"](wdn, pd["w_down"][l].rearrange("(f p) n -> p f n", p=128), D)
    cw = A.alloc("cw", [128, 3, 44], F32)
    cbv = A.alloc("cbv", [128, 44], F32)
    zl = A.alloc("zl", [128, 44, 2], F32)
    P.dma("sp", cw.t[:], pd["conv_w"][l].rearrange("w (f p) -> p w f", p=128), writes=[cw],
          allow_slow_non_contiguous=True)
    P.dma("sp", cbv.t[:], pd["conv_b"][l].rearrange("(f p) -> p f", p=128), writes=[cbv],
          allow_slow_non_contiguous=True)
    zls = [zl.sub(i) for i in range(44)]
    P.pool(lambda e: e.memset(zl.t[:], 0.0), writes=zls)
    bh = A.alloc("bh", [128, 44, 2], F32)
    hh = A.alloc("hh", [128, 2, 44], F32)
    off_xt = A.off
    xt = A.alloc("xt", [128, KT, NT], F32)
    ybuf = P.carve("ybuf_f", A.buf, off_xt, [128, KT, NT], F32)
    off_ht = A.off
    ht = A.alloc("ht", [128, KT, NT], BF16)
    sq2 = P.carve("sq2_f", A.buf, off_ht, [128, KT, NT], BF16)
    rstd = A.alloc("rstd", [128, NT], F32)
    off_g = A.off
    gbuf = A.alloc("gbuf", [128, NF, NT], BF16)
    sq = P.carve("sq_f", A.buf, off_g, [128, KT, NT], BF16)
    acc = [[A.alloc("accv0", [128, NT], F32), A.alloc("accg0", [128, NT], F32)],
           [A.alloc("accv1", [128, NT], F32), A.alloc("accg1", [128, NT], F32)]]
    xo = [A.alloc("xo0", [128, NT], F32), A.alloc("xo1", [128, NT], F32)]
    xs_r = c["xs_r"]
    xs_v = c["xs_v"]
    for ti in range(c["NTILES"]):
        c["load_x"](xt, ti)
        P.pool(lambda e: e.tensor_tensor(out=hh.t[:, 0, :], in0=cw.t[:, 0, :], in1=zl.t[:, :, 0], op=ALU.mult),
               reads=[cw] + zls, writes=[hh])
        P.pool(lambda e: e.tensor_tensor(out=hh.t[:, 1, :], in0=cw.t[:, 1, :], in1=zl.t[:, :, 1], op=ALU.mult),
               reads=[cw] + zls, writes=[hh])
        P.pool(lambda e: e.tensor_tensor(out=hh.t[:, 0, :], in0=hh.t[:, 0, :], in1=hh.t[:, 1, :], op=ALU.add),
               reads=[hh], writes=[hh])
        P.pool(lambda e: e.tensor_tensor(out=bh.t[:, :, 0], in0=hh.t[:, 0, :], in1=cbv.t[:], op=ALU.add),
               reads=[hh, cbv], writes=[bh])
        P.pool(lambda e: e.tensor_tensor(out=hh.t[:, 1, :], in0=cw.t[:, 0, :], in1=zl.t[:, :, 1], op=ALU.mult),
               reads=[cw] + zls, writes=[hh])
        P.pool(lambda e: e.tensor_tensor(out=bh.t[:, :, 1], in0=hh.t[:, 1, :], in1=cbv.t[:], op=ALU.add),
               reads=[hh, cbv], writes=[bh])
        c["prenorm"](xt, ht, sq, rstd, "g_ffn_pre", l)
        for f in range(NF):
            a2 = acc[f % 2]
            bks = (pb[3], pb[4]) if f % 2 == 0 else (pb[5], pb[6])
            for vg in range(2):
                ci = vg * NF + f
                bank = bks[vg]
                for k in range(KT):
                    P.pe(lambda e, ci=ci, k=k, bank=bank: e.matmul(bank.t[:], lhsT=wup.t[:, k, ci * 128:(ci + 1) * 128],
                                                                   rhs=ht.t[:, k, :], start=(k == 0),
                                                                   stop=(k == KT - 1)), reads=[wup, ht], writes=[bank])
            for vg in range(2):
                ci = vg * NF + f
                bank = bks[vg]
                a = a2[vg]
                P.act(lambda e, ci=ci, bank=bank, a=a: e.activation(out=a.t[:, 2:NT], in_=bank.t[:, 2:NT],
                                                                    func=AF.Identity, bias=cbv.t[:, ci:ci + 1],
                                                                    scale=cw.t[:, 2, ci:ci + 1]),
                      reads=[bank, cbv, cw], writes=[a])
                for col in range(2):
                    P.act(lambda e, ci=ci, bank=bank, a=a, col=col: e.activation(
                        out=a.t[:, col:col + 1], in_=bank.t[:, col:col + 1], func=AF.Identity,
                        bias=bh.t[:, ci, col:col + 1], scale=cw.t[:, 2, ci:ci + 1]), reads=[bank, bh, cw], writes=[a])
                P.act(lambda e, ci=ci, bank=bank: e.activation(out=zl.t[:, ci, :], in_=bank.t[:, NT - 2:NT],
                                                               func=AF.Copy), reads=[bank], writes=[zls[ci]])
            for vg in range(2):
                ci = vg * NF + f
                bank = bks[vg]
                a = a2[vg]
                P.dve(lambda e, ci=ci, bank=bank, a=a: e.scalar_tensor_tensor(
                    out=a.t[:, 1:NT], in0=bank.t[:, 0:NT - 1], scalar=cw.t[:, 1, ci:ci + 1], in1=a.t[:, 1:NT],
                    op0=ALU.mult, op1=ALU.add), reads=[bank, cw, a], writes=[a])
                P.dve(lambda e, ci=ci, bank=bank, a=a: e.scalar_tensor_tensor(
                    out=a.t[:, 2:NT], in0=bank.t[:, 0:NT - 2], scalar=cw.t[:, 0, ci:ci + 1], in1=a.t[:, 2:NT],
                    op0=ALU.mult, op1=ALU.add), reads=[bank, cw, a], writes=[a])
            P.act(lambda e, a2=a2: e.activation(out=a2[1].t[:], in_=a2[1].t[:], func=AF.Gelu_apprx_tanh),
                  reads=[a2[1]], writes=[a2[1]])
            P.pool(lambda e, f=f, a2=a2: e.tensor_tensor(out=gbuf.t[:, f, :], in0=a2[0].t[:], in1=a2[1].t[:],
                                                         op=ALU.mult), reads=[a2[0], a2[1]], writes=[gbuf.sub(f)])

        def mm(o, bank):
            for f in range(NF):
                P.pe(lambda e, o=o, f=f, bank=bank: e.matmul(bank.t[:], lhsT=wdn.t[:, f, o * 128:(o + 1) * 128],
                                                             rhs=gbuf.t[:, f, :], start=(f == 0), stop=(f == NF - 1)),
                     reads=[wdn, gbuf.sub(f)], writes=[bank])
        g = gv["g_ffn_post"]
        for o in range(KT):
            bank = pb[1 + (o % 2)]
            mm(o, bank)
            P.act(lambda e, o=o, bank=bank: e.activation(out=ybuf.t[:, o, :], in_=bank.t[:], func=AF.Copy),
                  reads=[bank], writes=[ybuf, xt])
            P.dve(lambda e, o=o, bank=bank: e.tensor_tensor(out=sq2.t[:, o, :], in0=bank.t[:], in1=ybuf.t[:, o, :],
                                                            op=ALU.mult), reads=[bank, ybuf], writes=[sq2, ht])
        for k in range(KT):
            P.pe(lambda e, k=k: e.matmul(pb[0].t[:], lhsT=onesb.t[:], rhs=sq2.t[:, k, :], start=(k == 0),
                                         stop=(k == KT - 1)), reads=[sq2, onesb], writes=[pb[0]])
        c["rstd_from_ss"](pb[0].t[:], rstd, D, [pb[0]])
        for o in range(KT):
            xb = xo[o % 2]
            P.dma("sp", xb.t[:], xs_v[:, o, ti * NT:(ti + 1) * NT], reads=[xs_r[ti]], writes=[xb])
            P.dve(lambda e, o=o: e.scalar_tensor_tensor(out=ybuf.t[:, o, :], in0=ybuf.t[:, o, :],
                                                        scalar=g.t[:, l, o:o + 1], in1=rstd.t[:], op0=ALU.mult,
                                                        op1=ALU.mult), reads=[ybuf, g, rstd], writes=[ybuf, xt])
            P.pool(lambda e, o=o, xb=xb: e.tensor_tensor(out=xb.t[:], in0=xb.t[:], in1=ybuf.t[:, o, :], op=ALU.add),
                   reads=[xb, ybuf], writes=[xb])
            P.dma("sp", xs_v[:, o, ti * NT:(ti + 1) * NT], xb.t[:], reads=[xb], writes=[xs_r[ti]])


def _run(inputs, S=8192, DEPTH=4, STAGES=("mix", "xat", "ffn"), NCORES=8):
    nc = build_program(S=S, DEPTH=DEPTH, STAGES=STAGES)
    in_maps = []
    for c in range(NCORES):
        b = c % 2
        m = {"x": np.ascontiguousarray(inputs["x"][b, :S]), "mem": np.ascontiguousarray(inputs["mem"][b])}
        for n in PNAMES:
            m[n] = np.ascontiguousarray(inputs[n])
        in_maps.append(m)
    import os
    if os.environ.get("KTRACE"):
        res = run_bass_kernel_spmd(nc, in_maps, core_ids=list(range(NCORES)), trace=True)
        print("EXEC_TIME_NS", res.exec_time_ns, flush=True)
    else:
        res = run_bass_kernel_spmd(nc, in_maps, core_ids=list(range(NCORES)))
    return np.stack([res.results[0]["out"], res.results[1 % NCORES]["out"]], axis=0)


def kernel(**inputs):
    inputs = {k: np.asarray(v) for k, v in inputs.items()}
    return _run(inputs).astype(np.float32)


def mixer_layer(c, l):
    P, A, pb, pd = c["P"], c["A"], c["pb"], c["pd"]
    onesb, gv, gvv, cb = c["onesb"], c["gv"], c["gvv"], c["cb"]
    ident, identb, blkones, maskJ, tril = c["ident"], c["identb"], c["blkones"], c["maskJ"], c["tril"]
    pst = c["pst"]
    NG, NPR = 32, 16
    w_in = A.alloc("w_in", [128, KT, 2048], BF16)
    w_out = A.alloc("w_out", [128, KT, D], BF16)
    c["load_w"](w_in, pd["w_in"][l].rearrange("(k p) n -> p k n", p=128), 2048)
    c["load_w"](w_out, pd["w_out"][l].rearrange("(k p) n -> p k n", p=128), D)
    WDT = [A.alloc("WDTr", [128, NPR, 128], BF16), A.alloc("WDTi", [128, NPR, 128], BF16)]
    Wo = [A.alloc("Wor", [128, NPR, 128], BF16), A.alloc("Woi", [128, NPR, 128], BF16)]
    Kmat = A.alloc("Kmat", [128, NG, 128], BF16)
    C1 = A.alloc("C1", [128, NPR, 64], F32)
    S1 = A.alloc("S1", [128, NPR, 64], F32)
    Mtab = A.alloc("Mtab", [128, NPR, 64], F32)
    rho8 = A.alloc("rho8", [128, NPR], F32)
    Lf = [A.alloc("Lfr", [128, NPR, 65], F32), A.alloc("Lfi", [128, NPR, 65], F32)]
    WsT = A.alloc("WsT", [128, 8, 128], BF16)
    bs2 = A.alloc("bs2", [128, 1024], BF16)
    off_tmp = A.off
    dv = P.res("dv")
    sm = A.alloc("sm", [128, 64, NPR], F32)
    Bc = [A.alloc("Bre", [128, NPR, 16], F32), A.alloc("Bim", [128, NPR, 16], F32)]
    Ct = [A.alloc("Ctr", [128, NPR, 16], F32), A.alloc("Cti", [128, NPR, 16], F32)]
    Bb = [A.alloc("Bbr", [128, NPR, 16], F32), A.alloc("Bbi", [128, NPR, 16], F32)]
    WD = [A.alloc("WDr", [128, NPR, 128], F32), A.alloc("WDi", [128, NPR, 128], F32)]
    WDt = [A.alloc("WDtr", [128, NPR, 128], BF16), A.alloc("WDti", [128, NPR, 128], BF16)]
    T = [A.alloc("dT1", [128, NPR, 128], F32), A.alloc("dT2", [128, NPR, 128], F32)]
    dcol = A.alloc("dcol", [128, NG], F32)
    wsl = A.alloc("wsl", [128, 8, 128], F32)
    bsf = A.alloc("bsf", [128, 1024], F32)
    tK = A.alloc("tK", [128, 128], F32)

    def S(i):
        return sm.t[:, i, :]

    def tt(out, a, b, op, eng="dve"):
        P.eng(eng, lambda e: e.tensor_tensor(out=out, in0=a, in1=b, op=op), reads=[dv], writes=[dv])

    def ts(out, a, s1, op, eng="dve"):
        P.eng(eng, lambda e: e.tensor_scalar(out=out, in0=a, scalar1=s1, scalar2=None, op0=op), reads=[dv], writes=[dv])

    def actf(out, a, func, scale=1.0, bias=None):
        b = cb.t[:, 2:3] if bias is None else bias
        P.act(lambda e: e.activation(out=out, in_=a, func=func, bias=b, scale=scale), reads=[dv, cb], writes=[dv])

    def cmul(o_r, o_i, ar, ai, br, bi, t1, t2):
        tt(t1, ar, br, ALU.mult); tt(t2, ai, bi, ALU.mult); tt(o_r, t1, t2, ALU.subtract)
        tt(t1, ar, bi, ALU.mult); tt(t2, ai, br, ALU.mult); tt(o_i, t1, t2, ALU.add)

    def dmap(out, src):
        P.dma("sp", out, src, writes=[dv], key="dvload", allow_slow_non_contiguous=True)

    LR, LI, LDT, DT, X1, X2, MG, CS, SN, AR, AI, NR, NI, TA, TB, RHO, IRHO, U8R, U8I, I8R, I8I, FR, FI, DEN = range(24)
    AP0 = 24
    dmap(S(LR), pd["lam_re"][l].rearrange("(pr m) p -> (m p) pr", m=2))
    dmap(S(LI), pd["lam_im"][l].rearrange("(pr m) p -> (m p) pr", m=2))
    for m in range(2):
        dmap(sm.t[m * 64:(m + 1) * 64, LDT, :],
             pd["log_dt"][l].rearrange("(pr m) -> m pr", m=2)[m].partition_broadcast(64))
        for comp, nm in ((0, "c_re"), (1, "c_im")):
            for pr_ in range(NPR):
                dmap(Ct[comp].t[m * 64:(m + 1) * 64, pr_, :],
                     pd[nm][l].rearrange("(pr m) c p -> m pr p c", m=2)[m][pr_])
    dmap(Bc[0].t[:], pd["b_re"][l].rearrange("(pr m) p c -> (m p) pr c", m=2))
    dmap(Bc[1].t[:], pd["b_im"][l].rearrange("(pr m) p c -> (m p) pr c", m=2))
    for j in range(8):
        dmap(dcol.t[j * 16:(j + 1) * 16, :], pd["d_skip"][l].rearrange("g c -> c g"))
    dmap(wsl.t[:], pd["w_s"][l].rearrange("h t s -> t h s"))
    for r in (0, 32):
        dmap(bsf.t[r:r + 1, :], pd["b_s"][l:l + 1].rearrange("o h t -> o (h t)"))
    actf(S(DT), S(LDT), AF.Exp)
    tt(S(X1), S(LR), S(DT), ALU.mult)
    tt(S(X2), S(LI), S(DT), ALU.mult)
    actf(S(MG), S(X1), AF.Exp, scale=1.0 / 64)
    actf(S(CS), S(X2), AF.Sin, scale=1.0 / 64, bias=cb.t[:, 0:1])
    actf(S(SN), S(X2), AF.Sin, scale=1.0 / 64)
    tt(S(AR), S(MG), S(CS), ALU.mult)
    tt(S(AI), S(MG), S(SN), ALU.mult)
    cur = (AR, AI)
    nxt = (NR, NI)
    for _ in range(6):
        tt(S(TA), S(cur[0]), S(cur[0]), ALU.mult)
        tt(S(TB), S(cur[1]), S(cur[1]), ALU.mult)
        tt(S(nxt[0]), S(TA), S(TB), ALU.subtract)
        tt(S(TA), S(cur[0]), S(cur[1]), ALU.mult)
        tt(S(nxt[1]), S(TA), S(TA), ALU.add)
        cur, nxt = nxt, cur
    A1 = cur

    def apr(m):
        return S(AP0 + 2 * m)

    def api(m):
        return S(AP0 + 2 * m + 1)
    P.dve(lambda e: e.memset(apr(0), 1.0), reads=[dv], writes=[dv])
    P.dve(lambda e: e.memset(api(0), 0.0), reads=[dv], writes=[dv])
    P.dve(lambda e: e.tensor_copy(out=apr(1), in_=S(A1[0])), reads=[dv], writes=[dv])
    P.dve(lambda e: e.tensor_copy(out=api(1), in_=S(A1[1])), reads=[dv], writes=[dv])
    for m in range(2, 9):
        cmul(apr(m), api(m), apr(m - 1), api(m - 1), apr(1), api(1), S(TA), S(TB))
    actf(S(RHO), S(X1), AF.Exp, scale=8.0)
    P.dve(lambda e: e.tensor_copy(out=rho8.t[:], in_=S(RHO)), reads=[dv], writes=[dv, rho8])
    P.dve(lambda e: e.reciprocal(out=S(IRHO), in_=S(RHO)), reads=[dv], writes=[dv])
    tt(S(U8R), apr(8), S(IRHO), ALU.mult)
    tt(S(U8I), api(8), S(IRHO), ALU.mult)
    tt(S(TA), S(IRHO), S(IRHO), ALU.mult)
    tt(S(I8R), apr(8), S(TA), ALU.mult)
    tt(S(I8I), api(8), S(TA), ALU.mult)
    ts(S(I8I), S(I8I), -1.0, ALU.mult)
    ts(S(NR), apr(1), -1.0, ALU.add)
    tt(S(TA), S(LR), S(LR), ALU.mult)
    tt(S(TB), S(LI), S(LI), ALU.mult)
    tt(S(DEN), S(TA), S(TB), ALU.add)
    P.dve(lambda e: e.reciprocal(out=S(DEN), in_=S(DEN)), reads=[dv], writes=[dv])
    tt(S(TA), S(NR), S(LR), ALU.mult)
    tt(S(TB), api(1), S(LI), ALU.mult)
    tt(S(FR), S(TA), S(TB), ALU.add)
    tt(S(FR), S(FR), S(DEN), ALU.mult)
    tt(S(TA), api(1), S(LR), ALU.mult)
    tt(S(TB), S(NR), S(LI), ALU.mult)
    tt(S(FI), S(TA), S(TB), ALU.subtract)
    tt(S(FI), S(FI), S(DEN), ALU.mult)

    def bc(ap2, n):
        return ap2.unsqueeze(2).to_broadcast([128, NPR, n])
    t16 = [T[0].t[:, :, 0:16], T[1].t[:, :, 0:16]]
    cmul(Bb[0].t[:], Bb[1].t[:], bc(S(FR), 16), bc(S(FI), 16), Bc[0].t[:], Bc[1].t[:], t16[0], t16[1])
    WD4 = [w.t[:].rearrange("p a (j c) -> p a j c", j=8) for w in WD]
    for j in range(8):
        cmul(WD4[0][:, :, j, :], WD4[1][:, :, j, :], bc(apr(7 - j), 16), bc(api(7 - j), 16), Bb[0].t[:], Bb[1].t[:],
             t16[0], t16[1])
    cmul(WDt[0].t[:], WDt[1].t[:], bc(S(I8R), 128), bc(S(I8I), 128), WD[0].t[:], WD[1].t[:], T[0].t[:], T[1].t[:])
    for comp in range(2):
        for q4 in range(4):
            bank = pb[3 + (q4 % 2)]
            for q in range(4):
                pr = q4 * 4 + q
                P.pe(lambda e, comp=comp, pr=pr, q=q, bank=bank: e.transpose(
                    out=bank.t[:, q * 128:(q + 1) * 128], in_=WD[comp].t[:, pr, :], identity=ident.t[:]),
                    reads=[dv, ident], writes=[bank])
            P.dve(lambda e, comp=comp, q4=q4, bank=bank: e.tensor_copy(
                out=WDT[comp].t[:, q4 * 4:(q4 + 1) * 4, :], in_=bank.t[:].rearrange("p (a b) -> p a b", a=4)),
                reads=[bank], writes=[WDT[comp]])
    Wo4 = [w.t[:].rearrange("p a (j c) -> p a j c", j=8) for w in Wo]
    for j in range(8):
        ar_, ai_ = bc(apr(j + 1), 16), bc(api(j + 1), 16)
        tt(t16[0], Ct[0].t[:], ar_, ALU.mult); tt(t16[1], Ct[1].t[:], ai_, ALU.mult)
        P.dve(lambda e, j=j: e.tensor_tensor(out=Wo4[0][:, :, j, :], in0=t16[0], in1=t16[1], op=ALU.subtract),
              reads=[dv], writes=[dv, Wo[0]])
        tt(t16[0], Ct[0].t[:], ai_, ALU.mult); tt(t16[1], Ct[1].t[:], ar_, ALU.mult)
        tt(t16[0], t16[0], t16[1], ALU.add)
        P.dve(lambda e, j=j: e.tensor_scalar(out=Wo4[1][:, :, j, :], in0=t16[0], scalar1=-1.0, scalar2=None,
                                             op0=ALU.mult), reads=[dv], writes=[dv, Wo[1]])
    for g in range(NG):
        pr, m = divmod(g, 2)
        bank = pb[5 + (g % 2)]
        sl = slice(m * 64, (m + 1) * 64)
        P.pe(lambda e, pr=pr, sl=sl, bank=bank: e.matmul(bank.t[:, 0:128], lhsT=WDt[0].t[sl, pr, :],
                                                          rhs=Wo[0].t[sl, pr, :], start=True, stop=False),
             reads=[dv, Wo[0]], writes=[bank])
        P.pe(lambda e, pr=pr, sl=sl, bank=bank: e.matmul(bank.t[:, 0:128], lhsT=WDt[1].t[sl, pr, :],
                                                          rhs=Wo[1].t[sl, pr, :], start=False, stop=True),
             reads=[dv, Wo[1]], writes=[bank])
        P.dve(lambda e, bank=bank: e.tensor_tensor(out=tK.t[:], in0=bank.t[:, 0:128], in1=maskJ.t[:], op=ALU.mult),
              reads=[bank, maskJ, dv], writes=[dv])
        P.dve(lambda e, g=g: e.scalar_tensor_tensor(out=Kmat.t[:, g, :], in0=ident.t[:], scalar=dcol.t[:, g:g + 1],
                                                    in1=tK.t[:], op0=ALU.mult, op1=ALU.add),
              reads=[dv, ident], writes=[dv, Kmat])
    P.dve(lambda e: e.tensor_copy(out=C1.t[:, :, 0], in_=S(U8R)), reads=[dv], writes=[dv, C1])
    P.dve(lambda e: e.tensor_copy(out=S1.t[:, :, 0], in_=S(U8I)), reads=[dv], writes=[dv, S1])
    s_ = 1
    while s_ < 64:
        cr = C1.t[:, :, s_ - 1:s_].to_broadcast([128, NPR, s_])
        ci = S1.t[:, :, s_ - 1:s_].to_broadcast([128, NPR, s_])
        t1, t2 = T[0].t[:, :, 0:s_], T[1].t[:, :, 0:s_]
        cmul(C1.t[:, :, s_:2 * s_], S1.t[:, :, s_:2 * s_], cr, ci, C1.t[:, :, 0:s_], S1.t[:, :, 0:s_], t1, t2)
        s_ *= 2
    P.dve(lambda e: e.tensor_copy(out=Mtab.t[:], in_=bc(rho8.t[:], 64)), reads=[dv, rho8], writes=[dv, Mtab])
    P.dve(lambda e: e.memset(Mtab.t[:, :, 0], 0.0), reads=[dv], writes=[dv, Mtab])
    for comp in range(2):
        P.dve(lambda e, comp=comp: e.memset(Lf[comp].t[:], 0.0), reads=[dv], writes=[dv, Lf[comp]])
    for h in range(8):
        bank = pb[3 + (h % 2)]
        P.pe(lambda e, h=h, bank=bank: e.transpose(out=bank.t[:, 0:128], in_=wsl.t[:, h, :], identity=ident.t[:]),
             reads=[dv, ident], writes=[bank])
        P.dve(lambda e, h=h, bank=bank: e.tensor_tensor(out=WsT.t[:, h, :], in0=bank.t[:, 0:128], in1=tril.t[:],
                                                        op=ALU.mult), reads=[bank, tril], writes=[WsT])
    P.dve(lambda e: e.memset(bs2.t[:], 0.0), reads=[dv], writes=[dv, bs2])
    P.dve(lambda e: e.tensor_copy(out=bs2.t[0:1, :], in_=bsf.t[0:1, :]), reads=[dv], writes=[dv, bs2])
    P.dve(lambda e: e.tensor_copy(out=bs2.t[32:33, :], in_=bsf.t[32:33, :]), reads=[dv], writes=[dv, bs2])
    P.dve(lambda e: e.tensor_tensor(out=bsf.t[32:33, :], in0=bsf.t[32:33, :], in1=bs2.t[32:33, :], op=ALU.subtract),
          reads=[dv, bs2], writes=[dv])
    P.dve(lambda e: e.tensor_copy(out=bs2.t[32:33, :], in_=bsf.t[32:33, :]), reads=[dv], writes=[dv, bs2])
    P.barrier()
    A.off = off_tmp
    xt = A.alloc("xt", [128, KT, NT], F32)
    off_ht = A.off
    ht = A.alloc("ht", [128, KT, NT], BF16)
    ycat = P.carve("ycat", A.buf, off_ht, [128, KT, NT], BF16)
    P.alias(ht, ycat)
    rstd = A.alloc("rstd", [128, NT], F32)
    off_sq = A.off
    sq = A.alloc("sq", [128, KT, NT], BF16)
    vtok = P.carve("vtok", A.buf, off_sq, [128, 4, NT], BF16)
    ub = P.carve("ub", A.buf, off_sq + 4096, [128, 4, NT], BF16)
    P.alias(sq, vtok); P.alias(sq, ub)
    off_u = A.off
    Ublk = A.alloc("Ublk", [128, NG, 8, 16], BF16)
    Tt = [P.carve("Tt1", A.buf, off_u, [128, NPR, 64], F32), P.carve("Tt2", A.buf, off_u + 4096, [128, NPR, 64], F32)]
    P.alias(Ublk, Tt[0]); P.alias(Ublk, Tt[1])
    UblkT = A.alloc("UblkT", [128, NG, 64], BF16)
    Z = [A.alloc("Zr", [128, NPR, 64], F32), A.alloc("Zi", [128, NPR, 64], F32)]
    W = [A.alloc("Wr", [128, NPR, 64], F32), A.alloc("Wi", [128, NPR, 64], F32)]
    Lb = [A.alloc("Lbr", [128, NPR, 64], BF16), A.alloc("Lbi", [128, NPR, 64], BF16)]
    off_y = A.off
    Yblk = A.alloc("Yblk", [128, 8, NT], F32)
    ybuf = P.carve("ybuf", A.buf, off_y, [128, KT, NT], F32)
    P.alias(Yblk, ybuf)
    ytmp = A.alloc("ytmp", [128, NT], F32)
    sg = A.alloc("sg", [128, 4, NT], BF16)
    vn = A.alloc("vn", [128, NT], BF16)
    vsq = A.alloc("vsq", [128, NT], BF16)
    vrs = A.alloc("vrs", [128, NT], F32)
    t16s = A.alloc("t16s", [128, 2, NPR], F32)
    Dre = pst.t[:, 3 * 512:5 * 512].rearrange("p (a n) -> p a n", a=NPR)
    Dim = pst.t[:, 5 * 512:7 * 512].rearrange("p (a n) -> p a n", a=NPR)
    Dv = [Dre, Dim]
    Dbank = [[pb[3], pb[4]], [pb[5], pb[6]]]
    pbT = {b: pb[b].t.bitcast(BF16) for b in (5, 6, 7)}
    gpost = "g_mix_post"

    for ti in range(c["NTILES"]):
        c["load_x"](xt, ti)
        c["prenorm"](xt, ht, sq, rstd, "g_mix_pre", l)
        for j in range(8):
            bank = pb[3 + (j % 2)]
            for k in range(KT):
                P.pe(lambda e, j=j, k=k, bank=bank: e.matmul(bank.t[0:64, :], lhsT=ht.t[:, k, j:NT:8],
                                                             rhs=w_in.t[:, k, 0:512], start=(k == 0),
                                                             stop=(k == KT - 1)), reads=[ht, w_in], writes=[bank])
            src = bank.t[0:64, :].rearrange("p (g c) -> p g c", g=NG)
            if j % 2 == 0:
                P.act(lambda e, j=j, src=src: e.activation(out=Ublk.t[0:64, :, j, :], in_=src, func=AF.Copy),
                      reads=[bank], writes=[Ublk])
            else:
                P.dve(lambda e, j=j, src=src: e.tensor_copy(out=Ublk.t[0:64, :, j, :], in_=src),
                      reads=[bank], writes=[Ublk])
        for g8 in range(4):
            bno = 5 + (g8 % 2)
            for q in range(8):
                g = g8 * 8 + q
                P.pe(lambda e, g=g, q=q, bno=bno: e.transpose(
                    out=pbT[bno][:, q * 64:(q + 1) * 64], in_=Ublk.t[0:64, g, :, :].rearrange("p j c -> p (j c)"),
                    identity=identb.t[0:64, 0:64]), reads=[Ublk, identb], writes=[pb[bno]])
            P.act(lambda e, g8=g8, bno=bno: e.activation(
                out=UblkT.t[:, g8 * 8:(g8 + 1) * 8, :], in_=pbT[bno][:, 0:512].rearrange("p (a n) -> p a n", a=8),
                func=AF.Copy), reads=[pb[bno]], writes=[UblkT])
        if ti > 0:
            for comp in range(2):
                P.act(lambda e, comp=comp: e.activation(out=Lf[comp].t[:, :, 0], in_=Lf[comp].t[:, :, 64], func=AF.Copy),
                      reads=[Lf[comp]], writes=[Lf[comp]])
        for g in range(NG):
            pr, m = divmod(g, 2)
            for comp in range(2):
                bank = Dbank[comp][pr // 8]
                P.pe(lambda e, g=g, pr=pr, m=m, comp=comp: e.matmul(
                    Dv[comp][m * 64:(m + 1) * 64, pr, :], lhsT=WDT[comp].t[:, pr, m * 64:(m + 1) * 64],
                    rhs=UblkT.t[:, g, :], start=True, stop=True), reads=[UblkT, WDT[comp]], writes=[bank])
        for ct in range(4):
            for grp in range(3):
                col = 512 * (grp + 1) + ct * 128
                bank = pb[1 + ((ct * 3 + grp) % 2)]
                for k in range(KT):
                    P.pe(lambda e, col=col, k=k, bank=bank: e.matmul(bank.t[:], lhsT=w_in.t[:, k, col:col + 128],
                                                                     rhs=ht.t[:, k, :], start=(k == 0),
                                                                     stop=(k == KT - 1)), reads=[ht, w_in], writes=[bank])
                if grp == 0:
                    P.act(lambda e, ct=ct, bank=bank: e.activation(out=sg.t[:, ct, :], in_=bank.t[:], func=AF.Sigmoid),
                          reads=[bank], writes=[sg])
                elif grp == 1:
                    P.act(lambda e, ct=ct, bank=bank: e.activation(out=ub.t[:, ct, :], in_=bank.t[:], func=AF.Copy),
                          reads=[bank], writes=[ub])
                else:
                    P.act(lambda e, bank=bank: e.activation(out=vsq.t[:], in_=bank.t[:], func=AF.Square),
                          reads=[bank], writes=[vsq])
                    P.pe(lambda e: e.matmul(pb[0].t[:], lhsT=blkones.t[:], rhs=vsq.t[:], start=True, stop=True),
                         reads=[vsq, blkones], writes=[pb[0]])
                    c["rstd_from_ss"](pb[0].t[:], vrs, 64, [pb[0]])
                    P.dve(lambda e, ct=ct, bank=bank: e.scalar_tensor_tensor(
                        out=vn.t[:], in0=bank.t[:], scalar=gvv.t[:, l, ct:ct + 1], in1=vrs.t[:], op0=ALU.mult,
                        op1=ALU.mult), reads=[bank, gvv, vrs], writes=[vn])
                    for cc in range(4):
                        P.pe(lambda e, cc=cc: e.transpose(out=pbT[7][:, cc * 128:(cc + 1) * 128],
                                                          in_=vn.t[:, cc * 128:(cc + 1) * 128], identity=identb.t[:]),
                             reads=[vn, identb], writes=[pb[7]])
                    P.dve(lambda e, ct=ct: e.tensor_copy(
                        out=vtok.t[:, :, ct * 128:(ct + 1) * 128],
                        in_=pbT[7][:, 0:512].rearrange("p (a n) -> p a n", a=4)), reads=[pb[7]], writes=[vtok])
        dr = [pb[3], pb[4], pb[5], pb[6]]

        def dtt(out, a, b, op, rd, wr):
            P.dve(lambda e: e.tensor_tensor(out=out, in0=a, in1=b, op=op), reads=rd, writes=wr)
        dtt(Tt[0].t[:], Dre, C1.t[:], ALU.mult, dr + [C1], [Tt[0]])
        dtt(Tt[1].t[:], Dim, S1.t[:], ALU.mult, dr + [S1], [Tt[1]])
        dtt(Z[0].t[:], Tt[0].t[:], Tt[1].t[:], ALU.add, Tt, [Z[0]])
        dtt(Tt[0].t[:], Dim, C1.t[:], ALU.mult, dr + [C1], [Tt[0]])
        dtt(Tt[1].t[:], Dre, S1.t[:], ALU.mult, dr + [S1], [Tt[1]])
        dtt(Z[1].t[:], Tt[0].t[:], Tt[1].t[:], ALU.subtract, Tt, [Z[1]])
        for comp in range(2):
            dtt(t16s.t[:, comp, :], Lf[comp].t[:, :, 0], rho8.t[:], ALU.mult, [Lf[comp], rho8], [t16s])
            dtt(Z[comp].t[:, :, 0], Z[comp].t[:, :, 0], t16s.t[:, comp, :], ALU.add, [Z[comp], t16s], [Z[comp]])
            P.dve(lambda e, comp=comp: e.tensor_tensor_scan(
                out=W[comp].t[:].rearrange("p a n -> p (a n)"), data0=Mtab.t[:].rearrange("p a n -> p (a n)"),
                data1=Z[comp].t[:].rearrange("p a n -> p (a n)"), initial=0.0, op0=ALU.mult, op1=ALU.add),
                reads=[Mtab, Z[comp]], writes=[W[comp]])
        dtt(Tt[0].t[:], C1.t[:], W[0].t[:], ALU.mult, [C1, W[0]], [Tt[0]])
        dtt(Tt[1].t[:], S1.t[:], W[1].t[:], ALU.mult, [S1, W[1]], [Tt[1]])
        dtt(Lf[0].t[:, :, 1:65], Tt[0].t[:], Tt[1].t[:], ALU.subtract, Tt, [Lf[0]])
        dtt(Tt[0].t[:], C1.t[:], W[1].t[:], ALU.mult, [C1, W[1]], [Tt[0]])
        dtt(Tt[1].t[:], S1.t[:], W[0].t[:], ALU.mult, [S1, W[0]], [Tt[1]])
        dtt(Lf[1].t[:, :, 1:65], Tt[0].t[:], Tt[1].t[:], ALU.add, Tt, [Lf[1]])
        for comp in range(2):
            P.act(lambda e, comp=comp: e.activation(out=Lb[comp].t[:], in_=Lf[comp].t[:, :, 0:64], func=AF.Copy),
                  reads=[Lf[comp]], writes=[Lb[comp]])
        for g4 in range(8):
            bank = pb[1 + (g4 % 2)]
            for q in range(4):
                g = g4 * 4 + q
                pr, m = divmod(g, 2)
                sl = slice(m * 64, (m + 1) * 64)
                osl = bank.t[0:64, q * 128:(q + 1) * 128]
                P.pe(lambda e, g=g, osl=osl: e.matmul(osl, lhsT=UblkT.t[:, g, :], rhs=Kmat.t[:, g, :], start=True,
                                                      stop=False), reads=[UblkT, Kmat], writes=[bank])
                P.pe(lambda e, pr=pr, sl=sl, osl=osl: e.matmul(osl, lhsT=Lb[0].t[sl, pr, :], rhs=Wo[0].t[sl, pr, :],
                                                               start=False, stop=False), reads=[Lb[0], Wo[0]],
                     writes=[bank])
                P.pe(lambda e, pr=pr, sl=sl, osl=osl: e.matmul(osl, lhsT=Lb[1].t[sl, pr, :], rhs=Wo[1].t[sl, pr, :],
                                                               start=False, stop=True), reads=[Lb[1], Wo[1]],
                     writes=[bank])
            P.dve(lambda e, g4=g4, bank=bank: e.tensor_copy(
                out=Yblk.t[0:64, :, g4 * 64:(g4 + 1) * 64].rearrange("p j (g c) -> p j g c", g=4),
                in_=bank.t[0:64, :].rearrange("p (g j c) -> p j g c", g=4, j=8)), reads=[bank], writes=[Yblk])
        for ct in range(4):
            bank = pb[3 + (ct % 2)]
            for j in range(8):
                P.pe(lambda e, ct=ct, j=j, bank=bank: e.transpose(
                    out=bank.t[:, j * 64:(j + 1) * 64], in_=Yblk.t[0:64, j, ct * 128:(ct + 1) * 128],
                    identity=ident.t[0:64, 0:64]), reads=[Yblk, ident], writes=[bank])
            P.act(lambda e, bank=bank: e.activation(out=ytmp.t[:].rearrange("p (n j) -> p n j", j=8),
                                                    in_=bank.t[:].rearrange("p (j n) -> p n j", j=8),
                                                    func=AF.Gelu_apprx_tanh), reads=[bank], writes=[ytmp])
            P.pool(lambda e, ct=ct: e.tensor_tensor(out=ycat.t[:, ct, :], in0=ytmp.t[:], in1=sg.t[:, ct, :],
                                                    op=ALU.mult), reads=[ytmp, sg], writes=[ycat])
        for ct in range(4):
            bank = pb[1 + (ct % 2)]
            for cc in range(4):
                for hh in range(2):
                    h = 2 * ct + hh
                    osl = bank.t[hh * 64:(hh + 1) * 64, cc * 128:(cc + 1) * 128]
                    P.pe(lambda e, cc=cc, h=h, osl=osl: e.matmul(osl, lhsT=vtok.t[:, cc, h * 64:(h + 1) * 64],
                                                                 rhs=WsT.t[:, h, :], start=True, stop=False),
                         reads=[vtok, WsT], writes=[bank])
                    P.pe(lambda e, h=h, osl=osl: e.matmul(osl, lhsT=onesb.t[0:64, 0:64],
                                                          rhs=bs2.t[0:64, h * 128:(h + 1) * 128], start=False,
                                                          stop=True), reads=[bs2, onesb], writes=[bank])
            P.dve(lambda e, ct=ct, bank=bank: e.tensor_tensor(out=ycat.t[:, 4 + ct, :], in0=bank.t[:],
                                                              in1=ub.t[:, ct, :], op=ALU.mult),
                  reads=[bank, ub], writes=[ycat])

        def mm(o, bank):
            for k in range(KT):
                P.pe(lambda e, o=o, k=k, bank=bank: e.matmul(bank.t[:], lhsT=w_out.t[:, k, o * 128:(o + 1) * 128],
                                                             rhs=ycat.t[:, k, :], start=(k == 0), stop=(k == KT - 1)),
                     reads=[w_out, ycat], writes=[bank])
        c["postnorm_res"](xt, ybuf, sq, rstd, gpost, l, mm)
        c["store_x"](xt, ti)
```

```python
import contextlib
import numpy as np
import concourse.bass as bass
import concourse.mybir as mybir
from concourse.bass_utils import run_bass_kernel_spmd

F32 = mybir.dt.float32
BF16 = mybir.dt.bfloat16
AF = mybir.ActivationFunctionType
ALU = mybir.AluOpType

ENGS = ("pe", "dve", "act", "pool", "sp")
SAME_ENGINE_SYNC = True


class Res:
    def __init__(self, name):
        self.name = name
        self.last_w = None
        self.readers = {}
        self._subs = {}

    def sub(self, k):
        r = self._subs.get(k)
        if r is None:
            r = Res(f"{self.name}.{k}")
            self._subs[k] = r
        return r


class Buf(Res):
    def __init__(self, name, t):
        super().__init__(name)
        self.t = t


class Prog:
    def __init__(self, nc):
        self.nc = nc
        self.stack = contextlib.ExitStack()
        self.q = {e: [] for e in ENGS}
        self.ecnt = {e: 0 for e in ENGS}
        self.seen = {e: {} for e in ENGS}
        self.dmacnt = {}
        self.nres = 0

    def sbuf(self, name, shape, dtype):
        t = self.stack.enter_context(self.nc.sbuf_tensor(name, list(shape), dtype))
        return Buf(name, t)

    def psum(self, name, shape, dtype):
        t = self.stack.enter_context(self.nc.psum_tensor(name, list(shape), dtype))
        return Buf(name, t)

    def res(self, name):
        return Res(name)

    def _record(self, eng, fn, reads, writes, dmakey=None, inc=1):
        reads = list(reads)
        writes = list(writes)
        for r in list(reads):
            reads.extend(getattr(r, "also", ()))
        for w in list(writes):
            writes.extend(getattr(w, "also", ()))
        waits = {}
        seen = self.seen[eng]

        def need(ev):
            if ev is None:
                return
            key, val, src = ev
            if src == eng and (eng == "pe" or not SAME_ENGINE_SYNC):
                return
            if seen.get(key, 0) >= val:
                return
            if waits.get(key, 0) < val:
                waits[key] = val

        for r in reads:
            need(r.last_w)
        for w in writes:
            lw = w.last_w
            if not (dmakey is not None and lw is not None and lw[0] == ("dma", dmakey)):
                need(lw)
            for k, (v, s) in w.readers.items():
                need((k, v, s))
        for k, v in waits.items():
            seen[k] = v
        if dmakey is None:
            self.ecnt[eng] += 1
            ev = (("eng", eng), self.ecnt[eng], eng)
        else:
            self.dmacnt[dmakey] = self.dmacnt.get(dmakey, 0) + inc
            ev = (("dma", dmakey), self.dmacnt[dmakey], None)
        self.q[eng].append((fn, list(waits.items()), ev, inc if dmakey is not None else 1))
        for r in reads:
            old = r.readers.get(ev[0])
            if old is None or old[0] < ev[1]:
                r.readers[ev[0]] = (ev[1], ev[2])
        for w in writes:
            w.last_w = ev
            w.readers = {}
        return ev

    def pe(self, fn, reads=(), writes=()):
        return self._record("pe", fn, reads, writes)

    def dve(self, fn, reads=(), writes=()):
        return self._record("dve", fn, reads, writes)

    def act(self, fn, reads=(), writes=()):
        return self._record("act", fn, reads, writes)

    def pool(self, fn, reads=(), writes=()):
        return self._record("pool", fn, reads, writes)

    def eng(self, e, fn, reads=(), writes=()):
        return self._record(e, fn, reads, writes)

    def dma(self, eng, out, in_, reads=(), writes=(), key=None, **kw):
        key = key or writes[0].name
        return self._record(eng, lambda e: e.dma_start(out=out, in_=in_, **kw), reads, writes, dmakey=key, inc=16)

    def collective(self, fn, reads=(), writes=(), key=None):
        key = key or writes[0].name
        return self._record("pool", fn, reads, writes, dmakey=key, inc=16)

    def finish(self, finals):
        nc = self.nc
        waits = {}
        for r in finals:
            key, val, _ = r.last_w
            waits[key] = max(waits.get(key, 0), val)
        self.q["sp"].append((None, list(waits.items()), None, 0))
        keys = [("eng", e) for e in ENGS if self.ecnt[e] > 0] + [("dma", k) for k in self.dmacnt]
        sems = {}
        for i, k in enumerate(keys):
            sems[k] = self.stack.enter_context(nc.semaphore(f"s{i}_{k[1]}"[:24].replace(".", "_")))
        self.nsems = len(keys)
        q = self.q

        def replay(ename, e):
            for fn, ws, ev, inc in q[ename]:
                for k, v in ws:
                    e.wait_ge(sems[k], v)
                if fn is None:
                    continue
                ins = fn(e)
                ins.then_inc(sems[ev[0]], inc)

        with nc.Block() as block:
            @block.tensor
            def _(e):
                replay("pe", e)

            @block.vector
            def _(e):
                replay("dve", e)

            @block.scalar
            def _(e):
                replay("act", e)

            @block.gpsimd
            def _(e):
                replay("pool", e)

            @block.sync
            def _(e):
                replay("sp", e)
        self.stack.close()

    def barrier(self):
        for e in ENGS:
            waits = {}
            for e2 in ENGS:
                if e2 != e and self.ecnt[e2] > self.seen[e].get(("eng", e2), 0):
                    waits[("eng", e2)] = self.ecnt[e2]
            if e != "pe" and self.ecnt[e] > self.seen[e].get(("eng", e), 0):
                waits[("eng", e)] = self.ecnt[e]
            for k, v in self.dmacnt.items():
                if v > self.seen[e].get(("dma", k), 0):
                    waits[("dma", k)] = v
            for k, v in waits.items():
                self.seen[e][k] = v
            if waits:
                self.q[e].append((None, list(waits.items()), None, 0))

    @staticmethod
    def alias(a, b):
        a.also = list(getattr(a, "also", [])) + [b]
        b.also = list(getattr(b, "also", [])) + [a]

    def carve(self, name, arena, off_bytes, shape, dtype):
        esz = 4 if dtype == F32 else 2
        n = 1
        for s in shape[1:]:
            n *= s
        nb = n * esz
        assert off_bytes % 4 == 0 and nb % 4 == 0
        a = arena.t[:, off_bytes // 4:(off_bytes + nb) // 4]
        if dtype != F32:
            a = a.bitcast(dtype)
        if len(shape) == 3:
            a = a.rearrange("p (a b) -> p a b", a=shape[1])
        elif len(shape) == 4:
            a = a.rearrange("p (a b c) -> p a b c", a=shape[1], b=shape[2])
        b = Buf(name, a)
        b.nbytes = nb
        return b


D = 1024
KT = 8
NT = 512
PI = 3.14159265358979
PNAMES = ["g_mix_pre", "w_in", "lam_re", "lam_im", "log_dt", "b_re", "b_im", "c_re", "c_im", "d_skip", "g_v",
          "w_s", "b_s", "w_out", "g_mix_post", "g_x_pre", "g_mem", "w_q", "w_k", "w_v", "w_o", "g_x_post",
          "g_ffn_pre", "w_up", "conv_w", "conv_b", "w_down", "g_ffn_post"]
PSHAPES = {"g_mix_pre": [4, 1024], "w_in": [4, 1024, 2048], "lam_re": [4, 32, 64], "lam_im": [4, 32, 64],
           "log_dt": [4, 32], "b_re": [4, 32, 64, 16], "b_im": [4, 32, 64, 16], "c_re": [4, 32, 16, 64],
           "c_im": [4, 32, 16, 64], "d_skip": [4, 32, 16], "g_v": [4, 512], "w_s": [4, 8, 128, 128],
           "b_s": [4, 8, 128], "w_out": [4, 1024, 1024], "g_mix_post": [4, 1024], "g_x_pre": [4, 1024],
           "g_mem": [4, 1024], "w_q": [4, 1024, 1024], "w_k": [4, 1024, 1024], "w_v": [4, 1024, 1024],
           "w_o": [4, 1024, 1024], "g_x_post": [4, 1024], "g_ffn_pre": [4, 1024], "w_up": [4, 1024, 5632],
           "conv_w": [4, 3, 5632], "conv_b": [4, 5632], "w_down": [4, 2816, 1024], "g_ffn_post": [4, 1024]}


class Arena:
    def __init__(self, P, buf, total):
        self.P, self.buf, self.total, self.off = P, buf, total, 0

    def alloc(self, name, shape, dtype):
        b = self.P.carve(name, self.buf, self.off, shape, dtype)
        self.off += (b.nbytes + 31) // 32 * 32
        assert self.off <= self.total, (name, self.off, self.total)
        return b

    def reset(self):
        self.P.barrier()
        self.off = 0


def build_program(S=8192, DEPTH=4, STAGES=("mix", "xat", "ffn")):
    nc = bass.Bass("TRN2", target_bir_lowering=False)
    NTILES = S // NT
    x_d = nc.dram_tensor("x", [S, D], F32, kind="ExternalInput").ap()
    mem_d = nc.dram_tensor("mem", [256, D], F32, kind="ExternalInput").ap()
    pd = {n: nc.dram_tensor(n, PSHAPES[n], F32, kind="ExternalInput").ap() for n in PNAMES}
    out_d = nc.dram_tensor("out", [S, D], F32, kind="ExternalOutput").ap()
    xs_d = nc.dram_tensor("xs", [D, S], F32).ap()
    xs_v = xs_d.rearrange("(k p) t -> p k t", p=128)
    P = Prog(nc)
    xs_r = [P.res(f"xs{t}") for t in range(NTILES)]
    out_r = P.res("out")

    ident = P.sbuf("ident", [128, 128], F32)
    identb = P.sbuf("identb", [128, 128], BF16)
    onesb = P.sbuf("onesb", [128, 128], BF16)
    blkones = P.sbuf("blkones", [128, 128], BF16)
    maskJ = P.sbuf("maskJ", [128, 128], F32)
    tril = P.sbuf("tril", [128, 128], F32)
    cb = P.sbuf("cbias", [128, 4], F32)
    gv = {n: P.sbuf("gv_" + n, [128, 4, 8], F32) for n in
          ["g_mix_pre", "g_mix_post", "g_x_pre", "g_mem", "g_x_post", "g_ffn_pre", "g_ffn_post"]}
    gvv = P.sbuf("gv_g_v", [128, 4, 4], F32)
    memhT = P.sbuf("memhT", [128, 8, 256], F32)
    ARENA_BYTES = 196 * 1024
    arena_buf = P.sbuf("arena", [128, ARENA_BYTES // 4], F32)
    A = Arena(P, arena_buf, ARENA_BYTES)
    pst = P.psum("pst", [128, 8 * 512], F32)
    pb = []
    for b in range(8):
        r = Buf(f"pb{b}", pst.t[:, b * 512:(b + 1) * 512])
        pb.append(r)

    def cst(e, ap, v, wr):
        P.eng(e, lambda en: en.memset(ap, v), writes=wr)

    cst("pool", ident.t[:], 1.0, [ident])
    P.pool(lambda e: e.affine_select(out=ident.t[:], in_=ident.t[:], pattern=[[1, 128]], compare_op=ALU.is_equal,
                                     fill=0.0, base=0, channel_multiplier=-1), reads=[ident], writes=[ident])
    P.dve(lambda e: e.tensor_copy(out=identb.t[:], in_=ident.t[:]), reads=[ident], writes=[identb])
    cst("pool", onesb.t[:], 1.0, [onesb])
    cst("pool", blkones.t[:], 0.0, [blkones])
    cst("pool", blkones.t[0:64, 0:64], 1.0, [blkones])
    cst("pool", blkones.t[64:128, 64:128], 1.0, [blkones])
    cst("pool", maskJ.t[:], 1.0, [maskJ])
    P.pool(lambda e: e.affine_select(out=maskJ.t[:].rearrange("p (j c) -> p j c", j=8),
                                     in_=maskJ.t[:].rearrange("p (j c) -> p j c", j=8),
                                     pattern=[[16, 8], [0, 16]], compare_op=ALU.is_ge, fill=0.0, base=15,
                                     channel_multiplier=-1), reads=[maskJ], writes=[maskJ])
    cst("pool", tril.t[:], 1.0, [tril])
    P.pool(lambda e: e.affine_select(out=tril.t[:], in_=tril.t[:], pattern=[[1, 128]], compare_op=ALU.is_ge,
                                     fill=0.0, base=0, channel_multiplier=-1), reads=[tril], writes=[tril])
    cst("pool", cb.t[:, 0:1], PI / 2, [cb])
    cst("pool", cb.t[:, 1:2], 1e-6, [cb])
    cst("pool", cb.t[:, 2:3], 0.0, [cb])
    with nc.allow_non_contiguous_dma(reason="small param loads"):
        pass
    import os
    KDBG = os.environ.get("KDBG", "")
    for n, t in (gv.items() if "nogv" not in KDBG else []):
        P.dma("sp", t.t[:], pd[n].rearrange("l (k p) -> p l k", p=128), writes=[t], allow_slow_non_contiguous=True)
    if "nogv" not in KDBG:
      P.dma("sp", gvv.t[:], pd["g_v"].rearrange("l (k p) -> p l k", p=128), writes=[gvv], allow_slow_non_contiguous=True)

    def rstd_from_ss(ss_ps, rstd, n_feat, rd):
        P.act(lambda e: e.activation(out=rstd.t[:], in_=ss_ps, func=AF.Sqrt, bias=cb.t[:, 1:2], scale=1.0 / n_feat),
              reads=rd + [cb], writes=[rstd])
        P.dve(lambda e: e.reciprocal(out=rstd.t[:], in_=rstd.t[:]), reads=[rstd], writes=[rstd])

    def load_x(xt, ti):
        P.dma("sp", xt.t[:], xs_v[:, :, ti * NT:(ti + 1) * NT], reads=[xs_r[ti]], writes=[xt])

    def store_x(xt, ti):
        P.dma("sp", xs_v[:, :, ti * NT:(ti + 1) * NT], xt.t[:], reads=[xt], writes=[xs_r[ti]])

    def prenorm(xt, ht, sq, rstd, gname, l):
        P.act(lambda e: e.activation(out=sq.t[:], in_=xt.t[:], func=AF.Square), reads=[xt], writes=[sq])
        for k in range(KT):
            P.pe(lambda e, k=k: e.matmul(pb[0].t[:], lhsT=onesb.t[:], rhs=sq.t[:, k, :], start=(k == 0),
                                         stop=(k == KT - 1)), reads=[sq, onesb], writes=[pb[0]])
        rstd_from_ss(pb[0].t[:], rstd, D, [pb[0]])
        g = gv[gname]
        for k in range(KT):
            P.dve(lambda e, k=k: e.scalar_tensor_tensor(out=ht.t[:, k, :], in0=xt.t[:, k, :], scalar=g.t[:, l, k:k + 1],
                                                        in1=rstd.t[:], op0=ALU.mult, op1=ALU.mult),
                  reads=[xt, g, rstd], writes=[ht])

    def postnorm_res(xt, ybuf, sq, rstd, gname, l, mm_fn):
        for o in range(KT):
            bank = pb[1 + (o % 2)]
            mm_fn(o, bank)
            P.act(lambda e, o=o, bank=bank: e.activation(out=ybuf.t[:, o, :], in_=bank.t[:], func=AF.Copy),
                  reads=[bank], writes=[ybuf])
            P.dve(lambda e, o=o, bank=bank: e.tensor_tensor(out=sq.t[:, o, :], in0=bank.t[:], in1=ybuf.t[:, o, :],
                                                            op=ALU.mult), reads=[bank, ybuf], writes=[sq])
        for k in range(KT):
            P.pe(lambda e, k=k: e.matmul(pb[0].t[:], lhsT=onesb.t[:], rhs=sq.t[:, k, :], start=(k == 0),
                                         stop=(k == KT - 1)), reads=[sq, onesb], writes=[pb[0]])
        rstd_from_ss(pb[0].t[:], rstd, D, [pb[0]])
        g = gv[gname]
        for o in range(KT):
            P.dve(lambda e, o=o: e.scalar_tensor_tensor(out=ybuf.t[:, o, :], in0=ybuf.t[:, o, :],
                                                        scalar=g.t[:, l, o:o + 1], in1=rstd.t[:], op0=ALU.mult,
                                                        op1=ALU.mult), reads=[ybuf, g, rstd], writes=[ybuf])
            P.pool(lambda e, o=o: e.tensor_tensor(out=xt.t[:, o, :], in0=xt.t[:, o, :], in1=ybuf.t[:, o, :],
                                                  op=ALU.add), reads=[xt, ybuf], writes=[xt])

    def load_w(dst, src_ap, ncols, eng="pool"):
        c0 = 0
        while c0 < ncols:
            c1 = min(ncols, c0 + 2048)
            P.dma(eng, dst.t[:, :, c0:c1], src_ap[:, :, c0:c1], writes=[dst])
            c0 = c1

    xtok = A.alloc("xtok", [128, 4, D], F32)
    xt0 = A.alloc("xt0", [128, KT, NT], F32)
    for ti in range(NTILES):
        P.dma("sp", xtok.t[:], x_d[ti * NT:(ti + 1) * NT, :].rearrange("(s p) d -> p s d", p=128), writes=[xtok])
        for k in range(KT):
            bank = pb[1 + (k % 4)]
            for s in range(4):
                P.pe(lambda e, k=k, s=s, bank=bank: e.transpose(out=bank.t[:, s * 128:(s + 1) * 128],
                                                                in_=xtok.t[:, s, k * 128:(k + 1) * 128],
                                                                identity=ident.t[:]),
                     reads=[xtok, ident], writes=[bank])
            if k % 2 == 0:
                P.act(lambda e, k=k, bank=bank: e.activation(out=xt0.t[:, k, :], in_=bank.t[:], func=AF.Copy),
                      reads=[bank], writes=[xt0])
            else:
                P.dve(lambda e, k=k, bank=bank: e.tensor_copy(out=xt0.t[:, k, :], in_=bank.t[:]),
                      reads=[bank], writes=[xt0])
        store_x(xt0, ti)
    if "nomem" in KDBG:
        return_early_mem = True
    memt = A.alloc("memt", [128, 2, D], F32)
    msq = A.alloc("msq", [128, D], F32)
    mss = A.alloc("mss", [128, 2], F32)
    P.dma("sp", memt.t[:], mem_d.rearrange("(s p) d -> p s d", p=128), writes=[memt])
    for s in range(2):
        P.act(lambda e, s=s: e.activation(out=msq.t[:], in_=memt.t[:, s, :], func=AF.Square,
                                          accum_out=mss.t[:, s:s + 1]), reads=[memt], writes=[msq, mss])
    P.act(lambda e: e.activation(out=mss.t[:], in_=mss.t[:], func=AF.Sqrt, bias=cb.t[:, 1:2], scale=1.0 / D),
          reads=[mss, cb], writes=[mss])
    P.dve(lambda e: e.reciprocal(out=mss.t[:], in_=mss.t[:]), reads=[mss], writes=[mss])
    for s in range(2):
        P.dve(lambda e, s=s: e.tensor_scalar(out=memt.t[:, s, :], in0=memt.t[:, s, :], scalar1=mss.t[:, s:s + 1],
                                             scalar2=None, op0=ALU.mult), reads=[memt, mss], writes=[memt])
    for k in range(KT):
        bank = pb[1 + (k % 4)]
        for s in range(2):
            P.pe(lambda e, k=k, s=s, bank=bank: e.transpose(out=bank.t[:, s * 128:(s + 1) * 128],
                                                            in_=memt.t[:, s, k * 128:(k + 1) * 128],
                                                            identity=ident.t[:]), reads=[memt, ident], writes=[bank])
        P.act(lambda e, k=k, bank=bank: e.activation(out=memhT.t[:, k, :], in_=bank.t[:, 0:256], func=AF.Copy),
              reads=[bank], writes=[memhT])

    ctx = dict(nc=nc, P=P, A=A, pb=pb, pd=pd, gv=gv, gvv=gvv, cb=cb, ident=ident, identb=identb, onesb=onesb,
               blkones=blkones, maskJ=maskJ, tril=tril, memhT=memhT, load_x=load_x, store_x=store_x,
               prenorm=prenorm, postnorm_res=postnorm_res, load_w=load_w, rstd_from_ss=rstd_from_ss,
               NTILES=NTILES, xs_r=xs_r, xs_v=xs_v, pst=pst)
    for l in range(DEPTH):
        if "mix" in STAGES:
            A.reset()
            mixer_layer(ctx, l)
        if "xat" in STAGES:
            A.reset()
            xattn_layer(ctx, l)
        if "ffn" in STAGES:
            A.reset()
            ffn_layer(ctx, l)

    A.reset()
    xt0e = A.alloc("xt0e", [128, KT, NT], F32)
    xtoke = A.alloc("xtoke", [128, 4, D], F32)
    for ti in range(NTILES):
        load_x(xt0e, ti)
        for s in range(4):
            for kh in range(2):
                bank = pb[1 + ((2 * s + kh) % 4)]
                for kk in range(4):
                    k = kh * 4 + kk
                    P.pe(lambda e, k=k, kk=kk, s=s, bank=bank: e.transpose(
                        out=bank.t[:, kk * 128:(kk + 1) * 128], in_=xt0e.t[:, k, s * 128:(s + 1) * 128],
                        identity=ident.t[:]), reads=[xt0e, ident], writes=[bank])
                if kh == 0:
                    P.act(lambda e, s=s, kh=kh, bank=bank: e.activation(out=xtoke.t[:, s, kh * 512:(kh + 1) * 512],
                                                                        in_=bank.t[:], func=AF.Copy),
                          reads=[bank], writes=[xtoke])
                else:
                    P.dve(lambda e, s=s, kh=kh, bank=bank: e.tensor_copy(out=xtoke.t[:, s, kh * 512:(kh + 1) * 512],
                                                                         in_=bank.t[:]), reads=[bank], writes=[xtoke])
        P.dma("sp", out_d[ti * NT:(ti + 1) * NT, :].rearrange("(s p) d -> p s d", p=128), xtoke.t[:],
              reads=[xtoke], writes=[out_r], key="out")
    P.finish([out_r])
    return nc


def xattn_layer(c, l):
    P, A, pb, pd = c["P"], c["A"], c["pb"], c["pd"]
    onesb, gv, memhT = c["onesb"], c["gv"], c["memhT"]
    wv3 = lambda n: pd[n][l].rearrange("(k p) n -> p k n", p=128)
    wq = A.alloc("wq", [128, KT, D], BF16)
    wo = A.alloc("wo", [128, KT, D], BF16)
    wk = A.alloc("wk", [128, KT, D], BF16)
    wvv = A.alloc("wv", [128, KT, D], BF16)
    for w, n in ((wk, "w_k"), (wvv, "w_v"), (wq, "w_q"), (wo, "w_o")):
        c["load_w"](w, wv3(n), D)
    memn = A.alloc("memn", [128, KT, 256], BF16)
    kT = A.alloc("kT", [128, KT, 256], BF16)
    vt = A.alloc("vt", [128, 2, D], BF16)
    xts = [A.alloc("xt_0", [128, KT, NT], F32), A.alloc("xt_1", [128, KT, NT], F32)]
    hts = [A.alloc("ht_0", [128, KT, NT], BF16), A.alloc("ht_1", [128, KT, NT], BF16)]
    sq_pre = A.alloc("sq_pre", [128, KT, NT], BF16)
    rstd_pre = A.alloc("rstd_pre", [128, NT], F32)
    sq = A.alloc("sq", [128, KT, NT], BF16)
    rstd = A.alloc("rstd", [128, NT], F32)
    qT = A.alloc("qT", [128, KT, NT], BF16)
    expTs = [A.alloc("expT0", [128, 2, NT], BF16), A.alloc("expT1", [128, 2, NT], BF16)]
    rdens = [A.alloc("rden0", [128, NT], F32), A.alloc("rden1", [128, NT], F32)]
    oT = A.alloc("oT", [128, KT, NT], BF16)
    ybuf = A.alloc("ybuf", [128, KT, NT], F32)
    gm = gv["g_mem"]
    for k in range(KT):
        P.dve(lambda e, k=k: e.tensor_scalar(out=memn.t[:, k, :], in0=memhT.t[:, k, :], scalar1=gm.t[:, l, k:k + 1],
                                             scalar2=None, op0=ALU.mult), reads=[memhT, gm], writes=[memn])
    for o in range(KT):
        bank = pb[3 + (o % 2)]
        for k in range(KT):
            P.pe(lambda e, o=o, k=k, bank=bank: e.matmul(bank.t[:, 0:256], lhsT=wk.t[:, k, o * 128:(o + 1) * 128],
                                                         rhs=memn.t[:, k, :], start=(k == 0), stop=(k == KT - 1)),
                 reads=[wk, memn], writes=[bank])
        P.act(lambda e, o=o, bank=bank: e.activation(out=kT.t[:, o, :], in_=bank.t[:, 0:256], func=AF.Copy),
              reads=[bank], writes=[kT])
    for mt in range(2):
        for hf in range(2):
            bank = pb[3 + (hf % 2)]
            for k in range(KT):
                P.pe(lambda e, mt=mt, hf=hf, k=k, bank=bank: e.matmul(
                    bank.t[:], lhsT=memn.t[:, k, mt * 128:(mt + 1) * 128], rhs=wvv.t[:, k, hf * 512:(hf + 1) * 512],
                    start=(k == 0), stop=(k == KT - 1)), reads=[wvv, memn], writes=[bank])
            P.act(lambda e, mt=mt, hf=hf, bank=bank: e.activation(out=vt.t[:, mt, hf * 512:(hf + 1) * 512],
                                                                  in_=bank.t[:], func=AF.Copy),
                  reads=[bank], writes=[vt])
    NTL = c["NTILES"]

    def head_load(ti):
        c["load_x"](xts[ti % 2], ti)

    def head_norm(ti):
        c["prenorm"](xts[ti % 2], hts[ti % 2], sq_pre, rstd_pre, "g_x_pre", l)

    def qproj(ti):
        ht = hts[ti % 2]
        for o in range(KT):
            bank = pb[3 + (o % 2)]
            for k in range(KT):
                P.pe(lambda e, o=o, k=k, bank=bank, ht=ht: e.matmul(bank.t[:], lhsT=wq.t[:, k, o * 128:(o + 1) * 128],
                                                                    rhs=ht.t[:, k, :], start=(k == 0),
                                                                    stop=(k == KT - 1)), reads=[wq, ht], writes=[bank])
            P.act(lambda e, o=o, bank=bank: e.activation(out=qT.t[:, o, :], in_=bank.t[:], func=AF.Copy),
                  reads=[bank], writes=[qT])

    def heads(hs):
        for h in hs:
            expT = expTs[h % 2]
            rden = rdens[h % 2]
            for mt in range(2):
                bank = pb[3 + (mt % 2)]
                for dd in range(2):
                    P.pe(lambda e, h=h, mt=mt, dd=dd, bank=bank: e.matmul(
                        bank.t[:], lhsT=kT.t[:, 2 * h + dd, mt * 128:(mt + 1) * 128], rhs=qT.t[:, 2 * h + dd, :],
                        start=(dd == 0), stop=(dd == 1)), reads=[kT, qT], writes=[bank])
                P.act(lambda e, mt=mt, bank=bank, expT=expT: e.activation(out=expT.t[:, mt, :], in_=bank.t[:], func=AF.Exp,
                                                               scale=1.0 / 16.0), reads=[bank], writes=[expT])
            for mt in range(2):
                P.pe(lambda e, mt=mt, expT=expT: e.matmul(pb[5].t[:], lhsT=onesb.t[:], rhs=expT.t[:, mt, :], start=(mt == 0),
                                               stop=(mt == 1)), reads=[expT, onesb], writes=[pb[5]])
            P.dve(lambda e, rden=rden: e.reciprocal(out=rden.t[:], in_=pb[5].t[:]), reads=[pb[5]], writes=[rden])
            for dd in range(2):
                bank = pb[6 + dd]
                for mt in range(2):
                    P.pe(lambda e, h=h, mt=mt, dd=dd, bank=bank, expT=expT: e.matmul(
                        bank.t[:], lhsT=vt.t[:, mt, (2 * h + dd) * 128:(2 * h + dd + 1) * 128], rhs=expT.t[:, mt, :],
                        start=(mt == 0), stop=(mt == 1)), reads=[vt, expT], writes=[bank])
                P.dve(lambda e, h=h, dd=dd, bank=bank, rden=rden: e.tensor_tensor(out=oT.t[:, 2 * h + dd, :], in0=bank.t[:],
                                                                       in1=rden.t[:], op=ALU.mult),
                      reads=[bank, rden], writes=[oT])

    def tail(ti):
        def mm(o, bank):
            for k in range(KT):
                P.pe(lambda e, o=o, k=k, bank=bank: e.matmul(bank.t[:], lhsT=wo.t[:, k, o * 128:(o + 1) * 128],
                                                             rhs=oT.t[:, k, :], start=(k == 0), stop=(k == KT - 1)),
                     reads=[wo, oT], writes=[bank])
        c["postnorm_res"](xts[ti % 2], ybuf, sq, rstd, "g_x_post", l, mm)
        c["store_x"](xts[ti % 2], ti)

    head_load(0)
    head_norm(0)
    for ti in range(NTL):
        if ti + 1 < NTL:
            head_load(ti + 1)
        qproj(ti)
        heads([0, 1])
        if ti + 1 < NTL:
            head_norm(ti + 1)
        heads([2, 3])
        tail(ti)


def ffn_layer(c, l):
    P, A, pb, pd = c["P"], c["A"], c["pb"], c["pd"]
    onesb, gv, cb = c["onesb"], c["gv"], c["cb"]
    NF = 22
    wup = A.alloc("wup", [128, KT, 5632], BF16)
    wdn = A.alloc("wdn", [128, NF, D], BF16)
    c["load_w"](wup, pd["w_up"][l].rearrange("(k p) n -> p k n", p=128), 5632)
    c["load_w"](wdn, pd["w_down"][l].rearrange("(f p) n -> p f n", p=128), D)
    cw = A.alloc("cw", [128, 3, 44], F32)
    cbv = A.alloc("cbv", [128, 44], F32)
    zl = A.alloc("zl", [128, 44, 2], F32)
    P.dma("sp", cw.t[:], pd["conv_w"][l].rearrange("w (f p) -> p w f", p=128), writes=[cw],
          allow_slow_non_contiguous=True)
    P.dma("sp", cbv.t[:], pd["conv_b"][l].rearrange("(f p) -> p f", p=128), writes=[cbv],
          allow_slow_non_contiguous=True)
    zls = [zl.sub(i) for i in range(44)]
    P.pool(lambda e: e.memset(zl.t[:], 0.0), writes=zls)
    bh = A.alloc("bh", [128, 44, 2], F32)
    hh = A.alloc("hh", [128, 2, 44], F32)
    off_xt = A.off
    xt = A.alloc("xt", [128, KT, NT], F32)
    ybuf = P.carve("ybuf_f", A.buf, off_xt, [128, KT, NT], F32)
    off_ht = A.off
    ht = A.alloc("ht", [128, KT, NT], BF16)
    sq2 = P.carve("sq2_f", A.buf, off_ht, [128, KT, NT], BF16)
    rstd = A.alloc("rstd", [128, NT], F32)
    off_g = A.off
    gbuf = A.alloc("gbuf", [128, NF, NT], BF16)
    sq = P.carve("sq_f", A.buf, off_g, [128, KT, NT], BF16)
    acc = [[A.alloc("accv0", [128, NT], F32), A.alloc("accg0", [128, NT], F32)],
           [A.alloc("accv1", [128, NT], F32), A.alloc("accg1", [128, NT], F32)]]
    xo = [A.alloc("xo0", [128, NT], F32), A.alloc("xo1", [128, NT], F32)]
    xs_r = c["xs_r"]
    xs_v = c["xs_v"]
    for ti in range(c["NTILES"]):
        c["load_x"](xt, ti)
        P.pool(lambda e: e.tensor_tensor(out=hh.t[:, 0, :], in0=cw.t[:, 0, :], in1=zl.t[:, :, 0], op=ALU.mult),
               reads=[cw] + zls, writes=[hh])
        P.pool(lambda e: e.tensor_tensor(out=hh.t[:, 1, :], in0=cw.t[:, 1, :], in1=zl.t[:, :, 1], op=ALU.mult),
               reads=[cw] + zls, writes=[hh])
        P.pool(lambda e: e.tensor_tensor(out=hh.t[:, 0, :], in0=hh.t[:, 0, :], in1=hh.t[:, 1, :], op=ALU.add),
               reads=[hh], writes=[hh])
        P.pool(lambda e: e.tensor_tensor(out=bh.t[:, :, 0], in0=hh.t[:, 0, :], in1=cbv.t[:], op=ALU.add),
               reads=[hh, cbv], writes=[bh])
        P.pool(lambda e: e.tensor_tensor(out=hh.t[:, 1, :], in0=cw.t[:, 0, :], in1=zl.t[:, :, 1], op=ALU.mult),
               reads=[cw] + zls, writes=[hh])
        P.pool(lambda e: e.tensor_tensor(out=bh.t[:, :, 1], in0=hh.t[:, 1, :], in1=cbv.t[:], op=ALU.add),
               reads=[hh, cbv], writes=[bh])
        c["prenorm"](xt, ht, sq, rstd, "g_ffn_pre", l)
        for f in range(NF):
            a2 = acc[f % 2]
            bks = (pb[3], pb[4]) if f % 2 == 0 else (pb[5], pb[6])
            for vg in range(2):
                ci = vg * NF + f
                bank = bks[vg]
                for k in range(KT):
                    P.pe(lambda e, ci=ci, k=k, bank=bank: e.matmul(bank.t[:], lhsT=wup.t[:, k, ci * 128:(ci + 1) * 128],
                                                                   rhs=ht.t[:, k, :], start=(k == 0),
                                                                   stop=(k == KT - 1)), reads=[wup, ht], writes=[bank])
            for vg in range(2):
                ci = vg * NF + f
                bank = bks[vg]
                a = a2[vg]
                P.act(lambda e, ci=ci, bank=bank, a=a: e.activation(out=a.t[:, 2:NT], in_=bank.t[:, 2:NT],
                                                                    func=AF.Identity, bias=cbv.t[:, ci:ci + 1],
                                                                    scale=cw.t[:, 2, ci:ci + 1]),
                      reads=[bank, cbv, cw], writes=[a])
                for col in range(2):
                    P.act(lambda e, ci=ci, bank=bank, a=a, col=col: e.activation(
                        out=a.t[:, col:col + 1], in_=bank.t[:, col:col + 1], func=AF.Identity,
                        bias=bh.t[:, ci, col:col + 1], scale=cw.t[:, 2, ci:ci + 1]), reads=[bank, bh, cw], writes=[a])
                P.act(lambda e, ci=ci, bank=bank: e.activation(out=zl.t[:, ci, :], in_=bank.t[:, NT - 2:NT],
                                                               func=AF.Copy), reads=[bank], writes=[zls[ci]])
            for vg in range(2):
                ci = vg * NF + f
                bank = bks[vg]
                a = a2[vg]
                P.dve(lambda e, ci=ci, bank=bank, a=a: e.scalar_tensor_tensor(
                    out=a.t[:, 1:NT], in0=bank.t[:, 0:NT - 1], scalar=cw.t[:, 1, ci:ci + 1], in1=a.t[:, 1:NT],
                    op0=ALU.mult, op1=ALU.add), reads=[bank, cw, a], writes=[a])
                P.dve(lambda e, ci=ci, bank=bank, a=a: e.scalar_tensor_tensor(
                    out=a.t[:, 2:NT], in0=bank.t[:, 0:NT - 2], scalar=cw.t[:, 0, ci:ci + 1], in1=a.t[:, 2:NT],
                    op0=ALU.mult, op1=ALU.add), reads=[bank, cw, a], writes=[a])
            for fp in ([f - 1] if f > 0 else []) + ([f] if f == NF - 1 else []):
                ap2 = acc[fp % 2]
                P.act(lambda e, ap2=ap2: e.activation(out=ap2[1].t[:], in_=ap2[1].t[:], func=AF.Gelu_apprx_tanh),
                      reads=[ap2[1]], writes=[ap2[1]])
                P.pool(lambda e, fp=fp, ap2=ap2: e.tensor_tensor(out=gbuf.t[:, fp, :], in0=ap2[0].t[:], in1=ap2[1].t[:],
                                                                 op=ALU.mult), reads=[ap2[0], ap2[1]],
                       writes=[gbuf.sub(fp)])

        def mm(o, bank):
            for f in range(NF):
                P.pe(lambda e, o=o, f=f, bank=bank: e.matmul(bank.t[:], lhsT=wdn.t[:, f, o * 128:(o + 1) * 128],
                                                             rhs=gbuf.t[:, f, :], start=(f == 0), stop=(f == NF - 1)),
                     reads=[wdn, gbuf.sub(f)], writes=[bank])
        g = gv["g_ffn_post"]
        for o in range(KT):
            bank = pb[1 + (o % 2)]
            mm(o, bank)
            P.act(lambda e, o=o, bank=bank: e.activation(out=ybuf.t[:, o, :], in_=bank.t[:], func=AF.Copy),
                  reads=[bank], writes=[ybuf, xt])
            P.dve(lambda e, o=o, bank=bank: e.tensor_tensor(out=sq2.t[:, o, :], in0=bank.t[:], in1=ybuf.t[:, o, :],
                                                            op=ALU.mult), reads=[bank, ybuf], writes=[sq2, ht])
        for k in range(KT):
            P.pe(lambda e, k=k: e.matmul(pb[0].t[:], lhsT=onesb.t[:], rhs=sq2.t[:, k, :], start=(k == 0),
                                         stop=(k == KT - 1)), reads=[sq2, onesb], writes=[pb[0]])
        c["rstd_from_ss"](pb[0].t[:], rstd, D, [pb[0]])
        for o in range(KT):
            xb = xo[o % 2]
            P.dma("sp", xb.t[:], xs_v[:, o, ti * NT:(ti + 1) * NT], reads=[xs_r[ti]], writes=[xb])
            P.dve(lambda e, o=o: e.scalar_tensor_tensor(out=ybuf.t[:, o, :], in0=ybuf.t[:, o, :],
                                                        scalar=g.t[:, l, o:o + 1], in1=rstd.t[:], op0=ALU.mult,
                                                        op1=ALU.mult), reads=[ybuf, g, rstd], writes=[ybuf, xt])
            P.pool(lambda e, o=o, xb=xb: e.tensor_tensor(out=xb.t[:], in0=xb.t[:], in1=ybuf.t[:, o, :], op=ALU.add),
                   reads=[xb, ybuf], writes=[xb])
            P.dma("sp", xs_v[:, o, ti * NT:(ti + 1) * NT], xb.t[:], reads=[xb], writes=[xs_r[ti]])


def _run(inputs, S=8192, DEPTH=4, STAGES=("mix", "xat", "ffn"), NCORES=8):
    nc = build_program(S=S, DEPTH=DEPTH, STAGES=STAGES)
    in_maps = []
    for c in range(NCORES):
        b = c % 2
        m = {"x": np.ascontiguousarray(inputs["x"][b, :S]), "mem": np.ascontiguousarray(inputs["mem"][b])}
        for n in PNAMES:
            m[n] = np.ascontiguousarray(inputs[n])
        in_maps.append(m)
    import os
    if os.environ.get("KTRACE"):
        res = run_bass_kernel_spmd(nc, in_maps, core_ids=list(range(NCORES)), trace=True)
        print("EXEC_TIME_NS", res.exec_time_ns, flush=True)
    else:
        res = run_bass_kernel_spmd(nc, in_maps, core_ids=list(range(NCORES)))
    return np.stack([res.results[0]["out"], res.results[1 % NCORES]["out"]], axis=0)


def kernel(**inputs):
    inputs = {k: np.asarray(v) for k, v in inputs.items()}
    return _run(inputs).astype(np.float32)


def mixer_layer(c, l):
    P, A, pb, pd = c["P"], c["A"], c["pb"], c["pd"]
    onesb, gv, gvv, cb = c["onesb"], c["gv"], c["gvv"], c["cb"]
    ident, identb, blkones, maskJ, tril = c["ident"], c["identb"], c["blkones"], c["maskJ"], c["tril"]
    pst = c["pst"]
    NG, NPR = 32, 16
    w_in = A.alloc("w_in", [128, KT, 2048], BF16)
    w_out = A.alloc("w_out", [128, KT, D], BF16)
    c["load_w"](w_in, pd["w_in"][l].rearrange("(k p) n -> p k n", p=128), 2048)
    c["load_w"](w_out, pd["w_out"][l].rearrange("(k p) n -> p k n", p=128), D)
    WDT = [A.alloc("WDTr", [128, NPR, 128], BF16), A.alloc("WDTi", [128, NPR, 128], BF16)]
    Wo = [A.alloc("Wor", [128, NPR, 128], BF16), A.alloc("Woi", [128, NPR, 128], BF16)]
    Kmat = A.alloc("Kmat", [128, NG, 128], BF16)
    C1 = A.alloc("C1", [128, NPR, 64], F32)
    S1 = A.alloc("S1", [128, NPR, 64], F32)
    Mtab = A.alloc("Mtab", [128, NPR, 64], F32)
    rho8 = A.alloc("rho8", [128, NPR], F32)
    Lf = [A.alloc("Lfr", [128, NPR, 65], F32), A.alloc("Lfi", [128, NPR, 65], F32)]
    WsT = A.alloc("WsT", [128, 8, 128], BF16)
    bs2 = A.alloc("bs2", [128, 1024], BF16)
    off_tmp = A.off
    dv = P.res("dv")
    sm = A.alloc("sm", [128, 64, NPR], F32)
    Bc = [A.alloc("Bre", [128, NPR, 16], F32), A.alloc("Bim", [128, NPR, 16], F32)]
    Ct = [A.alloc("Ctr", [128, NPR, 16], F32), A.alloc("Cti", [128, NPR, 16], F32)]
    Bb = [A.alloc("Bbr", [128, NPR, 16], F32), A.alloc("Bbi", [128, NPR, 16], F32)]
    WD = [A.alloc("WDr", [128, NPR, 128], F32), A.alloc("WDi", [128, NPR, 128], F32)]
    WDt = [A.alloc("WDtr", [128, NPR, 128], BF16), A.alloc("WDti", [128, NPR, 128], BF16)]
    T = [A.alloc("dT1", [128, NPR, 128], F32), A.alloc("dT2", [128, NPR, 128], F32)]
    dcol = A.alloc("dcol", [128, NG], F32)
    wsl = A.alloc("wsl", [128, 8, 128], F32)
    bsf = A.alloc("bsf", [128, 1024], F32)
    tK = A.alloc("tK", [128, 128], F32)

    def S(i):
        return sm.t[:, i, :]

    def tt(out, a, b, op, eng="dve"):
        P.eng(eng, lambda e: e.tensor_tensor(out=out, in0=a, in1=b, op=op), reads=[dv], writes=[dv])

    def ts(out, a, s1, op, eng="dve"):
        P.eng(eng, lambda e: e.tensor_scalar(out=out, in0=a, scalar1=s1, scalar2=None, op0=op), reads=[dv], writes=[dv])

    def actf(out, a, func, scale=1.0, bias=None):
        b = cb.t[:, 2:3] if bias is None else bias
        P.act(lambda e: e.activation(out=out, in_=a, func=func, bias=b, scale=scale), reads=[dv, cb], writes=[dv])

    def cmul(o_r, o_i, ar, ai, br, bi, t1, t2):
        tt(t1, ar, br, ALU.mult); tt(t2, ai, bi, ALU.mult); tt(o_r, t1, t2, ALU.subtract)
        tt(t1, ar, bi, ALU.mult); tt(t2, ai, br, ALU.mult); tt(o_i, t1, t2, ALU.add)

    def dmap(out, src):
        P.dma("sp", out, src, writes=[dv], key="dvload", allow_slow_non_contiguous=True)

    LR, LI, LDT, DT, X1, X2, MG, CS, SN, AR, AI, NR, NI, TA, TB, RHO, IRHO, U8R, U8I, I8R, I8I, FR, FI, DEN = range(24)
    AP0 = 24
    dmap(S(LR), pd["lam_re"][l].rearrange("(pr m) p -> (m p) pr", m=2))
    dmap(S(LI), pd["lam_im"][l].rearrange("(pr m) p -> (m p) pr", m=2))
    for m in range(2):
        dmap(sm.t[m * 64:(m + 1) * 64, LDT, :],
             pd["log_dt"][l].rearrange("(pr m) -> m pr", m=2)[m].partition_broadcast(64))
        for comp, nm in ((0, "c_re"), (1, "c_im")):
            for pr_ in range(NPR):
                dmap(Ct[comp].t[m * 64:(m + 1) * 64, pr_, :],
                     pd[nm][l].rearrange("(pr m) c p -> m pr p c", m=2)[m][pr_])
    dmap(Bc[0].t[:], pd["b_re"][l].rearrange("(pr m) p c -> (m p) pr c", m=2))
    dmap(Bc[1].t[:], pd["b_im"][l].rearrange("(pr m) p c -> (m p) pr c", m=2))
    for j in range(8):
        dmap(dcol.t[j * 16:(j + 1) * 16, :], pd["d_skip"][l].rearrange("g c -> c g"))
    dmap(wsl.t[:], pd["w_s"][l].rearrange("h t s -> t h s"))
    for r in (0, 32):
        dmap(bsf.t[r:r + 1, :], pd["b_s"][l:l + 1].rearrange("o h t -> o (h t)"))
    actf(S(DT), S(LDT), AF.Exp)
    tt(S(X1), S(LR), S(DT), ALU.mult)
    tt(S(X2), S(LI), S(DT), ALU.mult)
    actf(S(MG), S(X1), AF.Exp, scale=1.0 / 64)
    actf(S(CS), S(X2), AF.Sin, scale=1.0 / 64, bias=cb.t[:, 0:1])
    actf(S(SN), S(X2), AF.Sin, scale=1.0 / 64)
    tt(S(AR), S(MG), S(CS), ALU.mult)
    tt(S(AI), S(MG), S(SN), ALU.mult)
    cur = (AR, AI)
    nxt = (NR, NI)
    for _ in range(6):
        tt(S(TA), S(cur[0]), S(cur[0]), ALU.mult)
        tt(S(TB), S(cur[1]), S(cur[1]), ALU.mult)
        tt(S(nxt[0]), S(TA), S(TB), ALU.subtract)
        tt(S(TA), S(cur[0]), S(cur[1]), ALU.mult)
        tt(S(nxt[1]), S(TA), S(TA), ALU.add)
        cur, nxt = nxt, cur
    A1 = cur

    def apr(m):
        return S(AP0 + 2 * m)

    def api(m):
        return S(AP0 + 2 * m + 1)
    P.dve(lambda e: e.memset(apr(0), 1.0), reads=[dv], writes=[dv])
    P.dve(lambda e: e.memset(api(0), 0.0), reads=[dv], writes=[dv])
    P.dve(lambda e: e.tensor_copy(out=apr(1), in_=S(A1[0])), reads=[dv], writes=[dv])
    P.dve(lambda e: e.tensor_copy(out=api(1), in_=S(A1[1])), reads=[dv], writes=[dv])
    for m in range(2, 9):
        cmul(apr(m), api(m), apr(m - 1), api(m - 1), apr(1), api(1), S(TA), S(TB))
    actf(S(RHO), S(X1), AF.Exp, scale=8.0)
    P.dve(lambda e: e.tensor_copy(out=rho8.t[:], in_=S(RHO)), reads=[dv], writes=[dv, rho8])
    P.dve(lambda e: e.reciprocal(out=S(IRHO), in_=S(RHO)), reads=[dv], writes=[dv])
    tt(S(U8R), apr(8), S(IRHO), ALU.mult)
    tt(S(U8I), api(8), S(IRHO), ALU.mult)
    tt(S(TA), S(IRHO), S(IRHO), ALU.mult)
    tt(S(I8R), apr(8), S(TA), ALU.mult)
    tt(S(I8I), api(8), S(TA), ALU.mult)
    ts(S(I8I), S(I8I), -1.0, ALU.mult)
    ts(S(NR), apr(1), -1.0, ALU.add)
    tt(S(TA), S(LR), S(LR), ALU.mult)
    tt(S(TB), S(LI), S(LI), ALU.mult)
    tt(S(DEN), S(TA), S(TB), ALU.add)
    P.dve(lambda e: e.reciprocal(out=S(DEN), in_=S(DEN)), reads=[dv], writes=[dv])
    tt(S(TA), S(NR), S(LR), ALU.mult)
    tt(S(TB), api(1), S(LI), ALU.mult)
    tt(S(FR), S(TA), S(TB), ALU.add)
    tt(S(FR), S(FR), S(DEN), ALU.mult)
    tt(S(TA), api(1), S(LR), ALU.mult)
    tt(S(TB), S(NR), S(LI), ALU.mult)
    tt(S(FI), S(TA), S(TB), ALU.subtract)
    tt(S(FI), S(FI), S(DEN), ALU.mult)

    def bc(ap2, n):
        return ap2.unsqueeze(2).to_broadcast([128, NPR, n])
    t16 = [T[0].t[:, :, 0:16], T[1].t[:, :, 0:16]]
    cmul(Bb[0].t[:], Bb[1].t[:], bc(S(FR), 16), bc(S(FI), 16), Bc[0].t[:], Bc[1].t[:], t16[0], t16[1])
    WD4 = [w.t[:].rearrange("p a (j c) -> p a j c", j=8) for w in WD]
    for j in range(8):
        cmul(WD4[0][:, :, j, :], WD4[1][:, :, j, :], bc(apr(7 - j), 16), bc(api(7 - j), 16), Bb[0].t[:], Bb[1].t[:],
             t16[0], t16[1])
    cmul(WDt[0].t[:], WDt[1].t[:], bc(S(I8R), 128), bc(S(I8I), 128), WD[0].t[:], WD[1].t[:], T[0].t[:], T[1].t[:])
    for comp in range(2):
        for q4 in range(4):
            bank = pb[3 + (q4 % 2)]
            for q in range(4):
                pr = q4 * 4 + q
                P.pe(lambda e, comp=comp, pr=pr, q=q, bank=bank: e.transpose(
                    out=bank.t[:, q * 128:(q + 1) * 128], in_=WD[comp].t[:, pr, :], identity=ident.t[:]),
                    reads=[dv, ident], writes=[bank])
            P.dve(lambda e, comp=comp, q4=q4, bank=bank: e.tensor_copy(
                out=WDT[comp].t[:, q4 * 4:(q4 + 1) * 4, :], in_=bank.t[:].rearrange("p (a b) -> p a b", a=4)),
                reads=[bank], writes=[WDT[comp]])
    Wo4 = [w.t[:].rearrange("p a (j c) -> p a j c", j=8) for w in Wo]
    for j in range(8):
        ar_, ai_ = bc(apr(j + 1), 16), bc(api(j + 1), 16)
        tt(t16[0], Ct[0].t[:], ar_, ALU.mult); tt(t16[1], Ct[1].t[:], ai_, ALU.mult)
        P.dve(lambda e, j=j: e.tensor_tensor(out=Wo4[0][:, :, j, :], in0=t16[0], in1=t16[1], op=ALU.subtract),
              reads=[dv], writes=[dv, Wo[0]])
        tt(t16[0], Ct[0].t[:], ai_, ALU.mult); tt(t16[1], Ct[1].t[:], ar_, ALU.mult)
        tt(t16[0], t16[0], t16[1], ALU.add)
        P.dve(lambda e, j=j: e.tensor_scalar(out=Wo4[1][:, :, j, :], in0=t16[0], scalar1=-1.0, scalar2=None,
                                             op0=ALU.mult), reads=[dv], writes=[dv, Wo[1]])
    for g in range(NG):
        pr, m = divmod(g, 2)
        bank = pb[5 + (g % 2)]
        sl = slice(m * 64, (m + 1) * 64)
        P.pe(lambda e, pr=pr, sl=sl, bank=bank: e.matmul(bank.t[:, 0:128], lhsT=WDt[0].t[sl, pr, :],
                                                          rhs=Wo[0].t[sl, pr, :], start=True, stop=False),
             reads=[dv, Wo[0]], writes=[bank])
        P.pe(lambda e, pr=pr, sl=sl, bank=bank: e.matmul(bank.t[:, 0:128], lhsT=WDt[1].t[sl, pr, :],
                                                          rhs=Wo[1].t[sl, pr, :], start=False, stop=True),
             reads=[dv, Wo[1]], writes=[bank])
        P.dve(lambda e, bank=bank: e.tensor_tensor(out=tK.t[:], in0=bank.t[:, 0:128], in1=maskJ.t[:], op=ALU.mult),
              reads=[bank, maskJ, dv], writes=[dv])
        P.dve(lambda e, g=g: e.scalar_tensor_tensor(out=Kmat.t[:, g, :], in0=ident.t[:], scalar=dcol.t[:, g:g + 1],
                                                    in1=tK.t[:], op0=ALU.mult, op1=ALU.add),
              reads=[dv, ident], writes=[dv, Kmat])
    P.dve(lambda e: e.tensor_copy(out=C1.t[:, :, 0], in_=S(U8R)), reads=[dv], writes=[dv, C1])
    P.dve(lambda e: e.tensor_copy(out=S1.t[:, :, 0], in_=S(U8I)), reads=[dv], writes=[dv, S1])
    s_ = 1
    while s_ < 64:
        cr = C1.t[:, :, s_ - 1:s_].to_broadcast([128, NPR, s_])
        ci = S1.t[:, :, s_ - 1:s_].to_broadcast([128, NPR, s_])
        t1, t2 = T[0].t[:, :, 0:s_], T[1].t[:, :, 0:s_]
        cmul(C1.t[:, :, s_:2 * s_], S1.t[:, :, s_:2 * s_], cr, ci, C1.t[:, :, 0:s_], S1.t[:, :, 0:s_], t1, t2)
        s_ *= 2
    P.dve(lambda e: e.tensor_copy(out=Mtab.t[:], in_=bc(rho8.t[:], 64)), reads=[dv, rho8], writes=[dv, Mtab])
    P.dve(lambda e: e.memset(Mtab.t[:, :, 0], 0.0), reads=[dv], writes=[dv, Mtab])
    for comp in range(2):
        P.dve(lambda e, comp=comp: e.memset(Lf[comp].t[:], 0.0), reads=[dv], writes=[dv, Lf[comp]])
    for h in range(8):
        bank = pb[3 + (h % 2)]
        P.pe(lambda e, h=h, bank=bank: e.transpose(out=bank.t[:, 0:128], in_=wsl.t[:, h, :], identity=ident.t[:]),
             reads=[dv, ident], writes=[bank])
        P.dve(lambda e, h=h, bank=bank: e.tensor_tensor(out=WsT.t[:, h, :], in0=bank.t[:, 0:128], in1=tril.t[:],
                                                        op=ALU.mult), reads=[bank, tril], writes=[WsT])
    P.dve(lambda e: e.memset(bs2.t[:], 0.0), reads=[dv], writes=[dv, bs2])
    P.dve(lambda e: e.tensor_copy(out=bs2.t[0:1, :], in_=bsf.t[0:1, :]), reads=[dv], writes=[dv, bs2])
    P.dve(lambda e: e.tensor_copy(out=bs2.t[32:33, :], in_=bsf.t[32:33, :]), reads=[dv], writes=[dv, bs2])
    P.dve(lambda e: e.tensor_tensor(out=bsf.t[32:33, :], in0=bsf.t[32:33, :], in1=bs2.t[32:33, :], op=ALU.subtract),
          reads=[dv, bs2], writes=[dv])
    P.dve(lambda e: e.tensor_copy(out=bs2.t[32:33, :], in_=bsf.t[32:33, :]), reads=[dv], writes=[dv, bs2])
    P.barrier()
    A.off = off_tmp
    xt = A.alloc("xt", [128, KT, NT], F32)
    off_ht = A.off
    ht = A.alloc("ht", [128, KT, NT], BF16)
    ycat = P.carve("ycat", A.buf, off_ht, [128, KT, NT], BF16)
    P.alias(ht, ycat)
    rstd = A.alloc("rstd", [128, NT], F32)
    off_sq = A.off
    sq = A.alloc("sq", [128, KT, NT], BF16)
    vtok = P.carve("vtok", A.buf, off_sq, [128, 4, NT], BF16)
    ub = P.carve("ub", A.buf, off_sq + 4096, [128, 4, NT], BF16)
    P.alias(sq, vtok); P.alias(sq, ub)
    off_u = A.off
    Ublk = A.alloc("Ublk", [128, NG, 8, 16], BF16)
    Tt = [P.carve("Tt1", A.buf, off_u, [128, NPR, 64], F32), P.carve("Tt2", A.buf, off_u + 4096, [128, NPR, 64], F32)]
    P.alias(Ublk, Tt[0]); P.alias(Ublk, Tt[1])
    UblkT = A.alloc("UblkT", [128, NG, 64], BF16)
    Z = [A.alloc("Zr", [128, NPR, 64], F32), A.alloc("Zi", [128, NPR, 64], F32)]
    W = [A.alloc("Wr", [128, NPR, 64], F32), A.alloc("Wi", [128, NPR, 64], F32)]
    Lb = [A.alloc("Lbr", [128, NPR, 64], BF16), A.alloc("Lbi", [128, NPR, 64], BF16)]
    off_y = A.off
    Yblk = A.alloc("Yblk", [128, 8, NT], F32)
    ybuf = P.carve("ybuf", A.buf, off_y, [128, KT, NT], F32)
    P.alias(Yblk, ybuf)
    ytmp = A.alloc("ytmp", [128, NT], F32)
    sg = A.alloc("sg", [128, 4, NT], BF16)
    vn = A.alloc("vn", [128, NT], BF16)
    vsq = A.alloc("vsq", [128, NT], BF16)
    vrs = A.alloc("vrs", [128, NT], F32)
    t16s = A.alloc("t16s", [128, 2, NPR], F32)
    c.setdefault("dbg", {})["mix_off"] = A.off
    Dre = pst.t[:, 3 * 512:5 * 512].rearrange("p (a n) -> p a n", a=NPR)
    Dim = pst.t[:, 5 * 512:7 * 512].rearrange("p (a n) -> p a n", a=NPR)
    Dv = [Dre, Dim]
    Dbank = [[pb[3], pb[4]], [pb[5], pb[6]]]
    pbT = {b: pb[b].t.bitcast(BF16) for b in (5, 6, 7)}
    gpost = "g_mix_post"

    for ti in range(c["NTILES"]):
        c["load_x"](xt, ti)
        c["prenorm"](xt, ht, sq, rstd, "g_mix_pre", l)
        for j in range(8):
            bank = pb[3 + (j % 2)]
            for k in range(KT):
                P.pe(lambda e, j=j, k=k, bank=bank: e.matmul(bank.t[0:64, :], lhsT=ht.t[:, k, j:NT:8],
                                                             rhs=w_in.t[:, k, 0:512], start=(k == 0),
                                                             stop=(k == KT - 1)), reads=[ht, w_in], writes=[bank])
            src = bank.t[0:64, :].rearrange("p (g c) -> p g c", g=NG)
            if j % 2 == 0:
                P.act(lambda e, j=j, src=src: e.activation(out=Ublk.t[0:64, :, j, :], in_=src, func=AF.Copy),
                      reads=[bank], writes=[Ublk])
            else:
                P.dve(lambda e, j=j, src=src: e.tensor_copy(out=Ublk.t[0:64, :, j, :], in_=src),
                      reads=[bank], writes=[Ublk])
        for g8 in range(4):
            bno = 5 + (g8 % 2)
            for q in range(8):
                g = g8 * 8 + q
                P.pe(lambda e, g=g, q=q, bno=bno: e.transpose(
                    out=pbT[bno][:, q * 64:(q + 1) * 64], in_=Ublk.t[0:64, g, :, :].rearrange("p j c -> p (j c)"),
                    identity=identb.t[0:64, 0:64]), reads=[Ublk, identb], writes=[pb[bno]])
            P.act(lambda e, g8=g8, bno=bno: e.activation(
                out=UblkT.t[:, g8 * 8:(g8 + 1) * 8, :], in_=pbT[bno][:, 0:512].rearrange("p (a n) -> p a n", a=8),
                func=AF.Copy), reads=[pb[bno]], writes=[UblkT])
        if ti > 0:
            for comp in range(2):
                P.act(lambda e, comp=comp: e.activation(out=Lf[comp].t[:, :, 0], in_=Lf[comp].t[:, :, 64], func=AF.Copy),
                      reads=[Lf[comp]], writes=[Lf[comp]])
        for g in range(NG):
            pr, m = divmod(g, 2)
            for comp in range(2):
                bank = Dbank[comp][pr // 8]
                P.pe(lambda e, g=g, pr=pr, m=m, comp=comp: e.matmul(
                    Dv[comp][m * 64:(m + 1) * 64, pr, :], lhsT=WDT[comp].t[:, pr, m * 64:(m + 1) * 64],
                    rhs=UblkT.t[:, g, :], start=True, stop=True), reads=[UblkT, WDT[comp]], writes=[bank])
        dr = [pb[3], pb[4], pb[5], pb[6]]

        def dtt(out, a, b, op, rd, wr):
            P.dve(lambda e: e.tensor_tensor(out=out, in0=a, in1=b, op=op), reads=rd, writes=wr)
        dtt(Tt[0].t[:], Dre, C1.t[:], ALU.mult, dr + [C1], [Tt[0]])
        dtt(Tt[1].t[:], Dim, S1.t[:], ALU.mult, dr + [S1], [Tt[1]])
        dtt(Z[0].t[:], Tt[0].t[:], Tt[1].t[:], ALU.add, Tt, [Z[0]])
        dtt(Tt[0].t[:], Dim, C1.t[:], ALU.mult, dr + [C1], [Tt[0]])
        dtt(Tt[1].t[:], Dre, S1.t[:], ALU.mult, dr + [S1], [Tt[1]])
        dtt(Z[1].t[:], Tt[0].t[:], Tt[1].t[:], ALU.subtract, Tt, [Z[1]])
        for comp in range(2):
            dtt(t16s.t[:, comp, :], Lf[comp].t[:, :, 0], rho8.t[:], ALU.mult, [Lf[comp], rho8], [t16s])
            dtt(Z[comp].t[:, :, 0], Z[comp].t[:, :, 0], t16s.t[:, comp, :], ALU.add, [Z[comp], t16s], [Z[comp]])
            P.dve(lambda e, comp=comp: e.tensor_tensor_scan(
                out=W[comp].t[:].rearrange("p a n -> p (a n)"), data0=Mtab.t[:].rearrange("p a n -> p (a n)"),
                data1=Z[comp].t[:].rearrange("p a n -> p (a n)"), initial=0.0, op0=ALU.mult, op1=ALU.add),
                reads=[Mtab, Z[comp]], writes=[W[comp]])
        dtt(Tt[0].t[:], C1.t[:], W[0].t[:], ALU.mult, [C1, W[0]], [Tt[0]])
        dtt(Tt[1].t[:], S1.t[:], W[1].t[:], ALU.mult, [S1, W[1]], [Tt[1]])
        dtt(Lf[0].t[:, :, 1:65], Tt[0].t[:], Tt[1].t[:], ALU.subtract, Tt, [Lf[0]])
        dtt(Tt[0].t[:], C1.t[:], W[1].t[:], ALU.mult, [C1, W[1]], [Tt[0]])
        dtt(Tt[1].t[:], S1.t[:], W[0].t[:], ALU.mult, [S1, W[0]], [Tt[1]])
        dtt(Lf[1].t[:, :, 1:65], Tt[0].t[:], Tt[1].t[:], ALU.add, Tt, [Lf[1]])
        for comp in range(2):
            P.act(lambda e, comp=comp: e.activation(out=Lb[comp].t[:], in_=Lf[comp].t[:, :, 0:64], func=AF.Copy),
                  reads=[Lf[comp]], writes=[Lb[comp]])
        for ct in range(4):
            for grp in range(3):
                col = 512 * (grp + 1) + ct * 128
                bank = pb[1 + ((ct * 3 + grp) % 2)]
                for k in range(KT):
                    P.pe(lambda e, col=col, k=k, bank=bank: e.matmul(bank.t[:], lhsT=w_in.t[:, k, col:col + 128],
                                                                     rhs=ht.t[:, k, :], start=(k == 0),
                                                                     stop=(k == KT - 1)), reads=[ht, w_in], writes=[bank])
                if grp == 0:
                    P.act(lambda e, ct=ct, bank=bank: e.activation(out=sg.t[:, ct, :], in_=bank.t[:], func=AF.Sigmoid),
                          reads=[bank], writes=[sg])
                elif grp == 1:
                    P.act(lambda e, ct=ct, bank=bank: e.activation(out=ub.t[:, ct, :], in_=bank.t[:], func=AF.Copy),
                          reads=[bank], writes=[ub])
                else:
                    P.act(lambda e, bank=bank: e.activation(out=vsq.t[:], in_=bank.t[:], func=AF.Square),
                          reads=[bank], writes=[vsq])
                    P.pe(lambda e: e.matmul(pb[0].t[:], lhsT=blkones.t[:], rhs=vsq.t[:], start=True, stop=True),
                         reads=[vsq, blkones], writes=[pb[0]])
                    c["rstd_from_ss"](pb[0].t[:], vrs, 64, [pb[0]])
                    P.dve(lambda e, ct=ct, bank=bank: e.scalar_tensor_tensor(
                        out=vn.t[:], in0=bank.t[:], scalar=gvv.t[:, l, ct:ct + 1], in1=vrs.t[:], op0=ALU.mult,
                        op1=ALU.mult), reads=[bank, gvv, vrs], writes=[vn])
                    for cc in range(4):
                        P.pe(lambda e, cc=cc: e.transpose(out=pbT[7][:, cc * 128:(cc + 1) * 128],
                                                          in_=vn.t[:, cc * 128:(cc + 1) * 128], identity=identb.t[:]),
                             reads=[vn, identb], writes=[pb[7]])
                    P.dve(lambda e, ct=ct: e.tensor_copy(
                        out=vtok.t[:, :, ct * 128:(ct + 1) * 128],
                        in_=pbT[7][:, 0:512].rearrange("p (a n) -> p a n", a=4)), reads=[pb[7]], writes=[vtok])
        for g4 in range(8):
            bank = pb[1 + (g4 % 2)]
            for q in range(4):
                g = g4 * 4 + q
                pr, m = divmod(g, 2)
                sl = slice(m * 64, (m + 1) * 64)
                osl = bank.t[0:64, q * 128:(q + 1) * 128]
                P.pe(lambda e, g=g, osl=osl: e.matmul(osl, lhsT=UblkT.t[:, g, :], rhs=Kmat.t[:, g, :], start=True,
                                                      stop=False), reads=[UblkT, Kmat], writes=[bank])
                P.pe(lambda e, pr=pr, sl=sl, osl=osl: e.matmul(osl, lhsT=Lb[0].t[sl, pr, :], rhs=Wo[0].t[sl, pr, :],
                                                               start=False, stop=False), reads=[Lb[0], Wo[0]],
                     writes=[bank])
                P.pe(lambda e, pr=pr, sl=sl, osl=osl: e.matmul(osl, lhsT=Lb[1].t[sl, pr, :], rhs=Wo[1].t[sl, pr, :],
                                                               start=False, stop=True), reads=[Lb[1], Wo[1]],
                     writes=[bank])
            P.dve(lambda e, g4=g4, bank=bank: e.tensor_copy(
                out=Yblk.t[0:64, :, g4 * 64:(g4 + 1) * 64].rearrange("p j (g c) -> p j g c", g=4),
                in_=bank.t[0:64, :].rearrange("p (g j c) -> p j g c", g=4, j=8)), reads=[bank], writes=[Yblk])
        for ct in range(4):
            bank = pb[3 + (ct % 2)]
            for j in range(8):
                P.pe(lambda e, ct=ct, j=j, bank=bank: e.transpose(
                    out=bank.t[:, j * 64:(j + 1) * 64], in_=Yblk.t[0:64, j, ct * 128:(ct + 1) * 128],
                    identity=ident.t[0:64, 0:64]), reads=[Yblk, ident], writes=[bank])
            P.act(lambda e, bank=bank: e.activation(out=ytmp.t[:].rearrange("p (n j) -> p n j", j=8),
                                                    in_=bank.t[:].rearrange("p (j n) -> p n j", j=8),
                                                    func=AF.Gelu_apprx_tanh), reads=[bank], writes=[ytmp])
            P.pool(lambda e, ct=ct: e.tensor_tensor(out=ycat.t[:, ct, :], in0=ytmp.t[:], in1=sg.t[:, ct, :],
                                                    op=ALU.mult), reads=[ytmp, sg], writes=[ycat])
        for ct in range(4):
            bank = pb[1 + (ct % 2)]
            for cc in range(4):
                for hh in range(2):
                    h = 2 * ct + hh
                    osl = bank.t[hh * 64:(hh + 1) * 64, cc * 128:(cc + 1) * 128]
                    P.pe(lambda e, cc=cc, h=h, osl=osl: e.matmul(osl, lhsT=vtok.t[:, cc, h * 64:(h + 1) * 64],
                                                                 rhs=WsT.t[:, h, :], start=True, stop=False),
                         reads=[vtok, WsT], writes=[bank])
                    P.pe(lambda e, h=h, osl=osl: e.matmul(osl, lhsT=onesb.t[0:64, 0:64],
                                                          rhs=bs2.t[0:64, h * 128:(h + 1) * 128], start=False,
                                                          stop=True), reads=[bs2, onesb], writes=[bank])
            P.dve(lambda e, ct=ct, bank=bank: e.tensor_tensor(out=ycat.t[:, 4 + ct, :], in0=bank.t[:],
                                                              in1=ub.t[:, ct, :], op=ALU.mult),
                  reads=[bank, ub], writes=[ycat])

        def mm(o, bank):
            for k in range(KT):
                P.pe(lambda e, o=o, k=k, bank=bank: e.matmul(bank.t[:], lhsT=w_out.t[:, k, o * 128:(o + 1) * 128],
                                                             rhs=ycat.t[:, k, :], start=(k == 0), stop=(k == KT - 1)),
                     reads=[w_out, ycat], writes=[bank])
        c["postnorm_res"](xt, ybuf, sq, rstd, gpost, l, mm)
        c["store_x"](xt, ti)
```

```python
import contextlib
import numpy as np
import concourse.bass as bass
import concourse.mybir as mybir
from concourse.bass_utils import run_bass_kernel_spmd

F32 = mybir.dt.float32
BF16 = mybir.dt.bfloat16
AF = mybir.ActivationFunctionType
ALU = mybir.AluOpType

ENGS = ("pe", "dve", "act", "pool", "sp")
SAME_ENGINE_SYNC = True
RELAX_WAR = True


class Res:
    def __init__(self, name):
        self.name = name
        self.last_w = None
        self.readers = {}
        self._subs = {}

    def sub(self, k):
        r = self._subs.get(k)
        if r is None:
            r = Res(f"{self.name}.{k}")
            self._subs[k] = r
        return r


class Buf(Res):
    def __init__(self, name, t):
        super().__init__(name)
        self.t = t


class Prog:
    def __init__(self, nc):
        self.nc = nc
        self.stack = contextlib.ExitStack()
        self.q = {e: [] for e in ENGS}
        self.ecnt = {e: 0 for e in ENGS}
        self.seen = {e: {} for e in ENGS}
        self.dmacnt = {}
        self.nres = 0

    def sbuf(self, name, shape, dtype):
        t = self.stack.enter_context(self.nc.sbuf_tensor(name, list(shape), dtype))
        return Buf(name, t)

    def psum(self, name, shape, dtype):
        t = self.stack.enter_context(self.nc.psum_tensor(name, list(shape), dtype))
        return Buf(name, t)

    def res(self, name):
        return Res(name)

    def _record(self, eng, fn, reads, writes, dmakey=None, inc=1):
        reads = list(reads)
        writes = list(writes)
        for r in list(reads):
            reads.extend(getattr(r, "also", ()))
        for w in list(writes):
            writes.extend(getattr(w, "also", ()))
        waits = {}
        seen = self.seen[eng]

        def need(ev, raw=True):
            if ev is None:
                return
            key, val, src = ev
            if src == eng and (eng == "pe" or not SAME_ENGINE_SYNC or (not raw and RELAX_WAR)):
                return
            if seen.get(key, 0) >= val:
                return
            if waits.get(key, 0) < val:
                waits[key] = val

        for r in reads:
            need(r.last_w)
        for w in writes:
            lw = w.last_w
            if not (dmakey is not None and lw is not None and lw[0] == ("dma", dmakey)):
                need(lw, raw=False)
            for k, (v, s) in w.readers.items():
                need((k, v, s), raw=False)
        for k, v in waits.items():
            seen[k] = v
        if dmakey is None:
            self.ecnt[eng] += 1
            ev = (("eng", eng), self.ecnt[eng], eng)
        else:
            self.dmacnt[dmakey] = self.dmacnt.get(dmakey, 0) + inc
            ev = (("dma", dmakey), self.dmacnt[dmakey], None)
        self.q[eng].append((fn, list(waits.items()), ev, inc if dmakey is not None else 1))
        for r in reads:
            old = r.readers.get(ev[0])
            if old is None or old[0] < ev[1]:
                r.readers[ev[0]] = (ev[1], ev[2])
        for w in writes:
            w.last_w = ev
            w.readers = {}
        return ev

    def pe(self, fn, reads=(), writes=()):
        return self._record("pe", fn, reads, writes)

    def dve(self, fn, reads=(), writes=()):
        return self._record("dve", fn, reads, writes)

    def act(self, fn, reads=(), writes=()):
        return self._record("act", fn, reads, writes)

    def pool(self, fn, reads=(), writes=()):
        return self._record("pool", fn, reads, writes)

    def eng(self, e, fn, reads=(), writes=()):
        return self._record(e, fn, reads, writes)

    def dma(self, eng, out, in_, reads=(), writes=(), key=None, **kw):
        key = key or writes[0].name
        return self._record(eng, lambda e: e.dma_start(out=out, in_=in_, **kw), reads, writes, dmakey=key, inc=16)

    def collective(self, fn, reads=(), writes=(), key=None):
        key = key or writes[0].name
        return self._record("pool", fn, reads, writes, dmakey=key, inc=16)

    def finish(self, finals):
        nc = self.nc
        waits = {}
        for r in finals:
            key, val, _ = r.last_w
            waits[key] = max(waits.get(key, 0), val)
        self.q["sp"].append((None, list(waits.items()), None, 0))
        keys = [("eng", e) for e in ENGS if self.ecnt[e] > 0] + [("dma", k) for k in self.dmacnt]
        sems = {}
        for i, k in enumerate(keys):
            sems[k] = self.stack.enter_context(nc.semaphore(f"s{i}_{k[1]}"[:24].replace(".", "_")))
        self.nsems = len(keys)
        q = self.q

        def replay(ename, e):
            for fn, ws, ev, inc in q[ename]:
                for k, v in ws:
                    e.wait_ge(sems[k], v)
                if fn is None:
                    continue
                ins = fn(e)
                ins.then_inc(sems[ev[0]], inc)

        with nc.Block() as block:
            @block.tensor
            def _(e):
                replay("pe", e)

            @block.vector
            def _(e):
                replay("dve", e)

            @block.scalar
            def _(e):
                replay("act", e)

            @block.gpsimd
            def _(e):
                replay("pool", e)

            @block.sync
            def _(e):
                replay("sp", e)
        self.stack.close()

    def barrier(self):
        for e in ENGS:
            waits = {}
            for e2 in ENGS:
                if e2 != e and self.ecnt[e2] > self.seen[e].get(("eng", e2), 0):
                    waits[("eng", e2)] = self.ecnt[e2]
            if e != "pe" and self.ecnt[e] > self.seen[e].get(("eng", e), 0):
                waits[("eng", e)] = self.ecnt[e]
            for k, v in self.dmacnt.items():
                if v > self.seen[e].get(("dma", k), 0):
                    waits[("dma", k)] = v
            for k, v in waits.items():
                self.seen[e][k] = v
            if waits:
                self.q[e].append((None, list(waits.items()), None, 0))

    @staticmethod
    def alias(a, b):
        a.also = list(getattr(a, "also", [])) + [b]
        b.also = list(getattr(b, "also", [])) + [a]

    def carve(self, name, arena, off_bytes, shape, dtype):
        esz = 4 if dtype == F32 else 2
        n = 1
        for s in shape[1:]:
            n *= s
        nb = n * esz
        assert off_bytes % 4 == 0 and nb % 4 == 0
        a = arena.t[:, off_bytes // 4:(off_bytes + nb) // 4]
        if dtype != F32:
            a = a.bitcast(dtype)
        if len(shape) == 3:
            a = a.rearrange("p (a b) -> p a b", a=shape[1])
        elif len(shape) == 4:
            a = a.rearrange("p (a b c) -> p a b c", a=shape[1], b=shape[2])
        b = Buf(name, a)
        b.nbytes = nb
        return b


D = 1024
KT = 8
NT = 512
PI = 3.14159265358979
PNAMES = ["g_mix_pre", "w_in", "lam_re", "lam_im", "log_dt", "b_re", "b_im", "c_re", "c_im", "d_skip", "g_v",
          "w_s", "b_s", "w_out", "g_mix_post", "g_x_pre", "g_mem", "w_q", "w_k", "w_v", "w_o", "g_x_post",
          "g_ffn_pre", "w_up", "conv_w", "conv_b", "w_down", "g_ffn_post"]
PSHAPES = {"g_mix_pre": [4, 1024], "w_in": [4, 1024, 2048], "lam_re": [4, 32, 64], "lam_im": [4, 32, 64],
           "log_dt": [4, 32], "b_re": [4, 32, 64, 16], "b_im": [4, 32, 64, 16], "c_re": [4, 32, 16, 64],
           "c_im": [4, 32, 16, 64], "d_skip": [4, 32, 16], "g_v": [4, 512], "w_s": [4, 8, 128, 128],
           "b_s": [4, 8, 128], "w_out": [4, 1024, 1024], "g_mix_post": [4, 1024], "g_x_pre": [4, 1024],
           "g_mem": [4, 1024], "w_q": [4, 1024, 1024], "w_k": [4, 1024, 1024], "w_v": [4, 1024, 1024],
           "w_o": [4, 1024, 1024], "g_x_post": [4, 1024], "g_ffn_pre": [4, 1024], "w_up": [4, 1024, 5632],
           "conv_w": [4, 3, 5632], "conv_b": [4, 5632], "w_down": [4, 2816, 1024], "g_ffn_post": [4, 1024]}


class Arena:
    def __init__(self, P, buf, total):
        self.P, self.buf, self.total, self.off = P, buf, total, 0

    def alloc(self, name, shape, dtype):
        b = self.P.carve(name, self.buf, self.off, shape, dtype)
        self.off += (b.nbytes + 31) // 32 * 32
        assert self.off <= self.total, (name, self.off, self.total)
        return b

    def reset(self):
        self.P.barrier()
        self.off = 0


def build_program(S=8192, DEPTH=4, STAGES=("mix", "xat", "ffn")):
    nc = bass.Bass("TRN2", target_bir_lowering=False)
    NTILES = S // NT
    x_d = nc.dram_tensor("x", [S, D], F32, kind="ExternalInput").ap()
    mem_d = nc.dram_tensor("mem", [256, D], F32, kind="ExternalInput").ap()
    pd = {n: nc.dram_tensor(n, PSHAPES[n], F32, kind="ExternalInput").ap() for n in PNAMES}
    out_d = nc.dram_tensor("out", [S, D], F32, kind="ExternalOutput").ap()
    xs_d = nc.dram_tensor("xs", [D, S], F32).ap()
    xs_v = xs_d.rearrange("(k p) t -> p k t", p=128)
    P = Prog(nc)
    xs_r = [P.res(f"xs{t}") for t in range(NTILES)]
    out_rs = [P.res("out_a"), P.res("out_b")]

    ident = P.sbuf("ident", [128, 128], F32)
    identb = P.sbuf("identb", [128, 128], BF16)
    onesb = P.sbuf("onesb", [128, 128], BF16)
    blkones = P.sbuf("blkones", [128, 128], BF16)
    maskJ = P.sbuf("maskJ", [128, 128], F32)
    tril = P.sbuf("tril", [128, 128], F32)
    cb = P.sbuf("cbias", [128, 4], F32)
    gv = {n: P.sbuf("gv_" + n, [128, 4, 8], F32) for n in
          ["g_mix_pre", "g_mix_post", "g_x_pre", "g_mem", "g_x_post", "g_ffn_pre", "g_ffn_post"]}
    gvv = P.sbuf("gv_g_v", [128, 4, 4], F32)
    memhT = P.sbuf("memhT", [128, 8, 256], F32)
    ARENA_BYTES = 196 * 1024
    arena_buf = P.sbuf("arena", [128, ARENA_BYTES // 4], F32)
    A = Arena(P, arena_buf, ARENA_BYTES)
    pst = P.psum("pst", [128, 8 * 512], F32)
    pb = []
    for b in range(8):
        r = Buf(f"pb{b}", pst.t[:, b * 512:(b + 1) * 512])
        pb.append(r)

    def cst(e, ap, v, wr):
        P.eng(e, lambda en: en.memset(ap, v), writes=wr)

    cst("pool", ident.t[:], 1.0, [ident])
    P.pool(lambda e: e.affine_select(out=ident.t[:], in_=ident.t[:], pattern=[[1, 128]], compare_op=ALU.is_equal,
                                     fill=0.0, base=0, channel_multiplier=-1), reads=[ident], writes=[ident])
    P.dve(lambda e: e.tensor_copy(out=identb.t[:], in_=ident.t[:]), reads=[ident], writes=[identb])
    cst("pool", onesb.t[:], 1.0, [onesb])
    cst("pool", blkones.t[:], 0.0, [blkones])
    cst("pool", blkones.t[0:64, 0:64], 1.0, [blkones])
    cst("pool", blkones.t[64:128, 64:128], 1.0, [blkones])
    cst("pool", maskJ.t[:], 1.0, [maskJ])
    P.pool(lambda e: e.affine_select(out=maskJ.t[:].rearrange("p (j c) -> p j c", j=8),
                                     in_=maskJ.t[:].rearrange("p (j c) -> p j c", j=8),
                                     pattern=[[16, 8], [0, 16]], compare_op=ALU.is_ge, fill=0.0, base=15,
                                     channel_multiplier=-1), reads=[maskJ], writes=[maskJ])
    cst("pool", tril.t[:], 1.0, [tril])
    P.pool(lambda e: e.affine_select(out=tril.t[:], in_=tril.t[:], pattern=[[1, 128]], compare_op=ALU.is_ge,
                                     fill=0.0, base=0, channel_multiplier=-1), reads=[tril], writes=[tril])
    cst("pool", cb.t[:, 0:1], PI / 2, [cb])
    cst("pool", cb.t[:, 1:2], 1e-6, [cb])
    cst("pool", cb.t[:, 2:3], 0.0, [cb])
    with nc.allow_non_contiguous_dma(reason="small param loads"):
        pass
    import os
    KDBG = os.environ.get("KDBG", "")
    for n, t in (gv.items() if "nogv" not in KDBG else []):
        P.dma("sp", t.t[:], pd[n].rearrange("l (k p) -> p l k", p=128), writes=[t], allow_slow_non_contiguous=True)
    if "nogv" not in KDBG:
      P.dma("sp", gvv.t[:], pd["g_v"].rearrange("l (k p) -> p l k", p=128), writes=[gvv], allow_slow_non_contiguous=True)

    def rstd_from_ss(ss_ps, rstd, n_feat, rd):
        P.act(lambda e: e.activation(out=rstd.t[:], in_=ss_ps, func=AF.Sqrt, bias=cb.t[:, 1:2], scale=1.0 / n_feat),
              reads=rd + [cb], writes=[rstd])
        P.dve(lambda e: e.reciprocal(out=rstd.t[:], in_=rstd.t[:]), reads=[rstd], writes=[rstd])

    xs_all = [[xs_r[t]] + [xs_r[t].sub(o) for o in range(KT)] for t in range(NTILES)]

    def load_x(xt, ti):
        P.dma("sp", xt.t[:], xs_v[:, :, ti * NT:(ti + 1) * NT], reads=xs_all[ti], writes=[xt])

    def store_x(xt, ti):
        P.dma("sp", xs_v[:, :, ti * NT:(ti + 1) * NT], xt.t[:], reads=[xt], writes=xs_all[ti])

    def prenorm(xt, ht, sq, rstd, gname, l):
        P.act(lambda e: e.activation(out=sq.t[:], in_=xt.t[:], func=AF.Square), reads=[xt], writes=[sq])
        for k in range(KT):
            P.pe(lambda e, k=k: e.matmul(pb[0].t[:], lhsT=onesb.t[:], rhs=sq.t[:, k, :], start=(k == 0),
                                         stop=(k == KT - 1)), reads=[sq, onesb], writes=[pb[0]])
        rstd_from_ss(pb[0].t[:], rstd, D, [pb[0]])
        g = gv[gname]
        for k in range(KT):
            P.dve(lambda e, k=k: e.scalar_tensor_tensor(out=ht.t[:, k, :], in0=xt.t[:, k, :], scalar=g.t[:, l, k:k + 1],
                                                        in1=rstd.t[:], op0=ALU.mult, op1=ALU.mult),
                  reads=[xt, g, rstd], writes=[ht])

    def postnorm_res(xt, ybuf, sq, rstd, gname, l, mm_fn):
        for o in range(KT):
            bank = pb[1 + (o % 2)]
            mm_fn(o, bank)
            P.act(lambda e, o=o, bank=bank: e.activation(out=ybuf.t[:, o, :], in_=bank.t[:], func=AF.Copy),
                  reads=[bank], writes=[ybuf])
            P.dve(lambda e, o=o, bank=bank: e.tensor_tensor(out=sq.t[:, o, :], in0=bank.t[:], in1=ybuf.t[:, o, :],
                                                            op=ALU.mult), reads=[bank, ybuf], writes=[sq])
        for k in range(KT):
            P.pe(lambda e, k=k: e.matmul(pb[0].t[:], lhsT=onesb.t[:], rhs=sq.t[:, k, :], start=(k == 0),
                                         stop=(k == KT - 1)), reads=[sq, onesb], writes=[pb[0]])
        rstd_from_ss(pb[0].t[:], rstd, D, [pb[0]])
        g = gv[gname]
        for o in range(KT):
            P.dve(lambda e, o=o: e.scalar_tensor_tensor(out=ybuf.t[:, o, :], in0=ybuf.t[:, o, :],
                                                        scalar=g.t[:, l, o:o + 1], in1=rstd.t[:], op0=ALU.mult,
                                                        op1=ALU.mult), reads=[ybuf, g, rstd], writes=[ybuf])
            P.pool(lambda e, o=o: e.tensor_tensor(out=xt.t[:, o, :], in0=xt.t[:, o, :], in1=ybuf.t[:, o, :],
                                                  op=ALU.add), reads=[xt, ybuf], writes=[xt])

    def load_w(dst, src_ap, ncols, eng="pool"):
        c0 = 0
        while c0 < ncols:
            c1 = min(ncols, c0 + 2048)
            P.dma(eng, dst.t[:, :, c0:c1], src_ap[:, :, c0:c1], writes=[dst])
            c0 = c1

    xtok_b = [A.alloc("xtok_a", [128, 4, D], F32), A.alloc("xtok_b", [128, 4, D], F32)]
    xt0_b = [A.alloc("xt0_a", [128, KT, NT], F32), A.alloc("xt0_b", [128, KT, NT], F32)]
    for ti in range(NTILES):
        xtok = xtok_b[ti % 2]
        xt0 = xt0_b[ti % 2]
        P.dma("sp", xtok.t[:], x_d[ti * NT:(ti + 1) * NT, :].rearrange("(s p) d -> p s d", p=128), writes=[xtok])
        for k in range(KT):
            bank = pb[1 + (k % 4)]
            for s in range(4):
                P.pe(lambda e, k=k, s=s, bank=bank, xtok=xtok: e.transpose(out=bank.t[:, s * 128:(s + 1) * 128],
                                                                in_=xtok.t[:, s, k * 128:(k + 1) * 128],
                                                                identity=ident.t[:]),
                     reads=[xtok, ident], writes=[bank])
            if k % 2 == 0:
                P.act(lambda e, k=k, bank=bank, xt0=xt0: e.activation(out=xt0.t[:, k, :], in_=bank.t[:], func=AF.Copy),
                      reads=[bank], writes=[xt0])
            else:
                P.dve(lambda e, k=k, bank=bank, xt0=xt0: e.tensor_copy(out=xt0.t[:, k, :], in_=bank.t[:]),
                      reads=[bank], writes=[xt0])
        store_x(xt0, ti)
    if "nomem" in KDBG:
        return_early_mem = True
    memt = A.alloc("memt", [128, 2, D], F32)
    msq = A.alloc("msq", [128, D], F32)
    mss = A.alloc("mss", [128, 2], F32)
    P.dma("sp", memt.t[:], mem_d.rearrange("(s p) d -> p s d", p=128), writes=[memt])
    for s in range(2):
        P.act(lambda e, s=s: e.activation(out=msq.t[:], in_=memt.t[:, s, :], func=AF.Square,
                                          accum_out=mss.t[:, s:s + 1]), reads=[memt], writes=[msq, mss])
    P.act(lambda e: e.activation(out=mss.t[:], in_=mss.t[:], func=AF.Sqrt, bias=cb.t[:, 1:2], scale=1.0 / D),
          reads=[mss, cb], writes=[mss])
    P.dve(lambda e: e.reciprocal(out=mss.t[:], in_=mss.t[:]), reads=[mss], writes=[mss])
    for s in range(2):
        P.dve(lambda e, s=s: e.tensor_scalar(out=memt.t[:, s, :], in0=memt.t[:, s, :], scalar1=mss.t[:, s:s + 1],
                                             scalar2=None, op0=ALU.mult), reads=[memt, mss], writes=[memt])
    for k in range(KT):
        bank = pb[1 + (k % 4)]
        for s in range(2):
            P.pe(lambda e, k=k, s=s, bank=bank: e.transpose(out=bank.t[:, s * 128:(s + 1) * 128],
                                                            in_=memt.t[:, s, k * 128:(k + 1) * 128],
                                                            identity=ident.t[:]), reads=[memt, ident], writes=[bank])
        P.act(lambda e, k=k, bank=bank: e.activation(out=memhT.t[:, k, :], in_=bank.t[:, 0:256], func=AF.Copy),
              reads=[bank], writes=[memhT])

    ctx = dict(nc=nc, P=P, A=A, pb=pb, pd=pd, gv=gv, gvv=gvv, cb=cb, ident=ident, identb=identb, onesb=onesb,
               blkones=blkones, maskJ=maskJ, tril=tril, memhT=memhT, load_x=load_x, store_x=store_x,
               prenorm=prenorm, postnorm_res=postnorm_res, load_w=load_w, rstd_from_ss=rstd_from_ss,
               NTILES=NTILES, xs_r=xs_r, xs_v=xs_v, pst=pst)
    for l in range(DEPTH):
        if "mix" in STAGES:
            A.reset()
            mixer_layer(ctx, l)
        if "xat" in STAGES:
            A.reset()
            xattn_layer(ctx, l)
        if "ffn" in STAGES:
            A.reset()
            ffn_layer(ctx, l)

    A.reset()
    xt0e_b = [A.alloc("xt0e_a", [128, KT, NT], F32), A.alloc("xt0e_b", [128, KT, NT], F32)]
    xtoke_b = [A.alloc("xtoke_a", [128, 4, D], F32), A.alloc("xtoke_b", [128, 4, D], F32)]
    for ti in range(NTILES):
        xt0e = xt0e_b[ti % 2]
        xtoke = xtoke_b[ti % 2]
        load_x(xt0e, ti)
        for s in range(4):
            for kh in range(2):
                bank = pb[1 + ((2 * s + kh) % 4)]
                for kk in range(4):
                    k = kh * 4 + kk
                    P.pe(lambda e, k=k, kk=kk, s=s, bank=bank, xt0e=xt0e: e.transpose(
                        out=bank.t[:, kk * 128:(kk + 1) * 128], in_=xt0e.t[:, k, s * 128:(s + 1) * 128],
                        identity=ident.t[:]), reads=[xt0e, ident], writes=[bank])
                if kh == 0:
                    P.act(lambda e, s=s, kh=kh, bank=bank, xtoke=xtoke: e.activation(out=xtoke.t[:, s, kh * 512:(kh + 1) * 512],
                                                                        in_=bank.t[:], func=AF.Copy),
                          reads=[bank], writes=[xtoke])
                else:
                    P.dve(lambda e, s=s, kh=kh, bank=bank, xtoke=xtoke: e.tensor_copy(out=xtoke.t[:, s, kh * 512:(kh + 1) * 512],
                                                                         in_=bank.t[:]), reads=[bank], writes=[xtoke])
        P.dma("sp", out_d[ti * NT:(ti + 1) * NT, :].rearrange("(s p) d -> p s d", p=128), xtoke.t[:],
              reads=[xtoke], writes=[out_rs[ti % 2]], key="out%d" % (ti % 2))
    P.finish([r for r in out_rs if r.last_w is not None])
    return nc


def xattn_layer(c, l):
    P, A, pb, pd = c["P"], c["A"], c["pb"], c["pd"]
    onesb, gv, memhT = c["onesb"], c["gv"], c["memhT"]
    wv3 = lambda n: pd[n][l].rearrange("(k p) n -> p k n", p=128)
    wq = A.alloc("wq", [128, KT, D], BF16)
    wo = A.alloc("wo", [128, KT, D], BF16)
    wk = A.alloc("wk", [128, KT, D], BF16)
    wvv = A.alloc("wv", [128, KT, D], BF16)
    for w, n in ((wk, "w_k"), (wvv, "w_v"), (wq, "w_q"), (wo, "w_o")):
        c["load_w"](w, wv3(n), D)
    memn = A.alloc("memn", [128, KT, 256], BF16)
    kT = A.alloc("kT", [128, KT, 256], BF16)
    vt = A.alloc("vt", [128, 2, D], BF16)
    xts = [A.alloc("xt_0", [128, KT, NT], F32), A.alloc("xt_1", [128, KT, NT], F32)]
    hts = [A.alloc("ht_0", [128, KT, NT], BF16), A.alloc("ht_1", [128, KT, NT], BF16)]
    sq_pre = A.alloc("sq_pre", [128, KT, NT], BF16)
    rstd_pre = A.alloc("rstd_pre", [128, NT], F32)
    sq = A.alloc("sq", [128, KT, NT], BF16)
    rstd = A.alloc("rstd", [128, NT], F32)
    qT = A.alloc("qT", [128, KT, NT], BF16)
    expTs = [A.alloc("expT0", [128, 2, NT], BF16), A.alloc("expT1", [128, 2, NT], BF16)]
    rdens = [A.alloc("rden0", [128, NT], F32), A.alloc("rden1", [128, NT], F32)]
    oT = A.alloc("oT", [128, KT, NT], BF16)
    ybuf = A.alloc("ybuf", [128, KT, NT], F32)
    gm = gv["g_mem"]
    for k in range(KT):
        P.dve(lambda e, k=k: e.tensor_scalar(out=memn.t[:, k, :], in0=memhT.t[:, k, :], scalar1=gm.t[:, l, k:k + 1],
                                             scalar2=None, op0=ALU.mult), reads=[memhT, gm], writes=[memn])
    for o in range(KT):
        bank = pb[3 + (o % 2)]
        for k in range(KT):
            P.pe(lambda e, o=o, k=k, bank=bank: e.matmul(bank.t[:, 0:256], lhsT=wk.t[:, k, o * 128:(o + 1) * 128],
                                                         rhs=memn.t[:, k, :], start=(k == 0), stop=(k == KT - 1)),
                 reads=[wk, memn], writes=[bank])
        P.act(lambda e, o=o, bank=bank: e.activation(out=kT.t[:, o, :], in_=bank.t[:, 0:256], func=AF.Copy),
              reads=[bank], writes=[kT])
    for mt in range(2):
        for hf in range(2):
            bank = pb[3 + (hf % 2)]
            for k in range(KT):
                P.pe(lambda e, mt=mt, hf=hf, k=k, bank=bank: e.matmul(
                    bank.t[:], lhsT=memn.t[:, k, mt * 128:(mt + 1) * 128], rhs=wvv.t[:, k, hf * 512:(hf + 1) * 512],
                    start=(k == 0), stop=(k == KT - 1)), reads=[wvv, memn], writes=[bank])
            P.act(lambda e, mt=mt, hf=hf, bank=bank: e.activation(out=vt.t[:, mt, hf * 512:(hf + 1) * 512],
                                                                  in_=bank.t[:], func=AF.Copy),
                  reads=[bank], writes=[vt])
    NTL = c["NTILES"]

    def head_load(ti):
        c["load_x"](xts[ti % 2], ti)

    def head_norm(ti):
        c["prenorm"](xts[ti % 2], hts[ti % 2], sq_pre, rstd_pre, "g_x_pre", l)

    def qproj(ti):
        ht = hts[ti % 2]
        for o in range(KT):
            bank = pb[3 + (o % 2)]
            for k in range(KT):
                P.pe(lambda e, o=o, k=k, bank=bank, ht=ht: e.matmul(bank.t[:], lhsT=wq.t[:, k, o * 128:(o + 1) * 128],
                                                                    rhs=ht.t[:, k, :], start=(k == 0),
                                                                    stop=(k == KT - 1)), reads=[wq, ht], writes=[bank])
            P.act(lambda e, o=o, bank=bank: e.activation(out=qT.t[:, o, :], in_=bank.t[:], func=AF.Copy),
                  reads=[bank], writes=[qT])

    def heads(hs):
        for h in hs:
            expT = expTs[h % 2]
            rden = rdens[h % 2]
            for mt in range(2):
                bank = pb[3 + (mt % 2)]
                for dd in range(2):
                    P.pe(lambda e, h=h, mt=mt, dd=dd, bank=bank: e.matmul(
                        bank.t[:], lhsT=kT.t[:, 2 * h + dd, mt * 128:(mt + 1) * 128], rhs=qT.t[:, 2 * h + dd, :],
                        start=(dd == 0), stop=(dd == 1)), reads=[kT, qT], writes=[bank])
                P.act(lambda e, mt=mt, bank=bank, expT=expT: e.activation(out=expT.t[:, mt, :], in_=bank.t[:], func=AF.Exp,
                                                               scale=1.0 / 16.0), reads=[bank], writes=[expT])
            for mt in range(2):
                P.pe(lambda e, mt=mt, expT=expT: e.matmul(pb[5].t[:], lhsT=onesb.t[:], rhs=expT.t[:, mt, :], start=(mt == 0),
                                               stop=(mt == 1)), reads=[expT, onesb], writes=[pb[5]])
            P.dve(lambda e, rden=rden: e.reciprocal(out=rden.t[:], in_=pb[5].t[:]), reads=[pb[5]], writes=[rden])
            for dd in range(2):
                bank = pb[6 + dd]
                for mt in range(2):
                    P.pe(lambda e, h=h, mt=mt, dd=dd, bank=bank, expT=expT: e.matmul(
                        bank.t[:], lhsT=vt.t[:, mt, (2 * h + dd) * 128:(2 * h + dd + 1) * 128], rhs=expT.t[:, mt, :],
                        start=(mt == 0), stop=(mt == 1)), reads=[vt, expT], writes=[bank])
                P.dve(lambda e, h=h, dd=dd, bank=bank, rden=rden: e.tensor_tensor(out=oT.t[:, 2 * h + dd, :], in0=bank.t[:],
                                                                       in1=rden.t[:], op=ALU.mult),
                      reads=[bank, rden], writes=[oT])

    def tail(ti):
        def mm(o, bank):
            for k in range(KT):
                P.pe(lambda e, o=o, k=k, bank=bank: e.matmul(bank.t[:], lhsT=wo.t[:, k, o * 128:(o + 1) * 128],
                                                             rhs=oT.t[:, k, :], start=(k == 0), stop=(k == KT - 1)),
                     reads=[wo, oT], writes=[bank])
        c["postnorm_res"](xts[ti % 2], ybuf, sq, rstd, "g_x_post", l, mm)
        c["store_x"](xts[ti % 2], ti)

    head_load(0)
    head_norm(0)
    for ti in range(NTL):
        if ti + 1 < NTL:
            head_load(ti + 1)
        qproj(ti)
        heads([0, 1])
        if ti + 1 < NTL:
            head_norm(ti + 1)
        heads([2, 3])
        tail(ti)


def ffn_layer(c, l):
    P, A, pb, pd = c["P"], c["A"], c["pb"], c["pd"]
    onesb, gv, cb = c["onesb"], c["gv"], c["cb"]
    NF = 22
    wup = A.alloc("wup", [128, KT, 5632], BF16)
    wdn = A.alloc("wdn", [128, NF, D], BF16)
    c["load_w"](wup, pd["w_up"][l].rearrange("(k p) n -> p k n", p=128), 5632)
    c["load_w"](wdn, pd["w_down"][l].rearrange("(f p) n -> p f n", p=128), D)
    cw = A.alloc("cw", [128, 3, 44], F32)
    cbv = A.alloc("cbv", [128, 44], F32)
    zl = A.alloc("zl", [128, 44, 2], F32)
    P.dma("sp", cw.t[:], pd["conv_w"][l].rearrange("w (f p) -> p w f", p=128), writes=[cw],
          allow_slow_non_contiguous=True)
    P.dma("sp", cbv.t[:], pd["conv_b"][l].rearrange("(f p) -> p f", p=128), writes=[cbv],
          allow_slow_non_contiguous=True)
    zls = [zl.sub(i) for i in range(44)]
    P.pool(lambda e: e.memset(zl.t[:], 0.0), writes=zls)
    bh = A.alloc("bh", [128, 44, 2], F32)
    hh = A.alloc("hh", [128, 2, 44], F32)
    off_xt = A.off
    xt = A.alloc("xt", [128, KT, NT], F32)
    ybuf = P.carve("ybuf_f", A.buf, off_xt, [128, KT, NT], F32)
    off_ht = A.off
    ht = A.alloc("ht", [128, KT, NT], BF16)
    sq2 = P.carve("sq2_f", A.buf, off_ht, [128, KT, NT], BF16)
    rstd = A.alloc("rstd", [128, NT], F32)
    off_g = A.off
    gbuf = A.alloc("gbuf", [128, NF, NT], BF16)
    sq = P.carve("sq_f", A.buf, off_g, [128, KT, NT], BF16)
    acc = [[A.alloc("accv0", [128, NT], F32), A.alloc("accg0", [128, NT], F32)],
           [A.alloc("accv1", [128, NT], F32), A.alloc("accg1", [128, NT], F32)]]
    xo = [A.alloc("xo0", [128, NT], F32), A.alloc("xo1", [128, NT], F32)]
    xs_r = c["xs_r"]
    xs_v = c["xs_v"]
    for ti in range(c["NTILES"]):
        c["load_x"](xt, ti)
        P.pool(lambda e: e.tensor_tensor(out=hh.t[:, 0, :], in0=cw.t[:, 0, :], in1=zl.t[:, :, 0], op=ALU.mult),
               reads=[cw] + zls, writes=[hh])
        P.pool(lambda e: e.tensor_tensor(out=hh.t[:, 1, :], in0=cw.t[:, 1, :], in1=zl.t[:, :, 1], op=ALU.mult),
               reads=[cw] + zls, writes=[hh])
        P.pool(lambda e: e.tensor_tensor(out=hh.t[:, 0, :], in0=hh.t[:, 0, :], in1=hh.t[:, 1, :], op=ALU.add),
               reads=[hh], writes=[hh])
        P.pool(lambda e: e.tensor_tensor(out=bh.t[:, :, 0], in0=hh.t[:, 0, :], in1=cbv.t[:], op=ALU.add),
               reads=[hh, cbv], writes=[bh])
        P.pool(lambda e: e.tensor_tensor(out=hh.t[:, 1, :], in0=cw.t[:, 0, :], in1=zl.t[:, :, 1], op=ALU.mult),
               reads=[cw] + zls, writes=[hh])
        P.pool(lambda e: e.tensor_tensor(out=bh.t[:, :, 1], in0=hh.t[:, 1, :], in1=cbv.t[:], op=ALU.add),
               reads=[hh, cbv], writes=[bh])
        c["prenorm"](xt, ht, sq, rstd, "g_ffn_pre", l)
        for f in range(NF):
            a2 = acc[f % 2]
            bks = (pb[3], pb[4]) if f % 2 == 0 else (pb[5], pb[6])
            for vg in range(2):
                ci = vg * NF + f
                bank = bks[vg]
                for k in range(KT):
                    P.pe(lambda e, ci=ci, k=k, bank=bank: e.matmul(bank.t[:], lhsT=wup.t[:, k, ci * 128:(ci + 1) * 128],
                                                                   rhs=ht.t[:, k, :], start=(k == 0),
                                                                   stop=(k == KT - 1)), reads=[wup, ht], writes=[bank])
            for vg in range(2):
                ci = vg * NF + f
                bank = bks[vg]
                a = a2[vg]
                P.act(lambda e, ci=ci, bank=bank, a=a: e.activation(out=a.t[:, 2:NT], in_=bank.t[:, 2:NT],
                                                                    func=AF.Identity, bias=cbv.t[:, ci:ci + 1],
                                                                    scale=cw.t[:, 2, ci:ci + 1]),
                      reads=[bank, cbv, cw], writes=[a])
                for col in range(2):
                    P.act(lambda e, ci=ci, bank=bank, a=a, col=col: e.activation(
                        out=a.t[:, col:col + 1], in_=bank.t[:, col:col + 1], func=AF.Identity,
                        bias=bh.t[:, ci, col:col + 1], scale=cw.t[:, 2, ci:ci + 1]), reads=[bank, bh, cw], writes=[a])
                P.act(lambda e, ci=ci, bank=bank: e.activation(out=zl.t[:, ci, :], in_=bank.t[:, NT - 2:NT],
                                                               func=AF.Copy), reads=[bank], writes=[zls[ci]])
            for vg in range(2):
                ci = vg * NF + f
                bank = bks[vg]
                a = a2[vg]
                P.dve(lambda e, ci=ci, bank=bank, a=a: e.scalar_tensor_tensor(
                    out=a.t[:, 1:NT], in0=bank.t[:, 0:NT - 1], scalar=cw.t[:, 1, ci:ci + 1], in1=a.t[:, 1:NT],
                    op0=ALU.mult, op1=ALU.add), reads=[bank, cw, a], writes=[a])
                P.dve(lambda e, ci=ci, bank=bank, a=a: e.scalar_tensor_tensor(
                    out=a.t[:, 2:NT], in0=bank.t[:, 0:NT - 2], scalar=cw.t[:, 0, ci:ci + 1], in1=a.t[:, 2:NT],
                    op0=ALU.mult, op1=ALU.add), reads=[bank, cw, a], writes=[a])
            for fp in ([f - 1] if f > 0 else []) + ([f] if f == NF - 1 else []):
                ap2 = acc[fp % 2]
                P.act(lambda e, ap2=ap2: e.activation(out=ap2[1].t[:], in_=ap2[1].t[:], func=AF.Gelu_apprx_tanh),
                      reads=[ap2[1]], writes=[ap2[1]])
                P.pool(lambda e, fp=fp, ap2=ap2: e.tensor_tensor(out=gbuf.t[:, fp, :], in0=ap2[0].t[:], in1=ap2[1].t[:],
                                                                 op=ALU.mult), reads=[ap2[0], ap2[1]],
                       writes=[gbuf.sub(fp)])

        def mm(o, bank):
            for f in range(NF):
                P.pe(lambda e, o=o, f=f, bank=bank: e.matmul(bank.t[:], lhsT=wdn.t[:, f, o * 128:(o + 1) * 128],
                                                             rhs=gbuf.t[:, f, :], start=(f == 0), stop=(f == NF - 1)),
                     reads=[wdn, gbuf.sub(f)], writes=[bank])
        g = gv["g_ffn_post"]
        for o in range(KT):
            bank = pb[1 + (o % 2)]
            mm(o, bank)
            P.act(lambda e, o=o, bank=bank: e.activation(out=ybuf.t[:, o, :], in_=bank.t[:], func=AF.Copy),
                  reads=[bank], writes=[ybuf, xt])
            P.dve(lambda e, o=o, bank=bank: e.tensor_tensor(out=sq2.t[:, o, :], in0=bank.t[:], in1=ybuf.t[:, o, :],
                                                            op=ALU.mult), reads=[bank, ybuf], writes=[sq2, ht])
        for k in range(KT):
            P.pe(lambda e, k=k: e.matmul(pb[0].t[:], lhsT=onesb.t[:], rhs=sq2.t[:, k, :], start=(k == 0),
                                         stop=(k == KT - 1)), reads=[sq2, onesb], writes=[pb[0]])
        c["rstd_from_ss"](pb[0].t[:], rstd, D, [pb[0]])
        xbufs = [acc[0][0], acc[0][1], acc[1][0], acc[1][1], xo[0], xo[1]]
        for o in range(KT):
            xb = xbufs[o % 6]
            P.dma("sp", xb.t[:], xs_v[:, o, ti * NT:(ti + 1) * NT], reads=[xs_r[ti].sub(o)], writes=[xb],
                  key="xoload%d" % (o % 6))
            P.dve(lambda e, o=o: e.scalar_tensor_tensor(out=ybuf.t[:, o, :], in0=ybuf.t[:, o, :],
                                                        scalar=g.t[:, l, o:o + 1], in1=rstd.t[:], op0=ALU.mult,
                                                        op1=ALU.mult), reads=[ybuf, g, rstd], writes=[ybuf, xt])
            P.pool(lambda e, o=o, xb=xb: e.tensor_tensor(out=xb.t[:], in0=xb.t[:], in1=ybuf.t[:, o, :], op=ALU.add),
                   reads=[xb, ybuf], writes=[xb])
            P.dma("sp", xs_v[:, o, ti * NT:(ti + 1) * NT], xb.t[:], reads=[xb], writes=[xs_r[ti].sub(o)],
                  key="xst%d" % (o % 6))


def _run(inputs, S=8192, DEPTH=4, STAGES=("mix", "xat", "ffn"), NCORES=8):
    nc = build_program(S=S, DEPTH=DEPTH, STAGES=STAGES)
    in_maps = []
    for c in range(NCORES):
        b = c % 2
        m = {"x": np.ascontiguousarray(inputs["x"][b, :S]), "mem": np.ascontiguousarray(inputs["mem"][b])}
        for n in PNAMES:
            m[n] = np.ascontiguousarray(inputs[n])
        in_maps.append(m)
    import os
    if os.environ.get("KTRACE"):
        res = run_bass_kernel_spmd(nc, in_maps, core_ids=list(range(NCORES)), trace=True)
        print("EXEC_TIME_NS", res.exec_time_ns, flush=True)
    else:
        res = run_bass_kernel_spmd(nc, in_maps, core_ids=list(range(NCORES)))
    return np.stack([res.results[0]["out"], res.results[1 % NCORES]["out"]], axis=0)


def kernel(**inputs):
    inputs = {k: np.asarray(v) for k, v in inputs.items()}
    return _run(inputs).astype(np.float32)


def mixer_layer(c, l):
    P, A, pb, pd = c["P"], c["A"], c["pb"], c["pd"]
    onesb, gv, gvv, cb = c["onesb"], c["gv"], c["gvv"], c["cb"]
    ident, identb, blkones, maskJ, tril = c["ident"], c["identb"], c["blkones"], c["maskJ"], c["tril"]
    pst = c["pst"]
    NG, NPR = 32, 16
    w_in = A.alloc("w_in", [128, KT, 2048], BF16)
    w_out = A.alloc("w_out", [128, KT, D], BF16)
    c["load_w"](w_in, pd["w_in"][l].rearrange("(k p) n -> p k n", p=128), 2048)
    c["load_w"](w_out, pd["w_out"][l].rearrange("(k p) n -> p k n", p=128), D)
    WDT = [A.alloc("WDTr", [128, NPR, 128], BF16), A.alloc("WDTi", [128, NPR, 128], BF16)]
    Wo = [A.alloc("Wor", [128, NPR, 128], BF16), A.alloc("Woi", [128, NPR, 128], BF16)]
    Kmat = A.alloc("Kmat", [128, NG, 128], BF16)
    C1 = A.alloc("C1", [128, NPR, 64], F32)
    S1 = A.alloc("S1", [128, NPR, 64], F32)
    Mtab = A.alloc("Mtab", [128, NPR, 64], F32)
    rho8 = A.alloc("rho8", [128, NPR], F32)
    Lf = [A.alloc("Lfr", [128, NPR, 65], F32), A.alloc("Lfi", [128, NPR, 65], F32)]
    WsT = A.alloc("WsT", [128, 8, 128], BF16)
    bs2 = A.alloc("bs2", [128, 1024], BF16)
    off_tmp = A.off
    dv = P.res("dv")
    sm = A.alloc("sm", [128, 64, NPR], F32)
    Bc = [A.alloc("Bre", [128, NPR, 16], F32), A.alloc("Bim", [128, NPR, 16], F32)]
    Ct = [A.alloc("Ctr", [128, NPR, 16], F32), A.alloc("Cti", [128, NPR, 16], F32)]
    Bb = [A.alloc("Bbr", [128, NPR, 16], F32), A.alloc("Bbi", [128, NPR, 16], F32)]
    WD = [A.alloc("WDr", [128, NPR, 128], F32), A.alloc("WDi", [128, NPR, 128], F32)]
    WDt = [A.alloc("WDtr", [128, NPR, 128], BF16), A.alloc("WDti", [128, NPR, 128], BF16)]
    T = [A.alloc("dT1", [128, NPR, 128], F32), A.alloc("dT2", [128, NPR, 128], F32)]
    dcol = A.alloc("dcol", [128, NG], F32)
    wsl = A.alloc("wsl", [128, 8, 128], F32)
    bsf = A.alloc("bsf", [128, 1024], F32)
    tK = A.alloc("tK", [128, 128], F32)

    def S(i):
        return sm.t[:, i, :]

    def tt(out, a, b, op, eng="dve"):
        P.eng(eng, lambda e: e.tensor_tensor(out=out, in0=a, in1=b, op=op), reads=[dv], writes=[dv])

    def ts(out, a, s1, op, eng="dve"):
        P.eng(eng, lambda e: e.tensor_scalar(out=out, in0=a, scalar1=s1, scalar2=None, op0=op), reads=[dv], writes=[dv])

    def actf(out, a, func, scale=1.0, bias=None):
        b = cb.t[:, 2:3] if bias is None else bias
        P.act(lambda e: e.activation(out=out, in_=a, func=func, bias=b, scale=scale), reads=[dv, cb], writes=[dv])

    def cmul(o_r, o_i, ar, ai, br, bi, t1, t2):
        tt(t1, ar, br, ALU.mult); tt(t2, ai, bi, ALU.mult); tt(o_r, t1, t2, ALU.subtract)
        tt(t1, ar, bi, ALU.mult); tt(t2, ai, br, ALU.mult); tt(o_i, t1, t2, ALU.add)

    def dmap(out, src):
        P.dma("sp", out, src, writes=[dv], key="dvload", allow_slow_non_contiguous=True)

    LR, LI, LDT, DT, X1, X2, MG, CS, SN, AR, AI, NR, NI, TA, TB, RHO, IRHO, U8R, U8I, I8R, I8I, FR, FI, DEN = range(24)
    AP0 = 24
    dmap(S(LR), pd["lam_re"][l].rearrange("(pr m) p -> (m p) pr", m=2))
    dmap(S(LI), pd["lam_im"][l].rearrange("(pr m) p -> (m p) pr", m=2))
    for m in range(2):
        dmap(sm.t[m * 64:(m + 1) * 64, LDT, :],
             pd["log_dt"][l].rearrange("(pr m) -> m pr", m=2)[m].partition_broadcast(64))
        for comp, nm in ((0, "c_re"), (1, "c_im")):
            for pr_ in range(NPR):
                dmap(Ct[comp].t[m * 64:(m + 1) * 64, pr_, :],
                     pd[nm][l].rearrange("(pr m) c p -> m pr p c", m=2)[m][pr_])
    dmap(Bc[0].t[:], pd["b_re"][l].rearrange("(pr m) p c -> (m p) pr c", m=2))
    dmap(Bc[1].t[:], pd["b_im"][l].rearrange("(pr m) p c -> (m p) pr c", m=2))
    for j in range(8):
        dmap(dcol.t[j * 16:(j + 1) * 16, :], pd["d_skip"][l].rearrange("g c -> c g"))
    dmap(wsl.t[:], pd["w_s"][l].rearrange("h t s -> t h s"))
    for r in (0, 32):
        dmap(bsf.t[r:r + 1, :], pd["b_s"][l:l + 1].rearrange("o h t -> o (h t)"))
    actf(S(DT), S(LDT), AF.Exp)
    tt(S(X1), S(LR), S(DT), ALU.mult)
    tt(S(X2), S(LI), S(DT), ALU.mult)
    actf(S(MG), S(X1), AF.Exp, scale=1.0 / 64)
    actf(S(CS), S(X2), AF.Sin, scale=1.0 / 64, bias=cb.t[:, 0:1])
    actf(S(SN), S(X2), AF.Sin, scale=1.0 / 64)
    tt(S(AR), S(MG), S(CS), ALU.mult)
    tt(S(AI), S(MG), S(SN), ALU.mult)
    cur = (AR, AI)
    nxt = (NR, NI)
    for _ in range(6):
        tt(S(TA), S(cur[0]), S(cur[0]), ALU.mult)
        tt(S(TB), S(cur[1]), S(cur[1]), ALU.mult)
        tt(S(nxt[0]), S(TA), S(TB), ALU.subtract)
        tt(S(TA), S(cur[0]), S(cur[1]), ALU.mult)
        tt(S(nxt[1]), S(TA), S(TA), ALU.add)
        cur, nxt = nxt, cur
    A1 = cur

    def apr(m):
        return S(AP0 + 2 * m)

    def api(m):
        return S(AP0 + 2 * m + 1)
    P.dve(lambda e: e.memset(apr(0), 1.0), reads=[dv], writes=[dv])
    P.dve(lambda e: e.memset(api(0), 0.0), reads=[dv], writes=[dv])
    P.dve(lambda e: e.tensor_copy(out=apr(1), in_=S(A1[0])), reads=[dv], writes=[dv])
    P.dve(lambda e: e.tensor_copy(out=api(1), in_=S(A1[1])), reads=[dv], writes=[dv])
    for m in range(2, 9):
        cmul(apr(m), api(m), apr(m - 1), api(m - 1), apr(1), api(1), S(TA), S(TB))
    actf(S(RHO), S(X1), AF.Exp, scale=8.0)
    P.dve(lambda e: e.tensor_copy(out=rho8.t[:], in_=S(RHO)), reads=[dv], writes=[dv, rho8])
    P.dve(lambda e: e.reciprocal(out=S(IRHO), in_=S(RHO)), reads=[dv], writes=[dv])
    tt(S(U8R), apr(8), S(IRHO), ALU.mult)
    tt(S(U8I), api(8), S(IRHO), ALU.mult)
    tt(S(TA), S(IRHO), S(IRHO), ALU.mult)
    tt(S(I8R), apr(8), S(TA), ALU.mult)
    tt(S(I8I), api(8), S(TA), ALU.mult)
    ts(S(I8I), S(I8I), -1.0, ALU.mult)
    ts(S(NR), apr(1), -1.0, ALU.add)
    tt(S(TA), S(LR), S(LR), ALU.mult)
    tt(S(TB), S(LI), S(LI), ALU.mult)
    tt(S(DEN), S(TA), S(TB), ALU.add)
    P.dve(lambda e: e.reciprocal(out=S(DEN), in_=S(DEN)), reads=[dv], writes=[dv])
    tt(S(TA), S(NR), S(LR), ALU.mult)
    tt(S(TB), api(1), S(LI), ALU.mult)
    tt(S(FR), S(TA), S(TB), ALU.add)
    tt(S(FR), S(FR), S(DEN), ALU.mult)
    tt(S(TA), api(1), S(LR), ALU.mult)
    tt(S(TB), S(NR), S(LI), ALU.mult)
    tt(S(FI), S(TA), S(TB), ALU.subtract)
    tt(S(FI), S(FI), S(DEN), ALU.mult)

    def bc(ap2, n):
        return ap2.unsqueeze(2).to_broadcast([128, NPR, n])
    t16 = [T[0].t[:, :, 0:16], T[1].t[:, :, 0:16]]
    cmul(Bb[0].t[:], Bb[1].t[:], bc(S(FR), 16), bc(S(FI), 16), Bc[0].t[:], Bc[1].t[:], t16[0], t16[1])
    WD4 = [w.t[:].rearrange("p a (j c) -> p a j c", j=8) for w in WD]
    for j in range(8):
        cmul(WD4[0][:, :, j, :], WD4[1][:, :, j, :], bc(apr(7 - j), 16), bc(api(7 - j), 16), Bb[0].t[:], Bb[1].t[:],
             t16[0], t16[1])
    cmul(WDt[0].t[:], WDt[1].t[:], bc(S(I8R), 128), bc(S(I8I), 128), WD[0].t[:], WD[1].t[:], T[0].t[:], T[1].t[:])
    for comp in range(2):
        for q4 in range(4):
            bank = pb[3 + (q4 % 2)]
            for q in range(4):
                pr = q4 * 4 + q
                P.pe(lambda e, comp=comp, pr=pr, q=q, bank=bank: e.transpose(
                    out=bank.t[:, q * 128:(q + 1) * 128], in_=WD[comp].t[:, pr, :], identity=ident.t[:]),
                    reads=[dv, ident], writes=[bank])
            P.dve(lambda e, comp=comp, q4=q4, bank=bank: e.tensor_copy(
                out=WDT[comp].t[:, q4 * 4:(q4 + 1) * 4, :], in_=bank.t[:].rearrange("p (a b) -> p a b", a=4)),
                reads=[bank], writes=[WDT[comp]])
    Wo4 = [w.t[:].rearrange("p a (j c) -> p a j c", j=8) for w in Wo]
    for j in range(8):
        ar_, ai_ = bc(apr(j + 1), 16), bc(api(j + 1), 16)
        tt(t16[0], Ct[0].t[:], ar_, ALU.mult); tt(t16[1], Ct[1].t[:], ai_, ALU.mult)
        P.dve(lambda e, j=j: e.tensor_tensor(out=Wo4[0][:, :, j, :], in0=t16[0], in1=t16[1], op=ALU.subtract),
              reads=[dv], writes=[dv, Wo[0]])
        tt(t16[0], Ct[0].t[:], ai_, ALU.mult); tt(t16[1], Ct[1].t[:], ar_, ALU.mult)
        tt(t16[0], t16[0], t16[1], ALU.add)
        P.dve(lambda e, j=j: e.tensor_scalar(out=Wo4[1][:, :, j, :], in0=t16[0], scalar1=-1.0, scalar2=None,
                                             op0=ALU.mult), reads=[dv], writes=[dv, Wo[1]])
    for g in range(NG):
        pr, m = divmod(g, 2)
        bank = pb[5 + (g % 2)]
        sl = slice(m * 64, (m + 1) * 64)
        P.pe(lambda e, pr=pr, sl=sl, bank=bank: e.matmul(bank.t[:, 0:128], lhsT=WDt[0].t[sl, pr, :],
                                                          rhs=Wo[0].t[sl, pr, :], start=True, stop=False),
             reads=[dv, Wo[0]], writes=[bank])
        P.pe(lambda e, pr=pr, sl=sl, bank=bank: e.matmul(bank.t[:, 0:128], lhsT=WDt[1].t[sl, pr, :],
                                                          rhs=Wo[1].t[sl, pr, :], start=False, stop=True),
             reads=[dv, Wo[1]], writes=[bank])
        P.dve(lambda e, bank=bank: e.tensor_tensor(out=tK.t[:], in0=bank.t[:, 0:128], in1=maskJ.t[:], op=ALU.mult),
              reads=[bank, maskJ, dv], writes=[dv])
        P.dve(lambda e, g=g: e.scalar_tensor_tensor(out=Kmat.t[:, g, :], in0=ident.t[:], scalar=dcol.t[:, g:g + 1],
                                                    in1=tK.t[:], op0=ALU.mult, op1=ALU.add),
              reads=[dv, ident], writes=[dv, Kmat])
    P.dve(lambda e: e.tensor_copy(out=C1.t[:, :, 0], in_=S(U8R)), reads=[dv], writes=[dv, C1])
    P.dve(lambda e: e.tensor_copy(out=S1.t[:, :, 0], in_=S(U8I)), reads=[dv], writes=[dv, S1])
    s_ = 1
    while s_ < 64:
        cr = C1.t[:, :, s_ - 1:s_].to_broadcast([128, NPR, s_])
        ci = S1.t[:, :, s_ - 1:s_].to_broadcast([128, NPR, s_])
        t1, t2 = T[0].t[:, :, 0:s_], T[1].t[:, :, 0:s_]
        cmul(C1.t[:, :, s_:2 * s_], S1.t[:, :, s_:2 * s_], cr, ci, C1.t[:, :, 0:s_], S1.t[:, :, 0:s_], t1, t2)
        s_ *= 2
    P.dve(lambda e: e.tensor_copy(out=Mtab.t[:], in_=bc(rho8.t[:], 64)), reads=[dv, rho8], writes=[dv, Mtab])
    P.dve(lambda e: e.memset(Mtab.t[:, :, 0], 0.0), reads=[dv], writes=[dv, Mtab])
    for comp in range(2):
        P.dve(lambda e, comp=comp: e.memset(Lf[comp].t[:], 0.0), reads=[dv], writes=[dv, Lf[comp]])
    for h in range(8):
        bank = pb[3 + (h % 2)]
        P.pe(lambda e, h=h, bank=bank: e.transpose(out=bank.t[:, 0:128], in_=wsl.t[:, h, :], identity=ident.t[:]),
             reads=[dv, ident], writes=[bank])
        P.dve(lambda e, h=h, bank=bank: e.tensor_tensor(out=WsT.t[:, h, :], in0=bank.t[:, 0:128], in1=tril.t[:],
                                                        op=ALU.mult), reads=[bank, tril], writes=[WsT])
    P.dve(lambda e: e.memset(bs2.t[:], 0.0), reads=[dv], writes=[dv, bs2])
    P.dve(lambda e: e.tensor_copy(out=bs2.t[0:1, :], in_=bsf.t[0:1, :]), reads=[dv], writes=[dv, bs2])
    P.dve(lambda e: e.tensor_copy(out=bs2.t[32:33, :], in_=bsf.t[32:33, :]), reads=[dv], writes=[dv, bs2])
    P.dve(lambda e: e.tensor_tensor(out=bsf.t[32:33, :], in0=bsf.t[32:33, :], in1=bs2.t[32:33, :], op=ALU.subtract),
          reads=[dv, bs2], writes=[dv])
    P.dve(lambda e: e.tensor_copy(out=bs2.t[32:33, :], in_=bsf.t[32:33, :]), reads=[dv], writes=[dv, bs2])
    P.barrier()
    A.off = off_tmp
    xt = A.alloc("xt", [128, KT, NT], F32)
    off_ht = A.off
    ht = A.alloc("ht", [128, KT, NT], BF16)
    ycat = P.carve("ycat", A.buf, off_ht, [128, KT, NT], BF16)
    P.alias(ht, ycat)
    rstd = A.alloc("rstd", [128, NT], F32)
    off_sq = A.off
    sq = A.alloc("sq", [128, KT, NT], BF16)
    vtok = P.carve("vtok", A.buf, off_sq, [128, 4, NT], BF16)
    ub = P.carve("ub", A.buf, off_sq + 4096, [128, 4, NT], BF16)
    P.alias(sq, vtok); P.alias(sq, ub)
    off_u = A.off
    Ublk = A.alloc("Ublk", [128, NG, 8, 16], BF16)
    Tt = [P.carve("Tt1", A.buf, off_u, [128, NPR, 64], F32), P.carve("Tt2", A.buf, off_u + 4096, [128, NPR, 64], F32)]
    P.alias(Ublk, Tt[0]); P.alias(Ublk, Tt[1])
    UblkT = A.alloc("UblkT", [128, NG, 64], BF16)
    Z = [A.alloc("Zr", [128, NPR, 64], F32), A.alloc("Zi", [128, NPR, 64], F32)]
    W = [A.alloc("Wr", [128, NPR, 64], F32), A.alloc("Wi", [128, NPR, 64], F32)]
    Lb = [A.alloc("Lbr", [128, NPR, 64], BF16), A.alloc("Lbi", [128, NPR, 64], BF16)]
    off_y = A.off
    Yblk = A.alloc("Yblk", [128, 8, NT], F32)
    ybuf = P.carve("ybuf", A.buf, off_y, [128, KT, NT], F32)
    P.alias(Yblk, ybuf)
    ytmp = A.alloc("ytmp", [128, NT], F32)
    sg = A.alloc("sg", [128, 4, NT], BF16)
    vn = A.alloc("vn", [128, NT], BF16)
    vsq = A.alloc("vsq", [128, NT], BF16)
    vrs = A.alloc("vrs", [128, NT], F32)
    t16s = A.alloc("t16s", [128, 2, NPR], F32)
    c.setdefault("dbg", {})["mix_off"] = A.off
    Dre = pst.t[:, 3 * 512:5 * 512].rearrange("p (a n) -> p a n", a=NPR)
    Dim = pst.t[:, 5 * 512:7 * 512].rearrange("p (a n) -> p a n", a=NPR)
    Dv = [Dre, Dim]
    Dbank = [[pb[3], pb[4]], [pb[5], pb[6]]]
    pbT = {b: pb[b].t.bitcast(BF16) for b in (5, 6, 7)}
    gpost = "g_mix_post"

    for ti in range(c["NTILES"]):
        c["load_x"](xt, ti)
        c["prenorm"](xt, ht, sq, rstd, "g_mix_pre", l)
        for j in range(8):
            bank = pb[3 + (j % 2)]
            for k in range(KT):
                P.pe(lambda e, j=j, k=k, bank=bank: e.matmul(bank.t[0:64, :], lhsT=ht.t[:, k, j:NT:8],
                                                             rhs=w_in.t[:, k, 0:512], start=(k == 0),
                                                             stop=(k == KT - 1)), reads=[ht, w_in], writes=[bank])
            src = bank.t[0:64, :].rearrange("p (g c) -> p g c", g=NG)
            if j % 2 == 0:
                P.act(lambda e, j=j, src=src: e.activation(out=Ublk.t[0:64, :, j, :], in_=src, func=AF.Copy),
                      reads=[bank], writes=[Ublk])
            else:
                P.dve(lambda e, j=j, src=src: e.tensor_copy(out=Ublk.t[0:64, :, j, :], in_=src),
                      reads=[bank], writes=[Ublk])
        for g8 in range(4):
            bno = 5 + (g8 % 2)
            for q in range(8):
                g = g8 * 8 + q
                P.pe(lambda e, g=g, q=q, bno=bno: e.transpose(
                    out=pbT[bno][:, q * 64:(q + 1) * 64], in_=Ublk.t[0:64, g, :, :].rearrange("p j c -> p (j c)"),
                    identity=identb.t[0:64, 0:64]), reads=[Ublk, identb], writes=[pb[bno]])
            P.act(lambda e, g8=g8, bno=bno: e.activation(
                out=UblkT.t[:, g8 * 8:(g8 + 1) * 8, :], in_=pbT[bno][:, 0:512].rearrange("p (a n) -> p a n", a=8),
                func=AF.Copy), reads=[pb[bno]], writes=[UblkT])
        if ti > 0:
            for comp in range(2):
                P.act(lambda e, comp=comp: e.activation(out=Lf[comp].t[:, :, 0], in_=Lf[comp].t[:, :, 64], func=AF.Copy),
                      reads=[Lf[comp]], writes=[Lf[comp]])
        for g in range(NG):
            pr, m = divmod(g, 2)
            for comp in range(2):
                bank = Dbank[comp][pr // 8]
                P.pe(lambda e, g=g, pr=pr, m=m, comp=comp: e.matmul(
                    Dv[comp][m * 64:(m + 1) * 64, pr, :], lhsT=WDT[comp].t[:, pr, m * 64:(m + 1) * 64],
                    rhs=UblkT.t[:, g, :], start=True, stop=True), reads=[UblkT, WDT[comp]], writes=[bank])
        dr = [pb[3], pb[4], pb[5], pb[6]]

        def dtt(out, a, b, op, rd, wr):
            P.dve(lambda e: e.tensor_tensor(out=out, in0=a, in1=b, op=op), reads=rd, writes=wr)
        dtt(Tt[0].t[:], Dre, C1.t[:], ALU.mult, dr + [C1], [Tt[0]])
        dtt(Tt[1].t[:], Dim, S1.t[:], ALU.mult, dr + [S1], [Tt[1]])
        dtt(Z[0].t[:], Tt[0].t[:], Tt[1].t[:], ALU.add, Tt, [Z[0]])
        dtt(Tt[0].t[:], Dim, C1.t[:], ALU.mult, dr + [C1], [Tt[0]])
        dtt(Tt[1].t[:], Dre, S1.t[:], ALU.mult, dr + [S1], [Tt[1]])
        dtt(Z[1].t[:], Tt[0].t[:], Tt[1].t[:], ALU.subtract, Tt, [Z[1]])
        for comp in range(2):
            dtt(t16s.t[:, comp, :], Lf[comp].t[:, :, 0], rho8.t[:], ALU.mult, [Lf[comp], rho8], [t16s])
            dtt(Z[comp].t[:, :, 0], Z[comp].t[:, :, 0], t16s.t[:, comp, :], ALU.add, [Z[comp], t16s], [Z[comp]])
            P.dve(lambda e, comp=comp: e.tensor_tensor_scan(
                out=W[comp].t[:].rearrange("p a n -> p (a n)"), data0=Mtab.t[:].rearrange("p a n -> p (a n)"),
                data1=Z[comp].t[:].rearrange("p a n -> p (a n)"), initial=0.0, op0=ALU.mult, op1=ALU.add),
                reads=[Mtab, Z[comp]], writes=[W[comp]])
        dtt(Tt[0].t[:], C1.t[:], W[0].t[:], ALU.mult, [C1, W[0]], [Tt[0]])
        dtt(Tt[1].t[:], S1.t[:], W[1].t[:], ALU.mult, [S1, W[1]], [Tt[1]])
        dtt(Lf[0].t[:, :, 1:65], Tt[0].t[:], Tt[1].t[:], ALU.subtract, Tt, [Lf[0]])
        dtt(Tt[0].t[:], C1.t[:], W[1].t[:], ALU.mult, [C1, W[1]], [Tt[0]])
        dtt(Tt[1].t[:], S1.t[:], W[0].t[:], ALU.mult, [S1, W[0]], [Tt[1]])
        dtt(Lf[1].t[:, :, 1:65], Tt[0].t[:], Tt[1].t[:], ALU.add, Tt, [Lf[1]])
        for comp in range(2):
            P.act(lambda e, comp=comp: e.activation(out=Lb[comp].t[:], in_=Lf[comp].t[:, :, 0:64], func=AF.Copy),
                  reads=[Lf[comp]], writes=[Lb[comp]])
        for ct in range(4):
            for grp in range(3):
                col = 512 * (grp + 1) + ct * 128
                bank = pb[1 + ((ct * 3 + grp) % 2)]
                for k in range(KT):
                    P.pe(lambda e, col=col, k=k, bank=bank: e.matmul(bank.t[:], lhsT=w_in.t[:, k, col:col + 128],
                                                                     rhs=ht.t[:, k, :], start=(k == 0),
                                                                     stop=(k == KT - 1)), reads=[ht, w_in], writes=[bank])
                if grp == 0:
                    P.act(lambda e, ct=ct, bank=bank: e.activation(out=sg.t[:, ct, :], in_=bank.t[:], func=AF.Sigmoid),
                          reads=[bank], writes=[sg])
                elif grp == 1:
                    P.act(lambda e, ct=ct, bank=bank: e.activation(out=ub.t[:, ct, :], in_=bank.t[:], func=AF.Copy),
                          reads=[bank], writes=[ub])
                else:
                    P.act(lambda e, bank=bank: e.activation(out=vsq.t[:], in_=bank.t[:], func=AF.Square),
                          reads=[bank], writes=[vsq])
                    P.pe(lambda e: e.matmul(pb[0].t[:], lhsT=blkones.t[:], rhs=vsq.t[:], start=True, stop=True),
                         reads=[vsq, blkones], writes=[pb[0]])
                    c["rstd_from_ss"](pb[0].t[:], vrs, 64, [pb[0]])
                    P.dve(lambda e, ct=ct, bank=bank: e.scalar_tensor_tensor(
                        out=vn.t[:], in0=bank.t[:], scalar=gvv.t[:, l, ct:ct + 1], in1=vrs.t[:], op0=ALU.mult,
                        op1=ALU.mult), reads=[bank, gvv, vrs], writes=[vn])
                    for cc in range(4):
                        P.pe(lambda e, cc=cc: e.transpose(out=pbT[7][:, cc * 128:(cc + 1) * 128],
                                                          in_=vn.t[:, cc * 128:(cc + 1) * 128], identity=identb.t[:]),
                             reads=[vn, identb], writes=[pb[7]])
                    P.dve(lambda e, ct=ct: e.tensor_copy(
                        out=vtok.t[:, :, ct * 128:(ct + 1) * 128],
                        in_=pbT[7][:, 0:512].rearrange("p (a n) -> p a n", a=4)), reads=[pb[7]], writes=[vtok])
        for g4 in range(8):
            bank = pb[1 + (g4 % 2)]
            for q in range(4):
                g = g4 * 4 + q
                pr, m = divmod(g, 2)
                sl = slice(m * 64, (m + 1) * 64)
                osl = bank.t[0:64, q * 128:(q + 1) * 128]
                P.pe(lambda e, g=g, osl=osl: e.matmul(osl, lhsT=UblkT.t[:, g, :], rhs=Kmat.t[:, g, :], start=True,
                                                      stop=False), reads=[UblkT, Kmat], writes=[bank])
                P.pe(lambda e, pr=pr, sl=sl, osl=osl: e.matmul(osl, lhsT=Lb[0].t[sl, pr, :], rhs=Wo[0].t[sl, pr, :],
                                                               start=False, stop=False), reads=[Lb[0], Wo[0]],
                     writes=[bank])
                P.pe(lambda e, pr=pr, sl=sl, osl=osl: e.matmul(osl, lhsT=Lb[1].t[sl, pr, :], rhs=Wo[1].t[sl, pr, :],
                                                               start=False, stop=True), reads=[Lb[1], Wo[1]],
                     writes=[bank])
            P.dve(lambda e, g4=g4, bank=bank: e.tensor_copy(
                out=Yblk.t[0:64, :, g4 * 64:(g4 + 1) * 64].rearrange("p j (g c) -> p j g c", g=4),
                in_=bank.t[0:64, :].rearrange("p (g j c) -> p j g c", g=4, j=8)), reads=[bank], writes=[Yblk])
        for ct in range(4):
            bank = pb[3 + (ct % 2)]
            for j in range(8):
                P.pe(lambda e, ct=ct, j=j, bank=bank: e.transpose(
                    out=bank.t[:, j * 64:(j + 1) * 64], in_=Yblk.t[0:64, j, ct * 128:(ct + 1) * 128],
                    identity=ident.t[0:64, 0:64]), reads=[Yblk, ident], writes=[bank])
            P.act(lambda e, bank=bank: e.activation(out=ytmp.t[:].rearrange("p (n j) -> p n j", j=8),
                                                    in_=bank.t[:].rearrange("p (j n) -> p n j", j=8),
                                                    func=AF.Gelu_apprx_tanh), reads=[bank], writes=[ytmp])
            P.pool(lambda e, ct=ct: e.tensor_tensor(out=ycat.t[:, ct, :], in0=ytmp.t[:], in1=sg.t[:, ct, :],
                                                    op=ALU.mult), reads=[ytmp, sg], writes=[ycat])
        for ct in range(4):
            bank = pb[1 + (ct % 2)]
            for cc in range(4):
                for hh in range(2):
                    h = 2 * ct + hh
                    osl = bank.t[hh * 64:(hh + 1) * 64, cc * 128:(cc + 1) * 128]
                    P.pe(lambda e, cc=cc, h=h, osl=osl: e.matmul(osl, lhsT=vtok.t[:, cc, h * 64:(h + 1) * 64],
                                                                 rhs=WsT.t[:, h, :], start=True, stop=False),
                         reads=[vtok, WsT], writes=[bank])
                    P.pe(lambda e, h=h, osl=osl: e.matmul(osl, lhsT=onesb.t[0:64, 0:64],
                                                          rhs=bs2.t[0:64, h * 128:(h + 1) * 128], start=False,
                                                          stop=True), reads=[bs2, onesb], writes=[bank])
            P.dve(lambda e, ct=ct, bank=bank: e.tensor_tensor(out=ycat.t[:, 4 + ct, :], in0=bank.t[:],
                                                              in1=ub.t[:, ct, :], op=ALU.mult),
                  reads=[bank, ub], writes=[ycat])

        def mm(o, bank):
            for k in range(KT):
                P.pe(lambda e, o=o, k=k, bank=bank: e.matmul(bank.t[:], lhsT=w_out.t[:, k, o * 128:(o + 1) * 128],
                                                             rhs=ycat.t[:, k, :], start=(k == 0), stop=(k == KT - 1)),
                     reads=[w_out, ycat], writes=[bank])
        c["postnorm_res"](xt, ybuf, sq, rstd, gpost, l, mm)
        c["store_x"](xt, ti)
```

```python
import contextlib
import numpy as np
import concourse.bass as bass
import concourse.mybir as mybir
from concourse.bass_utils import run_bass_kernel_spmd

F32 = mybir.dt.float32
BF16 = mybir.dt.bfloat16
AF = mybir.ActivationFunctionType
ALU = mybir.AluOpType

ENGS = ("pe", "dve", "act", "pool", "sp")
SAME_ENGINE_SYNC = True
RELAX_WAR = True


class Res:
    def __init__(self, name):
        self.name = name
        self.last_w = None
        self.readers = {}
        self._subs = {}

    def sub(self, k):
        r = self._subs.get(k)
        if r is None:
            r = Res(f"{self.name}.{k}")
            self._subs[k] = r
        return r


class Buf(Res):
    def __init__(self, name, t):
        super().__init__(name)
        self.t = t


class Prog:
    def __init__(self, nc):
        self.nc = nc
        self.stack = contextlib.ExitStack()
        self.q = {e: [] for e in ENGS}
        self.ecnt = {e: 0 for e in ENGS}
        self.seen = {e: {} for e in ENGS}
        self.dmacnt = {}
        self.nres = 0

    def sbuf(self, name, shape, dtype):
        t = self.stack.enter_context(self.nc.sbuf_tensor(name, list(shape), dtype))
        return Buf(name, t)

    def psum(self, name, shape, dtype):
        t = self.stack.enter_context(self.nc.psum_tensor(name, list(shape), dtype))
        return Buf(name, t)

    def res(self, name):
        return Res(name)

    def _record(self, eng, fn, reads, writes, dmakey=None, inc=1):
        reads = list(reads)
        writes = list(writes)
        for r in list(reads):
            reads.extend(getattr(r, "also", ()))
        for w in list(writes):
            writes.extend(getattr(w, "also", ()))
        waits = {}
        seen = self.seen[eng]

        def need(ev, raw=True):
            if ev is None:
                return
            key, val, src = ev
            if src == eng and (eng == "pe" or not SAME_ENGINE_SYNC or (not raw and RELAX_WAR)):
                return
            if seen.get(key, 0) >= val:
                return
            if waits.get(key, 0) < val:
                waits[key] = val

        for r in reads:
            need(r.last_w)
        for w in writes:
            lw = w.last_w
            if not (dmakey is not None and lw is not None and lw[0] == ("dma", dmakey)):
                need(lw, raw=False)
            for k, (v, s) in w.readers.items():
                need((k, v, s), raw=False)
        for k, v in waits.items():
            seen[k] = v
        if dmakey is None:
            self.ecnt[eng] += 1
            ev = (("eng", eng), self.ecnt[eng], eng)
        else:
            self.dmacnt[dmakey] = self.dmacnt.get(dmakey, 0) + inc
            ev = (("dma", dmakey), self.dmacnt[dmakey], None)
        self.q[eng].append((fn, list(waits.items()), ev, inc if dmakey is not None else 1))
        for r in reads:
            old = r.readers.get(ev[0])
            if old is None or old[0] < ev[1]:
                r.readers[ev[0]] = (ev[1], ev[2])
        for w in writes:
            w.last_w = ev
            w.readers = {}
        return ev

    def pe(self, fn, reads=(), writes=()):
        return self._record("pe", fn, reads, writes)

    def dve(self, fn, reads=(), writes=()):
        return self._record("dve", fn, reads, writes)

    def act(self, fn, reads=(), writes=()):
        return self._record("act", fn, reads, writes)

    def pool(self, fn, reads=(), writes=()):
        return self._record("pool", fn, reads, writes)

    def eng(self, e, fn, reads=(), writes=()):
        return self._record(e, fn, reads, writes)

    def dma(self, eng, out, in_, reads=(), writes=(), key=None, **kw):
        key = key or writes[0].name
        return self._record(eng, lambda e: e.dma_start(out=out, in_=in_, **kw), reads, writes, dmakey=key, inc=16)

    def collective(self, fn, reads=(), writes=(), key=None):
        key = key or writes[0].name
        return self._record("pool", fn, reads, writes, dmakey=key, inc=16)

    def finish(self, finals):
        nc = self.nc
        waits = {}
        for r in finals:
            key, val, _ = r.last_w
            waits[key] = max(waits.get(key, 0), val)
        self.q["sp"].append((None, list(waits.items()), None, 0))
        keys = [("eng", e) for e in ENGS if self.ecnt[e] > 0] + [("dma", k) for k in self.dmacnt]
        sems = {}
        for i, k in enumerate(keys):
            sems[k] = self.stack.enter_context(nc.semaphore(f"s{i}_{k[1]}"[:24].replace(".", "_")))
        self.nsems = len(keys)
        q = self.q

        def replay(ename, e):
            for fn, ws, ev, inc in q[ename]:
                for k, v in ws:
                    e.wait_ge(sems[k], v)
                if fn is None:
                    continue
                ins = fn(e)
                ins.then_inc(sems[ev[0]], inc)

        with nc.Block() as block:
            @block.tensor
            def _(e):
                replay("pe", e)

            @block.vector
            def _(e):
                replay("dve", e)

            @block.scalar
            def _(e):
                replay("act", e)

            @block.gpsimd
            def _(e):
                replay("pool", e)

            @block.sync
            def _(e):
                replay("sp", e)
        self.stack.close()

    def barrier(self):
        for e in ENGS:
            waits = {}
            for e2 in ENGS:
                if e2 != e and self.ecnt[e2] > self.seen[e].get(("eng", e2), 0):
                    waits[("eng", e2)] = self.ecnt[e2]
            if e != "pe" and self.ecnt[e] > self.seen[e].get(("eng", e), 0):
                waits[("eng", e)] = self.ecnt[e]
            for k, v in self.dmacnt.items():
                if v > self.seen[e].get(("dma", k), 0):
                    waits[("dma", k)] = v
            for k, v in waits.items():
                self.seen[e][k] = v
            if waits:
                self.q[e].append((None, list(waits.items()), None, 0))

    @staticmethod
    def alias(a, b):
        a.also = list(getattr(a, "also", [])) + [b]
        b.also = list(getattr(b, "also", [])) + [a]

    def carve(self, name, arena, off_bytes, shape, dtype):
        esz = 4 if dtype == F32 else 2
        n = 1
        for s in shape[1:]:
            n *= s
        nb = n * esz
        assert off_bytes % 4 == 0 and nb % 4 == 0
        a = arena.t[:, off_bytes // 4:(off_bytes + nb) // 4]
        if dtype != F32:
            a = a.bitcast(dtype)
        if len(shape) == 3:
            a = a.rearrange("p (a b) -> p a b", a=shape[1])
        elif len(shape) == 4:
            a = a.rearrange("p (a b c) -> p a b c", a=shape[1], b=shape[2])
        b = Buf(name, a)
        b.nbytes = nb
        return b


D = 1024
KT = 8
NT = 512
PI = 3.14159265358979
PNAMES = ["g_mix_pre", "w_in", "lam_re", "lam_im", "log_dt", "b_re", "b_im", "c_re", "c_im", "d_skip", "g_v",
          "w_s", "b_s", "w_out", "g_mix_post", "g_x_pre", "g_mem", "w_q", "w_k", "w_v", "w_o", "g_x_post",
          "g_ffn_pre", "w_up", "conv_w", "conv_b", "w_down", "g_ffn_post"]
PSHAPES = {"g_mix_pre": [4, 1024], "w_in": [4, 1024, 2048], "lam_re": [4, 32, 64], "lam_im": [4, 32, 64],
           "log_dt": [4, 32], "b_re": [4, 32, 64, 16], "b_im": [4, 32, 64, 16], "c_re": [4, 32, 16, 64],
           "c_im": [4, 32, 16, 64], "d_skip": [4, 32, 16], "g_v": [4, 512], "w_s": [4, 8, 128, 128],
           "b_s": [4, 8, 128], "w_out": [4, 1024, 1024], "g_mix_post": [4, 1024], "g_x_pre": [4, 1024],
           "g_mem": [4, 1024], "w_q": [4, 1024, 1024], "w_k": [4, 1024, 1024], "w_v": [4, 1024, 1024],
           "w_o": [4, 1024, 1024], "g_x_post": [4, 1024], "g_ffn_pre": [4, 1024], "w_up": [4, 1024, 5632],
           "conv_w": [4, 3, 5632], "conv_b": [4, 5632], "w_down": [4, 2816, 1024], "g_ffn_post": [4, 1024]}


class Arena:
    def __init__(self, P, buf, total):
        self.P, self.buf, self.total, self.off = P, buf, total, 0

    def alloc(self, name, shape, dtype):
        b = self.P.carve(name, self.buf, self.off, shape, dtype)
        self.off += (b.nbytes + 31) // 32 * 32
        assert self.off <= self.total, (name, self.off, self.total)
        return b

    def reset(self):
        self.P.barrier()
        self.off = 0


def build_program(S=8192, DEPTH=4, STAGES=("mix", "xat", "ffn")):
    nc = bass.Bass("TRN2", target_bir_lowering=False)
    NTILES = S // NT
    x_d = nc.dram_tensor("x", [S, D], F32, kind="ExternalInput").ap()
    mem_d = nc.dram_tensor("mem", [256, D], F32, kind="ExternalInput").ap()
    pd = {n: nc.dram_tensor(n, PSHAPES[n], F32, kind="ExternalInput").ap() for n in PNAMES}
    out_d = nc.dram_tensor("out", [S, D], F32, kind="ExternalOutput").ap()
    xs_d = nc.dram_tensor("xs", [D, S], F32).ap()
    xs_v = xs_d.rearrange("(k p) t -> p k t", p=128)
    P = Prog(nc)
    xs_r = [P.res(f"xs{t}") for t in range(NTILES)]
    out_rs = [P.res("out_a"), P.res("out_b")]

    ident = P.sbuf("ident", [128, 128], F32)
    identb = P.sbuf("identb", [128, 128], BF16)
    onesb = P.sbuf("onesb", [128, 128], BF16)
    blkones = P.sbuf("blkones", [128, 128], BF16)
    maskJ = P.sbuf("maskJ", [128, 128], F32)
    tril = P.sbuf("tril", [128, 128], F32)
    cb = P.sbuf("cbias", [128, 4], F32)
    gv = {n: P.sbuf("gv_" + n, [128, 4, 8], F32) for n in
          ["g_mix_pre", "g_mix_post", "g_x_pre", "g_mem", "g_x_post", "g_ffn_pre", "g_ffn_post"]}
    gvv = P.sbuf("gv_g_v", [128, 4, 4], F32)
    memhT = P.sbuf("memhT", [128, 8, 256], F32)
    ARENA_BYTES = 196 * 1024
    arena_buf = P.sbuf("arena", [128, ARENA_BYTES // 4], F32)
    A = Arena(P, arena_buf, ARENA_BYTES)
    pst = P.psum("pst", [128, 8 * 512], F32)
    pb = []
    for b in range(8):
        r = Buf(f"pb{b}", pst.t[:, b * 512:(b + 1) * 512])
        pb.append(r)

    def cst(e, ap, v, wr):
        P.eng(e, lambda en: en.memset(ap, v), writes=wr)

    cst("pool", ident.t[:], 1.0, [ident])
    P.pool(lambda e: e.affine_select(out=ident.t[:], in_=ident.t[:], pattern=[[1, 128]], compare_op=ALU.is_equal,
                                     fill=0.0, base=0, channel_multiplier=-1), reads=[ident], writes=[ident])
    P.dve(lambda e: e.tensor_copy(out=identb.t[:], in_=ident.t[:]), reads=[ident], writes=[identb])
    cst("pool", onesb.t[:], 1.0, [onesb])
    cst("pool", blkones.t[:], 0.0, [blkones])
    cst("pool", blkones.t[0:64, 0:64], 1.0, [blkones])
    cst("pool", blkones.t[64:128, 64:128], 1.0, [blkones])
    cst("pool", maskJ.t[:], 1.0, [maskJ])
    P.pool(lambda e: e.affine_select(out=maskJ.t[:].rearrange("p (j c) -> p j c", j=8),
                                     in_=maskJ.t[:].rearrange("p (j c) -> p j c", j=8),
                                     pattern=[[16, 8], [0, 16]], compare_op=ALU.is_ge, fill=0.0, base=15,
                                     channel_multiplier=-1), reads=[maskJ], writes=[maskJ])
    cst("pool", tril.t[:], 1.0, [tril])
    P.pool(lambda e: e.affine_select(out=tril.t[:], in_=tril.t[:], pattern=[[1, 128]], compare_op=ALU.is_ge,
                                     fill=0.0, base=0, channel_multiplier=-1), reads=[tril], writes=[tril])
    cst("pool", cb.t[:, 0:1], PI / 2, [cb])
    cst("pool", cb.t[:, 1:2], 1e-6, [cb])
    cst("pool", cb.t[:, 2:3], 0.0, [cb])
    with nc.allow_non_contiguous_dma(reason="small param loads"):
        pass
    import os
    KDBG = os.environ.get("KDBG", "")
    for n, t in (gv.items() if "nogv" not in KDBG else []):
        P.dma("sp", t.t[:], pd[n].rearrange("l (k p) -> p l k", p=128), writes=[t], allow_slow_non_contiguous=True)
    if "nogv" not in KDBG:
      P.dma("sp", gvv.t[:], pd["g_v"].rearrange("l (k p) -> p l k", p=128), writes=[gvv], allow_slow_non_contiguous=True)

    def rstd_from_ss(ss_ps, rstd, n_feat, rd):
        P.act(lambda e: e.activation(out=rstd.t[:], in_=ss_ps, func=AF.Ln, bias=cb.t[:, 1:2], scale=1.0 / n_feat),
              reads=rd + [cb], writes=[rstd])
        P.act(lambda e: e.activation(out=rstd.t[:], in_=rstd.t[:], func=AF.Exp, bias=cb.t[:, 2:3], scale=-0.5),
              reads=[rstd, cb], writes=[rstd])

    xs_all = [[xs_r[t]] + [xs_r[t].sub(o) for o in range(KT)] for t in range(NTILES)]

    def load_x(xt, ti):
        P.dma("sp", xt.t[:], xs_v[:, :, ti * NT:(ti + 1) * NT], reads=xs_all[ti], writes=[xt])

    def store_x(xt, ti):
        P.dma("sp", xs_v[:, :, ti * NT:(ti + 1) * NT], xt.t[:], reads=[xt], writes=xs_all[ti])

    def prenorm(xt, ht, sq, rstd, gname, l):
        P.act(lambda e: e.activation(out=sq.t[:], in_=xt.t[:], func=AF.Square), reads=[xt], writes=[sq])
        for k in range(KT):
            P.pe(lambda e, k=k: e.matmul(pb[0].t[:], lhsT=onesb.t[:], rhs=sq.t[:, k, :], start=(k == 0),
                                         stop=(k == KT - 1)), reads=[sq, onesb], writes=[pb[0]])
        rstd_from_ss(pb[0].t[:], rstd, D, [pb[0]])
        g = gv[gname]
        for k in range(KT):
            P.dve(lambda e, k=k: e.scalar_tensor_tensor(out=ht.t[:, k, :], in0=xt.t[:, k, :], scalar=g.t[:, l, k:k + 1],
                                                        in1=rstd.t[:], op0=ALU.mult, op1=ALU.mult),
                  reads=[xt, g, rstd], writes=[ht])

    def postnorm_res(xt, ybuf, sq, rstd, gname, l, mm_fn):
        for o in range(KT):
            bank = pb[1 + (o % 2)]
            mm_fn(o, bank)
            P.act(lambda e, o=o, bank=bank: e.activation(out=ybuf.t[:, o, :], in_=bank.t[:], func=AF.Copy),
                  reads=[bank], writes=[ybuf])
            P.dve(lambda e, o=o, bank=bank: e.tensor_tensor(out=sq.t[:, o, :], in0=bank.t[:], in1=ybuf.t[:, o, :],
                                                            op=ALU.mult), reads=[bank, ybuf], writes=[sq])
        for k in range(KT):
            P.pe(lambda e, k=k: e.matmul(pb[0].t[:], lhsT=onesb.t[:], rhs=sq.t[:, k, :], start=(k == 0),
                                         stop=(k == KT - 1)), reads=[sq, onesb], writes=[pb[0]])
        rstd_from_ss(pb[0].t[:], rstd, D, [pb[0]])
        g = gv[gname]
        for o in range(KT):
            P.dve(lambda e, o=o: e.scalar_tensor_tensor(out=ybuf.t[:, o, :], in0=ybuf.t[:, o, :],
                                                        scalar=g.t[:, l, o:o + 1], in1=rstd.t[:], op0=ALU.mult,
                                                        op1=ALU.mult), reads=[ybuf, g, rstd], writes=[ybuf])
            P.pool(lambda e, o=o: e.tensor_tensor(out=xt.t[:, o, :], in0=xt.t[:, o, :], in1=ybuf.t[:, o, :],
                                                  op=ALU.add), reads=[xt, ybuf], writes=[xt])

    def load_w(dst, src_ap, ncols, eng="pool"):
        c0 = 0
        while c0 < ncols:
            c1 = min(ncols, c0 + 2048)
            P.dma(eng, dst.t[:, :, c0:c1], src_ap[:, :, c0:c1], writes=[dst])
            c0 = c1

    xtok_b = [A.alloc("xtok_a", [128, 4, D], F32), A.alloc("xtok_b", [128, 4, D], F32)]
    xt0_b = [A.alloc("xt0_a", [128, KT, NT], F32), A.alloc("xt0_b", [128, KT, NT], F32)]
    for ti in range(NTILES):
        xtok = xtok_b[ti % 2]
        xt0 = xt0_b[ti % 2]
        P.dma("sp", xtok.t[:], x_d[ti * NT:(ti + 1) * NT, :].rearrange("(s p) d -> p s d", p=128), writes=[xtok])
        for k in range(KT):
            bank = pb[1 + (k % 4)]
            for s in range(4):
                P.pe(lambda e, k=k, s=s, bank=bank, xtok=xtok: e.transpose(out=bank.t[:, s * 128:(s + 1) * 128],
                                                                in_=xtok.t[:, s, k * 128:(k + 1) * 128],
                                                                identity=ident.t[:]),
                     reads=[xtok, ident], writes=[bank])
            if k % 2 == 0:
                P.act(lambda e, k=k, bank=bank, xt0=xt0: e.activation(out=xt0.t[:, k, :], in_=bank.t[:], func=AF.Copy),
                      reads=[bank], writes=[xt0])
            else:
                P.dve(lambda e, k=k, bank=bank, xt0=xt0: e.tensor_copy(out=xt0.t[:, k, :], in_=bank.t[:]),
                      reads=[bank], writes=[xt0])
        store_x(xt0, ti)
    if "nomem" in KDBG:
        return_early_mem = True
    memt = A.alloc("memt", [128, 2, D], F32)
    msq = A.alloc("msq", [128, D], F32)
    mss = A.alloc("mss", [128, 2], F32)
    P.dma("sp", memt.t[:], mem_d.rearrange("(s p) d -> p s d", p=128), writes=[memt])
    for s in range(2):
        P.act(lambda e, s=s: e.activation(out=msq.t[:], in_=memt.t[:, s, :], func=AF.Square,
                                          accum_out=mss.t[:, s:s + 1]), reads=[memt], writes=[msq, mss])
    P.act(lambda e: e.activation(out=mss.t[:], in_=mss.t[:], func=AF.Sqrt, bias=cb.t[:, 1:2], scale=1.0 / D),
          reads=[mss, cb], writes=[mss])
    P.dve(lambda e: e.reciprocal(out=mss.t[:], in_=mss.t[:]), reads=[mss], writes=[mss])
    for s in range(2):
        P.dve(lambda e, s=s: e.tensor_scalar(out=memt.t[:, s, :], in0=memt.t[:, s, :], scalar1=mss.t[:, s:s + 1],
                                             scalar2=None, op0=ALU.mult), reads=[memt, mss], writes=[memt])
    for k in range(KT):
        bank = pb[1 + (k % 4)]
        for s in range(2):
            P.pe(lambda e, k=k, s=s, bank=bank: e.transpose(out=bank.t[:, s * 128:(s + 1) * 128],
                                                            in_=memt.t[:, s, k * 128:(k + 1) * 128],
                                                            identity=ident.t[:]), reads=[memt, ident], writes=[bank])
        P.act(lambda e, k=k, bank=bank: e.activation(out=memhT.t[:, k, :], in_=bank.t[:, 0:256], func=AF.Copy),
              reads=[bank], writes=[memhT])

    ctx = dict(nc=nc, P=P, A=A, pb=pb, pd=pd, gv=gv, gvv=gvv, cb=cb, ident=ident, identb=identb, onesb=onesb,
               blkones=blkones, maskJ=maskJ, tril=tril, memhT=memhT, load_x=load_x, store_x=store_x,
               prenorm=prenorm, postnorm_res=postnorm_res, load_w=load_w, rstd_from_ss=rstd_from_ss,
               NTILES=NTILES, xs_r=xs_r, xs_v=xs_v, pst=pst)
    for l in range(DEPTH):
        if "mix" in STAGES:
            A.reset()
            mixer_layer(ctx, l)
        if "xat" in STAGES:
            A.reset()
            xattn_layer(ctx, l)
        if "ffn" in STAGES:
            A.reset()
            ffn_layer(ctx, l)

    A.reset()
    xt0e_b = [A.alloc("xt0e_a", [128, KT, NT], F32), A.alloc("xt0e_b", [128, KT, NT], F32)]
    xtoke_b = [A.alloc("xtoke_a", [128, 4, D], F32), A.alloc("xtoke_b", [128, 4, D], F32)]
    for ti in range(NTILES):
        xt0e = xt0e_b[ti % 2]
        xtoke = xtoke_b[ti % 2]
        load_x(xt0e, ti)
        for s in range(4):
            for kh in range(2):
                bank = pb[1 + ((2 * s + kh) % 4)]
                for kk in range(4):
                    k = kh * 4 + kk
                    P.pe(lambda e, k=k, kk=kk, s=s, bank=bank, xt0e=xt0e: e.transpose(
                        out=bank.t[:, kk * 128:(kk + 1) * 128], in_=xt0e.t[:, k, s * 128:(s + 1) * 128],
                        identity=ident.t[:]), reads=[xt0e, ident], writes=[bank])
                if kh == 0:
                    P.act(lambda e, s=s, kh=kh, bank=bank, xtoke=xtoke: e.activation(out=xtoke.t[:, s, kh * 512:(kh + 1) * 512],
                                                                        in_=bank.t[:], func=AF.Copy),
                          reads=[bank], writes=[xtoke])
                else:
                    P.dve(lambda e, s=s, kh=kh, bank=bank, xtoke=xtoke: e.tensor_copy(out=xtoke.t[:, s, kh * 512:(kh + 1) * 512],
                                                                         in_=bank.t[:]), reads=[bank], writes=[xtoke])
        P.dma("sp", out_d[ti * NT:(ti + 1) * NT, :].rearrange("(s p) d -> p s d", p=128), xtoke.t[:],
              reads=[xtoke], writes=[out_rs[ti % 2]], key="out%d" % (ti % 2))
    P.finish([r for r in out_rs if r.last_w is not None])
    return nc


def xattn_layer(c, l):
    P, A, pb, pd = c["P"], c["A"], c["pb"], c["pd"]
    onesb, gv, memhT, cb = c["onesb"], c["gv"], c["memhT"], c["cb"]
    wv3 = lambda n: pd[n][l].rearrange("(k p) n -> p k n", p=128)
    wq = A.alloc("wq", [128, KT, D], BF16)
    wo = A.alloc("wo", [128, KT, D], BF16)
    wk = A.alloc("wk", [128, KT, D], BF16)
    wvv = A.alloc("wv", [128, KT, D], BF16)
    for w, n in ((wk, "w_k"), (wvv, "w_v"), (wq, "w_q"), (wo, "w_o")):
        c["load_w"](w, wv3(n), D)
    memn = A.alloc("memn", [128, KT, 256], BF16)
    kT = A.alloc("kT", [128, KT, 256], BF16)
    vt = A.alloc("vt", [128, 2, D], BF16)
    xts = [A.alloc("xt_0", [128, KT, NT], F32), A.alloc("xt_1", [128, KT, NT], F32)]
    hts = [A.alloc("ht_0", [128, KT, NT], BF16), A.alloc("ht_1", [128, KT, NT], BF16)]
    sq_pre = A.alloc("sq_pre", [128, KT, NT], BF16)
    rstd_pre = A.alloc("rstd_pre", [128, NT], F32)
    sq = A.alloc("sq", [128, KT, NT], BF16)
    rstd = A.alloc("rstd", [128, NT], F32)
    qT = A.alloc("qT", [128, KT, NT], BF16)
    expTs = [A.alloc("expT0", [128, 2, NT], BF16), A.alloc("expT1", [128, 2, NT], BF16)]
    rdens = [A.alloc("rden0", [128, NT], F32), A.alloc("rden1", [128, NT], F32)]
    oT = A.alloc("oT", [128, KT, NT], BF16)
    ybuf = A.alloc("ybuf", [128, KT, NT], F32)
    gm = gv["g_mem"]
    for k in range(KT):
        P.dve(lambda e, k=k: e.tensor_scalar(out=memn.t[:, k, :], in0=memhT.t[:, k, :], scalar1=gm.t[:, l, k:k + 1],
                                             scalar2=None, op0=ALU.mult), reads=[memhT, gm], writes=[memn])
    for o in range(KT):
        bank = pb[3 + (o % 2)]
        for k in range(KT):
            P.pe(lambda e, o=o, k=k, bank=bank: e.matmul(bank.t[:, 0:256], lhsT=wk.t[:, k, o * 128:(o + 1) * 128],
                                                         rhs=memn.t[:, k, :], start=(k == 0), stop=(k == KT - 1)),
                 reads=[wk, memn], writes=[bank])
        P.act(lambda e, o=o, bank=bank: e.activation(out=kT.t[:, o, :], in_=bank.t[:, 0:256], func=AF.Copy),
              reads=[bank], writes=[kT])
    for mt in range(2):
        for hf in range(2):
            bank = pb[3 + (hf % 2)]
            for k in range(KT):
                P.pe(lambda e, mt=mt, hf=hf, k=k, bank=bank: e.matmul(
                    bank.t[:], lhsT=memn.t[:, k, mt * 128:(mt + 1) * 128], rhs=wvv.t[:, k, hf * 512:(hf + 1) * 512],
                    start=(k == 0), stop=(k == KT - 1)), reads=[wvv, memn], writes=[bank])
            P.act(lambda e, mt=mt, hf=hf, bank=bank: e.activation(out=vt.t[:, mt, hf * 512:(hf + 1) * 512],
                                                                  in_=bank.t[:], func=AF.Copy),
                  reads=[bank], writes=[vt])
    NTL = c["NTILES"]

    def head_load(ti):
        c["load_x"](xts[ti % 2], ti)

    def head_norm(ti):
        c["prenorm"](xts[ti % 2], hts[ti % 2], sq_pre, rstd_pre, "g_x_pre", l)

    def qproj(ti):
        ht = hts[ti % 2]
        for o in range(KT):
            bank = pb[3 + (o % 2)]
            for k in range(KT):
                P.pe(lambda e, o=o, k=k, bank=bank, ht=ht: e.matmul(bank.t[:], lhsT=wq.t[:, k, o * 128:(o + 1) * 128],
                                                                    rhs=ht.t[:, k, :], start=(k == 0),
                                                                    stop=(k == KT - 1)), reads=[wq, ht], writes=[bank])
            P.act(lambda e, o=o, bank=bank: e.activation(out=qT.t[:, o, :], in_=bank.t[:], func=AF.Copy),
                  reads=[bank], writes=[qT])

    def heads(hs):
        for h in hs:
            expT = expTs[h % 2]
            rden = rdens[h % 2]
            for mt in range(2):
                bank = pb[3 + (mt % 2)]
                for dd in range(2):
                    P.pe(lambda e, h=h, mt=mt, dd=dd, bank=bank: e.matmul(
                        bank.t[:], lhsT=kT.t[:, 2 * h + dd, mt * 128:(mt + 1) * 128], rhs=qT.t[:, 2 * h + dd, :],
                        start=(dd == 0), stop=(dd == 1)), reads=[kT, qT], writes=[bank])
                P.act(lambda e, mt=mt, bank=bank, expT=expT: e.activation(out=expT.t[:, mt, :], in_=bank.t[:], func=AF.Exp,
                                                               scale=1.0 / 16.0), reads=[bank], writes=[expT])
            for mt in range(2):
                P.pe(lambda e, mt=mt, expT=expT: e.matmul(pb[5].t[:], lhsT=onesb.t[:], rhs=expT.t[:, mt, :], start=(mt == 0),
                                               stop=(mt == 1)), reads=[expT, onesb], writes=[pb[5]])
            P.act(lambda e, rden=rden: e.activation(out=rden.t[:], in_=pb[5].t[:], func=AF.Ln, bias=cb.t[:, 2:3],
                                                    scale=1.0), reads=[pb[5], cb], writes=[rden])
            P.act(lambda e, rden=rden: e.activation(out=rden.t[:], in_=rden.t[:], func=AF.Exp, bias=cb.t[:, 2:3],
                                                    scale=-1.0), reads=[rden, cb], writes=[rden])
            for dd in range(2):
                bank = pb[6 + dd]
                for mt in range(2):
                    P.pe(lambda e, h=h, mt=mt, dd=dd, bank=bank, expT=expT: e.matmul(
                        bank.t[:], lhsT=vt.t[:, mt, (2 * h + dd) * 128:(2 * h + dd + 1) * 128], rhs=expT.t[:, mt, :],
                        start=(mt == 0), stop=(mt == 1)), reads=[vt, expT], writes=[bank])
                P.dve(lambda e, h=h, dd=dd, bank=bank, rden=rden: e.tensor_tensor(out=oT.t[:, 2 * h + dd, :], in0=bank.t[:],
                                                                       in1=rden.t[:], op=ALU.mult),
                      reads=[bank, rden], writes=[oT])

    def tail(ti):
        def mm(o, bank):
            for k in range(KT):
                P.pe(lambda e, o=o, k=k, bank=bank: e.matmul(bank.t[:], lhsT=wo.t[:, k, o * 128:(o + 1) * 128],
                                                             rhs=oT.t[:, k, :], start=(k == 0), stop=(k == KT - 1)),
                     reads=[wo, oT], writes=[bank])
        c["postnorm_res"](xts[ti % 2], ybuf, sq, rstd, "g_x_post", l, mm)
        c["store_x"](xts[ti % 2], ti)

    head_load(0)
    head_norm(0)
    for ti in range(NTL):
        if ti + 1 < NTL:
            head_load(ti + 1)
        qproj(ti)
        heads([0, 1])
        if ti + 1 < NTL:
            head_norm(ti + 1)
        heads([2, 3])
        tail(ti)


def ffn_layer(c, l):
    P, A, pb, pd = c["P"], c["A"], c["pb"], c["pd"]
    onesb, gv, cb = c["onesb"], c["gv"], c["cb"]
    NF = 22
    wup = A.alloc("wup", [128, KT, 5632], BF16)
    wdn = A.alloc("wdn", [128, NF, D], BF16)
    c["load_w"](wup, pd["w_up"][l].rearrange("(k p) n -> p k n", p=128), 5632)
    c["load_w"](wdn, pd["w_down"][l].rearrange("(f p) n -> p f n", p=128), D)
    cw = A.alloc("cw", [128, 3, 44], F32)
    cbv = A.alloc("cbv", [128, 44], F32)
    zl = A.alloc("zl", [128, 44, 2], F32)
    P.dma("sp", cw.t[:], pd["conv_w"][l].rearrange("w (f p) -> p w f", p=128), writes=[cw],
          allow_slow_non_contiguous=True)
    P.dma("sp", cbv.t[:], pd["conv_b"][l].rearrange("(f p) -> p f", p=128), writes=[cbv],
          allow_slow_non_contiguous=True)
    zls = [zl.sub(i) for i in range(44)]
    P.pool(lambda e: e.memset(zl.t[:], 0.0), writes=zls)
    bh = A.alloc("bh", [128, 44, 2], F32)
    hh = A.alloc("hh", [128, 2, 44], F32)
    off_xt = A.off
    xt = A.alloc("xt", [128, KT, NT], F32)
    ybuf = P.carve("ybuf_f", A.buf, off_xt, [128, KT, NT], F32)
    off_ht = A.off
    ht = A.alloc("ht", [128, KT, NT], BF16)
    sq2 = P.carve("sq2_f", A.buf, off_ht, [128, KT, NT], BF16)
    rstd = A.alloc("rstd", [128, NT], F32)
    off_g = A.off
    gbuf = A.alloc("gbuf", [128, NF, NT], BF16)
    sq = P.carve("sq_f", A.buf, off_g, [128, KT, NT], BF16)
    acc = [[A.alloc("accv0", [128, NT], F32), A.alloc("accg0", [128, NT], F32)],
           [A.alloc("accv1", [128, NT], F32), A.alloc("accg1", [128, NT], F32)]]
    xo = [A.alloc("xo0", [128, NT], F32), A.alloc("xo1", [128, NT], F32)]
    xs_r = c["xs_r"]
    xs_v = c["xs_v"]
    for ti in range(c["NTILES"]):
        c["load_x"](xt, ti)
        P.pool(lambda e: e.tensor_tensor(out=hh.t[:, 0, :], in0=cw.t[:, 0, :], in1=zl.t[:, :, 0], op=ALU.mult),
               reads=[cw] + zls, writes=[hh])
        P.pool(lambda e: e.tensor_tensor(out=hh.t[:, 1, :], in0=cw.t[:, 1, :], in1=zl.t[:, :, 1], op=ALU.mult),
               reads=[cw] + zls, writes=[hh])
        P.pool(lambda e: e.tensor_tensor(out=hh.t[:, 0, :], in0=hh.t[:, 0, :], in1=hh.t[:, 1, :], op=ALU.add),
               reads=[hh], writes=[hh])
        P.pool(lambda e: e.tensor_tensor(out=bh.t[:, :, 0], in0=hh.t[:, 0, :], in1=cbv.t[:], op=ALU.add),
               reads=[hh, cbv], writes=[bh])
        P.pool(lambda e: e.tensor_tensor(out=hh.t[:, 1, :], in0=cw.t[:, 0, :], in1=zl.t[:, :, 1], op=ALU.mult),
               reads=[cw] + zls, writes=[hh])
        P.pool(lambda e: e.tensor_tensor(out=bh.t[:, :, 1], in0=hh.t[:, 1, :], in1=cbv.t[:], op=ALU.add),
               reads=[hh, cbv], writes=[bh])
        c["prenorm"](xt, ht, sq, rstd, "g_ffn_pre", l)
        for f in range(NF):
            a2 = acc[f % 2]
            bks = (pb[3], pb[4]) if f % 2 == 0 else (pb[5], pb[6])
            for vg in range(2):
                ci = vg * NF + f
                bank = bks[vg]
                for k in range(KT):
                    P.pe(lambda e, ci=ci, k=k, bank=bank: e.matmul(bank.t[:], lhsT=wup.t[:, k, ci * 128:(ci + 1) * 128],
                                                                   rhs=ht.t[:, k, :], start=(k == 0),
                                                                   stop=(k == KT - 1)), reads=[wup, ht], writes=[bank])
            for vg in range(2):
                ci = vg * NF + f
                bank = bks[vg]
                a = a2[vg]
                P.act(lambda e, ci=ci, bank=bank, a=a: e.activation(out=a.t[:, 2:NT], in_=bank.t[:, 2:NT],
                                                                    func=AF.Identity, bias=cbv.t[:, ci:ci + 1],
                                                                    scale=cw.t[:, 2, ci:ci + 1]),
                      reads=[bank, cbv, cw], writes=[a])
                for col in range(2):
                    P.act(lambda e, ci=ci, bank=bank, a=a, col=col: e.activation(
                        out=a.t[:, col:col + 1], in_=bank.t[:, col:col + 1], func=AF.Identity,
                        bias=bh.t[:, ci, col:col + 1], scale=cw.t[:, 2, ci:ci + 1]), reads=[bank, bh, cw], writes=[a])
                P.act(lambda e, ci=ci, bank=bank: e.activation(out=zl.t[:, ci, :], in_=bank.t[:, NT - 2:NT],
                                                               func=AF.Copy), reads=[bank], writes=[zls[ci]])
            for vg in range(2):
                ci = vg * NF + f
                bank = bks[vg]
                a = a2[vg]
                P.dve(lambda e, ci=ci, bank=bank, a=a: e.scalar_tensor_tensor(
                    out=a.t[:, 1:NT], in0=bank.t[:, 0:NT - 1], scalar=cw.t[:, 1, ci:ci + 1], in1=a.t[:, 1:NT],
                    op0=ALU.mult, op1=ALU.add), reads=[bank, cw, a], writes=[a])
                P.dve(lambda e, ci=ci, bank=bank, a=a: e.scalar_tensor_tensor(
                    out=a.t[:, 2:NT], in0=bank.t[:, 0:NT - 2], scalar=cw.t[:, 0, ci:ci + 1], in1=a.t[:, 2:NT],
                    op0=ALU.mult, op1=ALU.add), reads=[bank, cw, a], writes=[a])
            for fp in ([f - 1] if f > 0 else []) + ([f] if f == NF - 1 else []):
                ap2 = acc[fp % 2]
                P.act(lambda e, ap2=ap2: e.activation(out=ap2[1].t[:], in_=ap2[1].t[:], func=AF.Gelu_apprx_tanh),
                      reads=[ap2[1]], writes=[ap2[1]])
                P.pool(lambda e, fp=fp, ap2=ap2: e.tensor_tensor(out=gbuf.t[:, fp, :], in0=ap2[0].t[:], in1=ap2[1].t[:],
                                                                 op=ALU.mult), reads=[ap2[0], ap2[1]],
                       writes=[gbuf.sub(fp)])

        def mm(o, bank):
            for f in range(NF):
                P.pe(lambda e, o=o, f=f, bank=bank: e.matmul(bank.t[:], lhsT=wdn.t[:, f, o * 128:(o + 1) * 128],
                                                             rhs=gbuf.t[:, f, :], start=(f == 0), stop=(f == NF - 1)),
                     reads=[wdn, gbuf.sub(f)], writes=[bank])
        g = gv["g_ffn_post"]
        for o in range(KT):
            bank = pb[1 + (o % 2)]
            mm(o, bank)
            P.act(lambda e, o=o, bank=bank: e.activation(out=ybuf.t[:, o, :], in_=bank.t[:], func=AF.Copy),
                  reads=[bank], writes=[ybuf, xt])
            P.dve(lambda e, o=o, bank=bank: e.tensor_tensor(out=sq2.t[:, o, :], in0=bank.t[:], in1=ybuf.t[:, o, :],
                                                            op=ALU.mult), reads=[bank, ybuf], writes=[sq2, ht])
        for k in range(KT):
            P.pe(lambda e, k=k: e.matmul(pb[0].t[:], lhsT=onesb.t[:], rhs=sq2.t[:, k, :], start=(k == 0),
                                         stop=(k == KT - 1)), reads=[sq2, onesb], writes=[pb[0]])
        c["rstd_from_ss"](pb[0].t[:], rstd, D, [pb[0]])
        xbufs = [acc[0][0], acc[0][1], acc[1][0], acc[1][1], xo[0], xo[1]]
        for o in range(KT):
            xb = xbufs[o % 6]
            P.dma("sp", xb.t[:], xs_v[:, o, ti * NT:(ti + 1) * NT], reads=[xs_r[ti].sub(o)], writes=[xb],
                  key="xoload%d" % (o % 6))
            P.dve(lambda e, o=o: e.scalar_tensor_tensor(out=ybuf.t[:, o, :], in0=ybuf.t[:, o, :],
                                                        scalar=g.t[:, l, o:o + 1], in1=rstd.t[:], op0=ALU.mult,
                                                        op1=ALU.mult), reads=[ybuf, g, rstd], writes=[ybuf, xt])
            P.pool(lambda e, o=o, xb=xb: e.tensor_tensor(out=xb.t[:], in0=xb.t[:], in1=ybuf.t[:, o, :], op=ALU.add),
                   reads=[xb, ybuf], writes=[xb])
            P.dma("sp", xs_v[:, o, ti * NT:(ti + 1) * NT], xb.t[:], reads=[xb], writes=[xs_r[ti].sub(o)],
                  key="xst%d" % (o % 6))


def _run(inputs, S=8192, DEPTH=4, STAGES=("mix", "xat", "ffn"), NCORES=8):
    nc = build_program(S=S, DEPTH=DEPTH, STAGES=STAGES)
    in_maps = []
    for c in range(NCORES):
        b = c % 2
        m = {"x": np.ascontiguousarray(inputs["x"][b, :S]), "mem": np.ascontiguousarray(inputs["mem"][b])}
        for n in PNAMES:
            m[n] = np.ascontiguousarray(inputs[n])
        in_maps.append(m)
    import os
    if os.environ.get("KTRACE"):
        res = run_bass_kernel_spmd(nc, in_maps, core_ids=list(range(NCORES)), trace=True)
        print("EXEC_TIME_NS", res.exec_time_ns, flush=True)
    else:
        res = run_bass_kernel_spmd(nc, in_maps, core_ids=list(range(NCORES)))
    return np.stack([res.results[0]["out"], res.results[1 % NCORES]["out"]], axis=0)


def kernel(**inputs):
    inputs = {k: np.asarray(v) for k, v in inputs.items()}
    return _run(inputs).astype(np.float32)


def mixer_layer(c, l):
    P, A, pb, pd = c["P"], c["A"], c["pb"], c["pd"]
    onesb, gv, gvv, cb = c["onesb"], c["gv"], c["gvv"], c["cb"]
    ident, identb, blkones, maskJ, tril = c["ident"], c["identb"], c["blkones"], c["maskJ"], c["tril"]
    pst = c["pst"]
    NG, NPR = 32, 16
    w_in = A.alloc("w_in", [128, KT, 2048], BF16)
    w_out = A.alloc("w_out", [128, KT, D], BF16)
    c["load_w"](w_in, pd["w_in"][l].rearrange("(k p) n -> p k n", p=128), 2048)
    c["load_w"](w_out, pd["w_out"][l].rearrange("(k p) n -> p k n", p=128), D)
    WDT = [A.alloc("WDTr", [128, NPR, 128], BF16), A.alloc("WDTi", [128, NPR, 128], BF16)]
    Wo = [A.alloc("Wor", [128, NPR, 128], BF16), A.alloc("Woi", [128, NPR, 128], BF16)]
    Kmat = A.alloc("Kmat", [128, NG, 128], BF16)
    C1 = A.alloc("C1", [128, NPR, 64], F32)
    S1 = A.alloc("S1", [128, NPR, 64], F32)
    Mtab = A.alloc("Mtab", [128, NPR, 64], F32)
    rho8 = A.alloc("rho8", [128, NPR], F32)
    Lf = [A.alloc("Lfr", [128, NPR, 65], F32), A.alloc("Lfi", [128, NPR, 65], F32)]
    WsT = A.alloc("WsT", [128, 8, 128], BF16)
    bs2 = A.alloc("bs2", [128, 1024], BF16)
    off_tmp = A.off
    dv = P.res("dv")
    sm = A.alloc("sm", [128, 64, NPR], F32)
    Bc = [A.alloc("Bre", [128, NPR, 16], F32), A.alloc("Bim", [128, NPR, 16], F32)]
    Ct = [A.alloc("Ctr", [128, NPR, 16], F32), A.alloc("Cti", [128, NPR, 16], F32)]
    Bb = [A.alloc("Bbr", [128, NPR, 16], F32), A.alloc("Bbi", [128, NPR, 16], F32)]
    WD = [A.alloc("WDr", [128, NPR, 128], F32), A.alloc("WDi", [128, NPR, 128], F32)]
    WDt = [A.alloc("WDtr", [128, NPR, 128], BF16), A.alloc("WDti", [128, NPR, 128], BF16)]
    T = [A.alloc("dT1", [128, NPR, 128], F32), A.alloc("dT2", [128, NPR, 128], F32)]
    dcol = A.alloc("dcol", [128, NG], F32)
    wsl = A.alloc("wsl", [128, 8, 128], F32)
    bsf = A.alloc("bsf", [128, 1024], F32)
    tK = A.alloc("tK", [128, 128], F32)

    def S(i):
        return sm.t[:, i, :]

    def tt(out, a, b, op, eng="dve"):
        P.eng(eng, lambda e: e.tensor_tensor(out=out, in0=a, in1=b, op=op), reads=[dv], writes=[dv])

    def ts(out, a, s1, op, eng="dve"):
        P.eng(eng, lambda e: e.tensor_scalar(out=out, in0=a, scalar1=s1, scalar2=None, op0=op), reads=[dv], writes=[dv])

    def actf(out, a, func, scale=1.0, bias=None):
        b = cb.t[:, 2:3] if bias is None else bias
        P.act(lambda e: e.activation(out=out, in_=a, func=func, bias=b, scale=scale), reads=[dv, cb], writes=[dv])

    def cmul(o_r, o_i, ar, ai, br, bi, t1, t2):
        tt(t1, ar, br, ALU.mult); tt(t2, ai, bi, ALU.mult); tt(o_r, t1, t2, ALU.subtract)
        tt(t1, ar, bi, ALU.mult); tt(t2, ai, br, ALU.mult); tt(o_i, t1, t2, ALU.add)

    def dmap(out, src):
        P.dma("sp", out, src, writes=[dv], key="dvload", allow_slow_non_contiguous=True)

    LR, LI, LDT, DT, X1, X2, MG, CS, SN, AR, AI, NR, NI, TA, TB, RHO, IRHO, U8R, U8I, I8R, I8I, FR, FI, DEN = range(24)
    AP0 = 24
    dmap(S(LR), pd["lam_re"][l].rearrange("(pr m) p -> (m p) pr", m=2))
    dmap(S(LI), pd["lam_im"][l].rearrange("(pr m) p -> (m p) pr", m=2))
    for m in range(2):
        dmap(sm.t[m * 64:(m + 1) * 64, LDT, :],
             pd["log_dt"][l].rearrange("(pr m) -> m pr", m=2)[m].partition_broadcast(64))
        for comp, nm in ((0, "c_re"), (1, "c_im")):
            for pr_ in range(NPR):
                dmap(Ct[comp].t[m * 64:(m + 1) * 64, pr_, :],
                     pd[nm][l].rearrange("(pr m) c p -> m pr p c", m=2)[m][pr_])
    dmap(Bc[0].t[:], pd["b_re"][l].rearrange("(pr m) p c -> (m p) pr c", m=2))
    dmap(Bc[1].t[:], pd["b_im"][l].rearrange("(pr m) p c -> (m p) pr c", m=2))
    for j in range(8):
        dmap(dcol.t[j * 16:(j + 1) * 16, :], pd["d_skip"][l].rearrange("g c -> c g"))
    dmap(wsl.t[:], pd["w_s"][l].rearrange("h t s -> t h s"))
    for r in (0, 32):
        dmap(bsf.t[r:r + 1, :], pd["b_s"][l:l + 1].rearrange("o h t -> o (h t)"))
    actf(S(DT), S(LDT), AF.Exp)
    tt(S(X1), S(LR), S(DT), ALU.mult)
    tt(S(X2), S(LI), S(DT), ALU.mult)
    actf(S(MG), S(X1), AF.Exp, scale=1.0 / 64)
    actf(S(CS), S(X2), AF.Sin, scale=1.0 / 64, bias=cb.t[:, 0:1])
    actf(S(SN), S(X2), AF.Sin, scale=1.0 / 64)
    tt(S(AR), S(MG), S(CS), ALU.mult)
    tt(S(AI), S(MG), S(SN), ALU.mult)
    cur = (AR, AI)
    nxt = (NR, NI)
    for _ in range(6):
        tt(S(TA), S(cur[0]), S(cur[0]), ALU.mult)
        tt(S(TB), S(cur[1]), S(cur[1]), ALU.mult)
        tt(S(nxt[0]), S(TA), S(TB), ALU.subtract)
        tt(S(TA), S(cur[0]), S(cur[1]), ALU.mult)
        tt(S(nxt[1]), S(TA), S(TA), ALU.add)
        cur, nxt = nxt, cur
    A1 = cur

    def apr(m):
        return S(AP0 + 2 * m)

    def api(m):
        return S(AP0 + 2 * m + 1)
    P.dve(lambda e: e.memset(apr(0), 1.0), reads=[dv], writes=[dv])
    P.dve(lambda e: e.memset(api(0), 0.0), reads=[dv], writes=[dv])
    P.dve(lambda e: e.tensor_copy(out=apr(1), in_=S(A1[0])), reads=[dv], writes=[dv])
    P.dve(lambda e: e.tensor_copy(out=api(1), in_=S(A1[1])), reads=[dv], writes=[dv])
    for m in range(2, 9):
        cmul(apr(m), api(m), apr(m - 1), api(m - 1), apr(1), api(1), S(TA), S(TB))
    actf(S(RHO), S(X1), AF.Exp, scale=8.0)
    P.dve(lambda e: e.tensor_copy(out=rho8.t[:], in_=S(RHO)), reads=[dv], writes=[dv, rho8])
    P.dve(lambda e: e.reciprocal(out=S(IRHO), in_=S(RHO)), reads=[dv], writes=[dv])
    tt(S(U8R), apr(8), S(IRHO), ALU.mult)
    tt(S(U8I), api(8), S(IRHO), ALU.mult)
    tt(S(TA), S(IRHO), S(IRHO), ALU.mult)
    tt(S(I8R), apr(8), S(TA), ALU.mult)
    tt(S(I8I), api(8), S(TA), ALU.mult)
    ts(S(I8I), S(I8I), -1.0, ALU.mult)
    ts(S(NR), apr(1), -1.0, ALU.add)
    tt(S(TA), S(LR), S(LR), ALU.mult)
    tt(S(TB), S(LI), S(LI), ALU.mult)
    tt(S(DEN), S(TA), S(TB), ALU.add)
    P.dve(lambda e: e.reciprocal(out=S(DEN), in_=S(DEN)), reads=[dv], writes=[dv])
    tt(S(TA), S(NR), S(LR), ALU.mult)
    tt(S(TB), api(1), S(LI), ALU.mult)
    tt(S(FR), S(TA), S(TB), ALU.add)
    tt(S(FR), S(FR), S(DEN), ALU.mult)
    tt(S(TA), api(1), S(LR), ALU.mult)
    tt(S(TB), S(NR), S(LI), ALU.mult)
    tt(S(FI), S(TA), S(TB), ALU.subtract)
    tt(S(FI), S(FI), S(DEN), ALU.mult)

    def bc(ap2, n):
        return ap2.unsqueeze(2).to_broadcast([128, NPR, n])
    t16 = [T[0].t[:, :, 0:16], T[1].t[:, :, 0:16]]
    cmul(Bb[0].t[:], Bb[1].t[:], bc(S(FR), 16), bc(S(FI), 16), Bc[0].t[:], Bc[1].t[:], t16[0], t16[1])
    WD4 = [w.t[:].rearrange("p a (j c) -> p a j c", j=8) for w in WD]
    for j in range(8):
        cmul(WD4[0][:, :, j, :], WD4[1][:, :, j, :], bc(apr(7 - j), 16), bc(api(7 - j), 16), Bb[0].t[:], Bb[1].t[:],
             t16[0], t16[1])
    cmul(WDt[0].t[:], WDt[1].t[:], bc(S(I8R), 128), bc(S(I8I), 128), WD[0].t[:], WD[1].t[:], T[0].t[:], T[1].t[:])
    for comp in range(2):
        for q4 in range(4):
            bank = pb[3 + (q4 % 2)]
            for q in range(4):
                pr = q4 * 4 + q
                P.pe(lambda e, comp=comp, pr=pr, q=q, bank=bank: e.transpose(
                    out=bank.t[:, q * 128:(q + 1) * 128], in_=WD[comp].t[:, pr, :], identity=ident.t[:]),
                    reads=[dv, ident], writes=[bank])
            P.dve(lambda e, comp=comp, q4=q4, bank=bank: e.tensor_copy(
                out=WDT[comp].t[:, q4 * 4:(q4 + 1) * 4, :], in_=bank.t[:].rearrange("p (a b) -> p a b", a=4)),
                reads=[bank], writes=[WDT[comp]])
    Wo4 = [w.t[:].rearrange("p a (j c) -> p a j c", j=8) for w in Wo]
    for j in range(8):
        ar_, ai_ = bc(apr(j + 1), 16), bc(api(j + 1), 16)
        tt(t16[0], Ct[0].t[:], ar_, ALU.mult); tt(t16[1], Ct[1].t[:], ai_, ALU.mult)
        P.dve(lambda e, j=j: e.tensor_tensor(out=Wo4[0][:, :, j, :], in0=t16[0], in1=t16[1], op=ALU.subtract),
              reads=[dv], writes=[dv, Wo[0]])
        tt(t16[0], Ct[0].t[:], ai_, ALU.mult); tt(t16[1], Ct[1].t[:], ar_, ALU.mult)
        tt(t16[0], t16[0], t16[1], ALU.add)
        P.dve(lambda e, j=j: e.tensor_scalar(out=Wo4[1][:, :, j, :], in0=t16[0], scalar1=-1.0, scalar2=None,
                                             op0=ALU.mult), reads=[dv], writes=[dv, Wo[1]])
    for g in range(NG):
        pr, m = divmod(g, 2)
        bank = pb[5 + (g % 2)]
        sl = slice(m * 64, (m + 1) * 64)
        P.pe(lambda e, pr=pr, sl=sl, bank=bank: e.matmul(bank.t[:, 0:128], lhsT=WDt[0].t[sl, pr, :],
                                                          rhs=Wo[0].t[sl, pr, :], start=True, stop=False),
             reads=[dv, Wo[0]], writes=[bank])
        P.pe(lambda e, pr=pr, sl=sl, bank=bank: e.matmul(bank.t[:, 0:128], lhsT=WDt[1].t[sl, pr, :],
                                                          rhs=Wo[1].t[sl, pr, :], start=False, stop=True),
             reads=[dv, Wo[1]], writes=[bank])
        P.dve(lambda e, bank=bank: e.tensor_tensor(out=tK.t[:], in0=bank.t[:, 0:128], in1=maskJ.t[:], op=ALU.mult),
              reads=[bank, maskJ, dv], writes=[dv])
        P.dve(lambda e, g=g: e.scalar_tensor_tensor(out=Kmat.t[:, g, :], in0=ident.t[:], scalar=dcol.t[:, g:g + 1],
                                                    in1=tK.t[:], op0=ALU.mult, op1=ALU.add),
              reads=[dv, ident], writes=[dv, Kmat])
    P.dve(lambda e: e.tensor_copy(out=C1.t[:, :, 0], in_=S(U8R)), reads=[dv], writes=[dv, C1])
    P.dve(lambda e: e.tensor_copy(out=S1.t[:, :, 0], in_=S(U8I)), reads=[dv], writes=[dv, S1])
    s_ = 1
    while s_ < 64:
        cr = C1.t[:, :, s_ - 1:s_].to_broadcast([128, NPR, s_])
        ci = S1.t[:, :, s_ - 1:s_].to_broadcast([128, NPR, s_])
        t1, t2 = T[0].t[:, :, 0:s_], T[1].t[:, :, 0:s_]
        cmul(C1.t[:, :, s_:2 * s_], S1.t[:, :, s_:2 * s_], cr, ci, C1.t[:, :, 0:s_], S1.t[:, :, 0:s_], t1, t2)
        s_ *= 2
    P.dve(lambda e: e.tensor_copy(out=Mtab.t[:], in_=bc(rho8.t[:], 64)), reads=[dv, rho8], writes=[dv, Mtab])
    P.dve(lambda e: e.memset(Mtab.t[:, :, 0], 0.0), reads=[dv], writes=[dv, Mtab])
    for comp in range(2):
        P.dve(lambda e, comp=comp: e.memset(Lf[comp].t[:], 0.0), reads=[dv], writes=[dv, Lf[comp]])
    for h in range(8):
        bank = pb[3 + (h % 2)]
        P.pe(lambda e, h=h, bank=bank: e.transpose(out=bank.t[:, 0:128], in_=wsl.t[:, h, :], identity=ident.t[:]),
             reads=[dv, ident], writes=[bank])
        P.dve(lambda e, h=h, bank=bank: e.tensor_tensor(out=WsT.t[:, h, :], in0=bank.t[:, 0:128], in1=tril.t[:],
                                                        op=ALU.mult), reads=[bank, tril], writes=[WsT])
    P.dve(lambda e: e.memset(bs2.t[:], 0.0), reads=[dv], writes=[dv, bs2])
    P.dve(lambda e: e.tensor_copy(out=bs2.t[0:1, :], in_=bsf.t[0:1, :]), reads=[dv], writes=[dv, bs2])
    P.dve(lambda e: e.tensor_copy(out=bs2.t[32:33, :], in_=bsf.t[32:33, :]), reads=[dv], writes=[dv, bs2])
    P.dve(lambda e: e.tensor_tensor(out=bsf.t[32:33, :], in0=bsf.t[32:33, :], in1=bs2.t[32:33, :], op=ALU.subtract),
          reads=[dv, bs2], writes=[dv])
    P.dve(lambda e: e.tensor_copy(out=bs2.t[32:33, :], in_=bsf.t[32:33, :]), reads=[dv], writes=[dv, bs2])
    P.barrier()
    A.off = off_tmp
    xt = A.alloc("xt", [128, KT, NT], F32)
    off_ht = A.off
    ht = A.alloc("ht", [128, KT, NT], BF16)
    ycat = P.carve("ycat", A.buf, off_ht, [128, KT, NT], BF16)
    P.alias(ht, ycat)
    rstd = A.alloc("rstd", [128, NT], F32)
    off_sq = A.off
    sq = A.alloc("sq", [128, KT, NT], BF16)
    vtok = P.carve("vtok", A.buf, off_sq, [128, 4, NT], BF16)
    ub = P.carve("ub", A.buf, off_sq + 4096, [128, 4, NT], BF16)
    P.alias(sq, vtok); P.alias(sq, ub)
    off_u = A.off
    Ublk = A.alloc("Ublk", [128, NG, 8, 16], BF16)
    Tt = [P.carve("Tt1", A.buf, off_u, [128, NPR, 64], F32), P.carve("Tt2", A.buf, off_u + 4096, [128, NPR, 64], F32)]
    P.alias(Ublk, Tt[0]); P.alias(Ublk, Tt[1])
    UblkT = A.alloc("UblkT", [128, NG, 64], BF16)
    Z = [A.alloc("Zr", [128, NPR, 64], F32), A.alloc("Zi", [128, NPR, 64], F32)]
    W = [A.alloc("Wr", [128, NPR, 64], F32), A.alloc("Wi", [128, NPR, 64], F32)]
    Lb = [A.alloc("Lbr", [128, NPR, 64], BF16), A.alloc("Lbi", [128, NPR, 64], BF16)]
    off_y = A.off
    Yblk = A.alloc("Yblk", [128, 8, NT], F32)
    ybuf = P.carve("ybuf", A.buf, off_y, [128, KT, NT], F32)
    P.alias(Yblk, ybuf)
    ytmp = A.alloc("ytmp", [128, NT], F32)
    sg = A.alloc("sg", [128, 4, NT], BF16)
    vn = A.alloc("vn", [128, NT], BF16)
    vsq = A.alloc("vsq", [128, NT], BF16)
    vrs = A.alloc("vrs", [128, NT], F32)
    t16s = A.alloc("t16s", [128, 2, NPR], F32)
    c.setdefault("dbg", {})["mix_off"] = A.off
    Dre = pst.t[:, 3 * 512:5 * 512].rearrange("p (a n) -> p a n", a=NPR)
    Dim = pst.t[:, 5 * 512:7 * 512].rearrange("p (a n) -> p a n", a=NPR)
    Dv = [Dre, Dim]
    Dbank = [[pb[3], pb[4]], [pb[5], pb[6]]]
    pbT = {b: pb[b].t.bitcast(BF16) for b in (5, 6, 7)}
    gpost = "g_mix_post"

    for ti in range(c["NTILES"]):
        c["load_x"](xt, ti)
        c["prenorm"](xt, ht, sq, rstd, "g_mix_pre", l)
        for j in range(8):
            bank = pb[3 + (j % 2)]
            for k in range(KT):
                P.pe(lambda e, j=j, k=k, bank=bank: e.matmul(bank.t[0:64, :], lhsT=ht.t[:, k, j:NT:8],
                                                             rhs=w_in.t[:, k, 0:512], start=(k == 0),
                                                             stop=(k == KT - 1)), reads=[ht, w_in], writes=[bank])
            src = bank.t[0:64, :].rearrange("p (g c) -> p g c", g=NG)
            if j % 2 == 0:
                P.act(lambda e, j=j, src=src: e.activation(out=Ublk.t[0:64, :, j, :], in_=src, func=AF.Copy),
                      reads=[bank], writes=[Ublk])
            else:
                P.dve(lambda e, j=j, src=src: e.tensor_copy(out=Ublk.t[0:64, :, j, :], in_=src),
                      reads=[bank], writes=[Ublk])
        for g8 in range(4):
            bno = 5 + (g8 % 2)
            for q in range(8):
                g = g8 * 8 + q
                P.pe(lambda e, g=g, q=q, bno=bno: e.transpose(
                    out=pbT[bno][:, q * 64:(q + 1) * 64], in_=Ublk.t[0:64, g, :, :].rearrange("p j c -> p (j c)"),
                    identity=identb.t[0:64, 0:64]), reads=[Ublk, identb], writes=[pb[bno]])
            P.act(lambda e, g8=g8, bno=bno: e.activation(
                out=UblkT.t[:, g8 * 8:(g8 + 1) * 8, :], in_=pbT[bno][:, 0:512].rearrange("p (a n) -> p a n", a=8),
                func=AF.Copy), reads=[pb[bno]], writes=[UblkT])
        if ti > 0:
            for comp in range(2):
                P.act(lambda e, comp=comp: e.activation(out=Lf[comp].t[:, :, 0], in_=Lf[comp].t[:, :, 64], func=AF.Copy),
                      reads=[Lf[comp]], writes=[Lf[comp]])
        for g in range(NG):
            pr, m = divmod(g, 2)
            for comp in range(2):
                bank = Dbank[comp][pr // 8]
                P.pe(lambda e, g=g, pr=pr, m=m, comp=comp: e.matmul(
                    Dv[comp][m * 64:(m + 1) * 64, pr, :], lhsT=WDT[comp].t[:, pr, m * 64:(m + 1) * 64],
                    rhs=UblkT.t[:, g, :], start=True, stop=True), reads=[UblkT, WDT[comp]], writes=[bank])
        dr = [pb[3], pb[4], pb[5], pb[6]]

        def dtt(out, a, b, op, rd, wr):
            P.dve(lambda e: e.tensor_tensor(out=out, in0=a, in1=b, op=op), reads=rd, writes=wr)
        dtt(Tt[0].t[:], Dre, C1.t[:], ALU.mult, dr + [C1], [Tt[0]])
        dtt(Tt[1].t[:], Dim, S1.t[:], ALU.mult, dr + [S1], [Tt[1]])
        dtt(Z[0].t[:], Tt[0].t[:], Tt[1].t[:], ALU.add, Tt, [Z[0]])
        dtt(Tt[0].t[:], Dim, C1.t[:], ALU.mult, dr + [C1], [Tt[0]])
        dtt(Tt[1].t[:], Dre, S1.t[:], ALU.mult, dr + [S1], [Tt[1]])
        dtt(Z[1].t[:], Tt[0].t[:], Tt[1].t[:], ALU.subtract, Tt, [Z[1]])
        for comp in range(2):
            dtt(t16s.t[:, comp, :], Lf[comp].t[:, :, 0], rho8.t[:], ALU.mult, [Lf[comp], rho8], [t16s])
            dtt(Z[comp].t[:, :, 0], Z[comp].t[:, :, 0], t16s.t[:, comp, :], ALU.add, [Z[comp], t16s], [Z[comp]])
            P.dve(lambda e, comp=comp: e.tensor_tensor_scan(
                out=W[comp].t[:].rearrange("p a n -> p (a n)"), data0=Mtab.t[:].rearrange("p a n -> p (a n)"),
                data1=Z[comp].t[:].rearrange("p a n -> p (a n)"), initial=0.0, op0=ALU.mult, op1=ALU.add),
                reads=[Mtab, Z[comp]], writes=[W[comp]])
        dtt(Tt[0].t[:], C1.t[:], W[0].t[:], ALU.mult, [C1, W[0]], [Tt[0]])
        dtt(Tt[1].t[:], S1.t[:], W[1].t[:], ALU.mult, [S1, W[1]], [Tt[1]])
        dtt(Lf[0].t[:, :, 1:65], Tt[0].t[:], Tt[1].t[:], ALU.subtract, Tt, [Lf[0]])
        dtt(Tt[0].t[:], C1.t[:], W[1].t[:], ALU.mult, [C1, W[1]], [Tt[0]])
        dtt(Tt[1].t[:], S1.t[:], W[0].t[:], ALU.mult, [S1, W[0]], [Tt[1]])
        dtt(Lf[1].t[:, :, 1:65], Tt[0].t[:], Tt[1].t[:], ALU.add, Tt, [Lf[1]])
        for comp in range(2):
            P.act(lambda e, comp=comp: e.activation(out=Lb[comp].t[:], in_=Lf[comp].t[:, :, 0:64], func=AF.Copy),
                  reads=[Lf[comp]], writes=[Lb[comp]])
        for ct in range(4):
            for grp in range(3):
                col = 512 * (grp + 1) + ct * 128
                bank = pb[1 + ((ct * 3 + grp) % 2)]
                for k in range(KT):
                    P.pe(lambda e, col=col, k=k, bank=bank: e.matmul(bank.t[:], lhsT=w_in.t[:, k, col:col + 128],
                                                                     rhs=ht.t[:, k, :], start=(k == 0),
                                                                     stop=(k == KT - 1)), reads=[ht, w_in], writes=[bank])
                if grp == 0:
                    P.act(lambda e, ct=ct, bank=bank: e.activation(out=sg.t[:, ct, :], in_=bank.t[:], func=AF.Sigmoid),
                          reads=[bank], writes=[sg])
                elif grp == 1:
                    P.act(lambda e, ct=ct, bank=bank: e.activation(out=ub.t[:, ct, :], in_=bank.t[:], func=AF.Copy),
                          reads=[bank], writes=[ub])
                else:
                    P.act(lambda e, bank=bank: e.activation(out=vsq.t[:], in_=bank.t[:], func=AF.Square),
                          reads=[bank], writes=[vsq])
                    P.pe(lambda e: e.matmul(pb[0].t[:], lhsT=blkones.t[:], rhs=vsq.t[:], start=True, stop=True),
                         reads=[vsq, blkones], writes=[pb[0]])
                    c["rstd_from_ss"](pb[0].t[:], vrs, 64, [pb[0]])
                    P.dve(lambda e, ct=ct, bank=bank: e.scalar_tensor_tensor(
                        out=vn.t[:], in0=bank.t[:], scalar=gvv.t[:, l, ct:ct + 1], in1=vrs.t[:], op0=ALU.mult,
                        op1=ALU.mult), reads=[bank, gvv, vrs], writes=[vn])
                    for cc in range(4):
                        P.pe(lambda e, cc=cc: e.transpose(out=pbT[7][:, cc * 128:(cc + 1) * 128],
                                                          in_=vn.t[:, cc * 128:(cc + 1) * 128], identity=identb.t[:]),
                             reads=[vn, identb], writes=[pb[7]])
                    P.dve(lambda e, ct=ct: e.tensor_copy(
                        out=vtok.t[:, :, ct * 128:(ct + 1) * 128],
                        in_=pbT[7][:, 0:512].rearrange("p (a n) -> p a n", a=4)), reads=[pb[7]], writes=[vtok])
        for g4 in range(8):
            bank = pb[1 + (g4 % 2)]
            for q in range(4):
                g = g4 * 4 + q
                pr, m = divmod(g, 2)
                sl = slice(m * 64, (m + 1) * 64)
                osl = bank.t[0:64, q * 128:(q + 1) * 128]
                P.pe(lambda e, g=g, osl=osl: e.matmul(osl, lhsT=UblkT.t[:, g, :], rhs=Kmat.t[:, g, :], start=True,
                                                      stop=False), reads=[UblkT, Kmat], writes=[bank])
                P.pe(lambda e, pr=pr, sl=sl, osl=osl: e.matmul(osl, lhsT=Lb[0].t[sl, pr, :], rhs=Wo[0].t[sl, pr, :],
                                                               start=False, stop=False), reads=[Lb[0], Wo[0]],
                     writes=[bank])
                P.pe(lambda e, pr=pr, sl=sl, osl=osl: e.matmul(osl, lhsT=Lb[1].t[sl, pr, :], rhs=Wo[1].t[sl, pr, :],
                                                               start=False, stop=True), reads=[Lb[1], Wo[1]],
                     writes=[bank])
            P.dve(lambda e, g4=g4, bank=bank: e.tensor_copy(
                out=Yblk.t[0:64, :, g4 * 64:(g4 + 1) * 64].rearrange("p j (g c) -> p j g c", g=4),
                in_=bank.t[0:64, :].rearrange("p (g j c) -> p j g c", g=4, j=8)), reads=[bank], writes=[Yblk])
        for ct in range(4):
            bank = pb[3 + (ct % 2)]
            for j in range(8):
                P.pe(lambda e, ct=ct, j=j, bank=bank: e.transpose(
                    out=bank.t[:, j * 64:(j + 1) * 64], in_=Yblk.t[0:64, j, ct * 128:(ct + 1) * 128],
                    identity=ident.t[0:64, 0:64]), reads=[Yblk, ident], writes=[bank])
            P.act(lambda e, bank=bank: e.activation(out=ytmp.t[:].rearrange("p (n j) -> p n j", j=8),
                                                    in_=bank.t[:].rearrange("p (j n) -> p n j", j=8),
                                                    func=AF.Gelu_apprx_tanh), reads=[bank], writes=[ytmp])
            P.pool(lambda e, ct=ct: e.tensor_tensor(out=ycat.t[:, ct, :], in0=ytmp.t[:], in1=sg.t[:, ct, :],
                                                    op=ALU.mult), reads=[ytmp, sg], writes=[ycat])
        for ct in range(4):
            bank = pb[1 + (ct % 2)]
            for cc in range(4):
                for hh in range(2):
                    h = 2 * ct + hh
                    osl = bank.t[hh * 64:(hh + 1) * 64, cc * 128:(cc + 1) * 128]
                    P.pe(lambda e, cc=cc, h=h, osl=osl: e.matmul(osl, lhsT=vtok.t[:, cc, h * 64:(h + 1) * 64],
                                                                 rhs=WsT.t[:, h, :], start=True, stop=False),
                         reads=[vtok, WsT], writes=[bank])
                    P.pe(lambda e, h=h, osl=osl: e.matmul(osl, lhsT=onesb.t[0:64, 0:64],
                                                          rhs=bs2.t[0:64, h * 128:(h + 1) * 128], start=False,
                                                          stop=True), reads=[bs2, onesb], writes=[bank])
            P.dve(lambda e, ct=ct, bank=bank: e.tensor_tensor(out=ycat.t[:, 4 + ct, :], in0=bank.t[:],
                                                              in1=ub.t[:, ct, :], op=ALU.mult),
                  reads=[bank, ub], writes=[ycat])

        def mm(o, bank):
            for k in range(KT):
                P.pe(lambda e, o=o, k=k, bank=bank: e.matmul(bank.t[:], lhsT=w_out.t[:, k, o * 128:(o + 1) * 128],
                                                             rhs=ycat.t[:, k, :], start=(k == 0), stop=(k == KT - 1)),
                     reads=[w_out, ycat], writes=[bank])
        c["postnorm_res"](xt, ybuf, sq, rstd, gpost, l, mm)
        c["store_x"](xt, ti)
```
